# Optimizing a Trainium2 kernel written in Bass

```python
import jax, jax.numpy as jnp
from jax import lax
import numpy as np

D_MODEL = 1024
BATCH = 4
SEQ = 4096
DEPTH = 2
DEC_BATCH = 128
DEC_SEQ = 4
PAST_LEN = 16384
PAGE_SIZE = 128

N_META = 16
N_A_LAYERS = DEPTH // 2
N_B_LAYERS = DEPTH - N_A_LAYERS
RW_HEAD = 64
RW_HEADS = D_MODEL // RW_HEAD
RW_DECAY_LORA = 64
RW_A_LORA = 64
RW_GATE_LORA = 128
GN_EPS = 64e-5
MLA_HEADS = D_MODEL // 128
QK_NOPE = 128
QK_ROPE = 64
V_HEAD = 128
KV_RANK = D_MODEL // 4
Q_RANK = 3 * D_MODEL // 8
ROPE_BASE = 10000.0
Q_BLOCK = 128
ATTN_SCALE = (QK_NOPE + QK_ROPE) ** -0.5
D_FF = 4 * D_MODEL
NORM_EPS = 1e-6

kernel_name = 'rwkv7_mla_yoco_decode_step'


def rmsnorm(x, g):
    xf = x.astype(jnp.float32)
    y = xf * lax.rsqrt(jnp.mean(xf * xf, axis=-1, keepdims=True) + NORM_EPS)
    return (y * g.astype(jnp.float32)).astype(x.dtype)


def sqrelu_mlp(x, w_up, w_down):
    return jnp.square(jax.nn.relu(x @ w_up)) @ w_down


def rope(x, pos):
    half = QK_ROPE // 2
    inv_freq = ROPE_BASE ** (-jnp.arange(half, dtype=jnp.float32) / half)
    ang = pos.astype(jnp.float32)[:, None] * inv_freq[None, :]
    ang = ang.reshape(ang.shape[0], *([1] * (x.ndim - 3)), half)
    cos, sin = jnp.cos(ang), jnp.sin(ang)
    x1 = x[..., :half].astype(jnp.float32)
    x2 = x[..., half:].astype(jnp.float32)
    return jnp.concatenate([x1 * cos - x2 * sin, x1 * sin + x2 * cos], axis=-1).astype(x.dtype)


def wkv7_scan(s0, r, decay, k, v, kk, ka):
    def step(s, inp):
        r_t, d_t, k_t, v_t, kk_t, ka_t = inp
        sa = jnp.einsum('bhvk,bhk->bhv', s, kk_t)
        s = s * d_t[:, :, None, :] - sa[..., None] * ka_t[:, :, None, :] + v_t[..., None] * k_t[:, :, None, :]
        return s, jnp.einsum('bhvk,bhk->bhv', s, r_t)
    seq = tuple(jnp.moveaxis(t.astype(jnp.float32), 1, 0) for t in (r, decay, k, v, kk, ka))
    s, o = lax.scan(step, s0.astype(jnp.float32), seq)
    return s, jnp.moveaxis(o, 0, 1)


def rwkv7_time_mix(xn, prev, s0, p, i):
    B, T, D = xn.shape
    xx = prev - xn
    mu = p['rw_mu'][i]
    xr, xw, xk, xv, xa, xg = (xn + xx * mu[m] for m in range(6))
    r = xr @ p['rw_wr'][i]
    k = xk @ p['rw_wk'][i]
    v = xv @ p['rw_wv'][i]
    w_log = -jax.nn.softplus(-(p['rw_w0'][i] + jnp.tanh(xw @ p['rw_w1'][i]) @ p['rw_w2'][i])) - 0.5
    a = jax.nn.sigmoid(p['rw_a0'][i] + (xa @ p['rw_a1'][i]) @ p['rw_a2'][i])
    g = jax.nn.sigmoid(xg @ p['rw_g1'][i]) @ p['rw_g2'][i]
    heads = lambda t: t.astype(jnp.float32).reshape(B, T, RW_HEADS, RW_HEAD)
    kk = heads(k * p['rw_kk'][i])
    kk = kk / jnp.maximum(jnp.sqrt(jnp.sum(kk * kk, axis=-1, keepdims=True)), 1e-12)
    a_h = heads(a)
    k_h = heads(k * (1.0 + (a - 1.0) * p['rw_ka'][i]))
    r_h, v_h = heads(r), heads(v)
    decay = jnp.exp(-jnp.exp(heads(w_log)))
    s, o = wkv7_scan(s0, r_h, decay, k_h, v_h, kk, kk * a_h)
    mean = jnp.mean(o, axis=-1, keepdims=True)
    var = jnp.mean(jnp.square(o - mean), axis=-1, keepdims=True)
    o = ((o - mean) * lax.rsqrt(var + GN_EPS)).reshape(B, T, D)
    o = o * p['rw_lnx_g'][i].astype(jnp.float32) + p['rw_lnx_b'][i].astype(jnp.float32)
    bonus = jnp.sum(r_h * k_h * p['rw_rk'][i].astype(jnp.float32), axis=-1, keepdims=True) * v_h
    o = (o + bonus.reshape(B, T, D)).astype(xn.dtype)
    return (o * g) @ p['rw_wo'][i], s


def mla_attend(q_lat, q_rope, q_pos, segments):
    def block(args):
        ql, qr, qp = args
        scores = []
        for lat, kr, kp in segments:
            s = jnp.einsum('bhqr,bkr->bhqk', ql, lat) + jnp.einsum('bhqp,bkp->bhqk', qr, kr)
            s = s.astype(jnp.float32) * ATTN_SCALE
            scores.append(jnp.where(kp[None, None, None, :] <= qp[None, None, :, None], s, -jnp.inf))
        probs = jax.nn.softmax(jnp.concatenate(scores, axis=-1), axis=-1)
        outs, off = [], 0
        for lat, _, _ in segments:
            n = lat.shape[1]
            outs.append(jnp.einsum('bhqk,bkr->bhqr', probs[..., off:off + n].astype(lat.dtype), lat))
            off += n
        return sum(outs[1:], outs[0])
    B, H, Q, R = q_lat.shape
    if Q <= Q_BLOCK:
        return block((q_lat, q_rope, q_pos))
    nb = -(-Q // Q_BLOCK)
    pad = nb * Q_BLOCK - Q
    ql = jnp.pad(q_lat, ((0, 0), (0, 0), (0, pad), (0, 0)))
    qr = jnp.pad(q_rope, ((0, 0), (0, 0), (0, pad), (0, 0)))
    qp = jnp.pad(q_pos, (0, pad), mode='edge')
    to_blocks = lambda t: jnp.moveaxis(t.reshape(B, H, nb, Q_BLOCK, t.shape[-1]), 2, 0)
    out = lax.map(block, (to_blocks(ql), to_blocks(qr), qp.reshape(nb, Q_BLOCK)))
    return jnp.moveaxis(out, 0, 2).reshape(B, H, nb * Q_BLOCK, R)[:, :, :Q]


def mla_layer(xn, pos, segments, p, j):
    B, T, _ = xn.shape
    cq = rmsnorm(xn @ p['w_dq'][j], p['q_norm'][j])
    q = jnp.einsum('btc,chd->bthd', cq, p['w_uq'][j])
    q_pe = rope(q[..., QK_NOPE:], pos)
    q_lat = jnp.einsum('bthn,rhn->bhtr', q[..., :QK_NOPE], p['w_uk'])
    q_rope = jnp.transpose(q_pe, (0, 2, 1, 3))
    o_lat = mla_attend(q_lat, q_rope, pos, segments)
    o = jnp.einsum('bhtr,rhv->bthv', o_lat, p['w_uv']).reshape(B, T, MLA_HEADS * V_HEAD)
    return o @ p['w_o_mla'][j]


def trunk(x, pos, shift0, wkv0, past, p):
    new_wkv, new_shift = [], []
    for i in range(N_A_LAYERS):
        xn = rmsnorm(x, p['norm_mix'][i])
        prev = jnp.concatenate([shift0[i][:, None].astype(xn.dtype), xn[:, :-1]], axis=1)
        o, s = rwkv7_time_mix(xn, prev, wkv0[i], p, i)
        new_wkv.append(s)
        new_shift.append(xn[:, -1])
        x = x + o
        x = x + sqrelu_mlp(rmsnorm(x, p['norm_ffn'][i]), p['ffn_up'][i], p['ffn_down'][i])
    kv_in = rmsnorm(x, p['kv_norm'])
    lat = rmsnorm(kv_in @ p['w_dkv'], p['lat_norm'])
    krope = rope(kv_in @ p['w_kr'], pos)
    segments = past + ((lat, krope, pos),)
    for j in range(N_B_LAYERS):
        l = N_A_LAYERS + j
        x = x + mla_layer(rmsnorm(x, p['norm_mix'][l]), pos, segments, p, j)
        x = x + sqrelu_mlp(rmsnorm(x, p['norm_ffn'][l]), p['ffn_up'][l], p['ffn_down'][l])
    y = rmsnorm(x, p['norm_final'])
    return y, jnp.stack(new_wkv), jnp.stack(new_shift), lat, krope


def setup_inputs(seed: int = 0) -> dict:
    key = jax.random.key(seed)
    f32 = jnp.float32
    keys = iter(jax.random.split(key, 64))

    def normal(shape, scale=1.0):
        return jax.random.normal(next(keys), shape, f32) * scale

    def gain(shape):
        return 1.0 + 0.02 * jax.random.normal(next(keys), shape, f32)

    nA, nB, D, H, N = N_A_LAYERS, N_B_LAYERS, D_MODEL, RW_HEADS, RW_HEAD
    n_pages = PAST_LEN // PAGE_SIZE
    n_used = DEC_BATCH * n_pages
    n_phys = n_used + n_used // 4
    page_table = jax.random.permutation(next(keys), n_phys)[:n_used].reshape(DEC_BATCH, n_pages).astype(jnp.int32)
    return {
        'x_prompt': normal((BATCH, SEQ, D)),
        'x_sample': normal((DEC_BATCH, DEC_SEQ, D)),
        'state_wkv': normal((nA, DEC_BATCH, H, N, N), 0.5),
        'state_shift': normal((nA, DEC_BATCH, D)),
        'cache_latent': normal((n_phys, PAGE_SIZE, KV_RANK)),
        'cache_krope': normal((n_phys, PAGE_SIZE, QK_ROPE)),
        'page_table': page_table,
        'meta_tokens': normal((N_META, D)),
        'rw_mu': jax.random.uniform(next(keys), (nA, 6, D), f32),
        'rw_wr': normal((nA, D, D), D ** -0.5),
        'rw_wk': normal((nA, D, D), D ** -0.5),
        'rw_wv': normal((nA, D, D), D ** -0.5),
        'rw_wo': normal((nA, D, D), D ** -0.5),
        'rw_w0': jax.random.uniform(next(keys), (nA, D), f32, minval=-6.0, maxval=1.0),
        'rw_w1': normal((nA, D, RW_DECAY_LORA), D ** -0.5),
        'rw_w2': normal((nA, RW_DECAY_LORA, D), RW_DECAY_LORA ** -0.5),
        'rw_a0': normal((nA, D), 0.1),
        'rw_a1': normal((nA, D, RW_A_LORA), D ** -0.5),
        'rw_a2': normal((nA, RW_A_LORA, D), RW_A_LORA ** -0.5),
        'rw_g1': normal((nA, D, RW_GATE_LORA), D ** -0.5),
        'rw_g2': normal((nA, RW_GATE_LORA, D), RW_GATE_LORA ** -0.5),
        'rw_kk': 0.85 + normal((nA, D), 0.05),
        'rw_ka': 1.0 + normal((nA, D), 0.05),
        'rw_rk': normal((nA, H, N), 0.1),
        'rw_lnx_g': gain((nA, D)),
        'rw_lnx_b': normal((nA, D), 0.02),
        'norm_mix': gain((DEPTH, D)),
        'norm_ffn': gain((DEPTH, D)),
        'ffn_up': normal((DEPTH, D, D_FF), D ** -0.5),
        'ffn_down': normal((DEPTH, D_FF, D), D_FF ** -0.5),
        'kv_norm': gain((D,)),
        'w_dkv': normal((D, KV_RANK), D ** -0.5),
        'lat_norm': gain((KV_RANK,)),
        'w_kr': normal((D, QK_ROPE), D ** -0.5),
        'w_uk': normal((KV_RANK, MLA_HEADS, QK_NOPE), KV_RANK ** -0.5),
        'w_uv': normal((KV_RANK, MLA_HEADS, V_HEAD), KV_RANK ** -0.5),
        'w_dq': normal((nB, D, Q_RANK), D ** -0.5),
        'q_norm': gain((nB, Q_RANK)),
        'w_uq': normal((nB, Q_RANK, MLA_HEADS, QK_NOPE + QK_ROPE), Q_RANK ** -0.5),
        'w_o_mla': normal((nB, MLA_HEADS * V_HEAD, D), (MLA_HEADS * V_HEAD) ** -0.5),
        'norm_final': gain((D,)),
    }


def reference(x_prompt, x_sample, state_wkv, state_shift, cache_latent, cache_krope, page_table,
              meta_tokens, rw_mu, rw_wr, rw_wk, rw_wv, rw_wo, rw_w0, rw_w1, rw_w2, rw_a0, rw_a1, rw_a2,
              rw_g1, rw_g2, rw_kk, rw_ka, rw_rk, rw_lnx_g, rw_lnx_b, norm_mix, norm_ffn, ffn_up, ffn_down,
              kv_norm, w_dkv, lat_norm, w_kr, w_uk, w_uv, w_dq, q_norm, w_uq, w_o_mla, norm_final):
    p = dict(rw_mu=rw_mu, rw_wr=rw_wr, rw_wk=rw_wk, rw_wv=rw_wv, rw_wo=rw_wo, rw_w0=rw_w0, rw_w1=rw_w1,
             rw_w2=rw_w2, rw_a0=rw_a0, rw_a1=rw_a1, rw_a2=rw_a2, rw_g1=rw_g1, rw_g2=rw_g2, rw_kk=rw_kk,
             rw_ka=rw_ka, rw_rk=rw_rk, rw_lnx_g=rw_lnx_g, rw_lnx_b=rw_lnx_b, norm_mix=norm_mix,
             norm_ffn=norm_ffn, ffn_up=ffn_up, ffn_down=ffn_down, kv_norm=kv_norm, w_dkv=w_dkv,
             lat_norm=lat_norm, w_kr=w_kr, w_uk=w_uk, w_uv=w_uv, w_dq=w_dq, q_norm=q_norm, w_uq=w_uq,
             w_o_mla=w_o_mla, norm_final=norm_final)

    B = x_prompt.shape[0]
    meta = jnp.broadcast_to(meta_tokens[None].astype(x_prompt.dtype), (B, N_META, D_MODEL))
    xp = jnp.concatenate([meta, x_prompt], axis=1)
    pos_p = jnp.arange(xp.shape[1], dtype=jnp.int32)
    shift0 = jnp.zeros((N_A_LAYERS, B, D_MODEL), xp.dtype)
    wkv0 = jnp.zeros((N_A_LAYERS, B, RW_HEADS, RW_HEAD, RW_HEAD), jnp.float32)
    yp, wkv_p, shift_p, lat_p, krope_p = trunk(xp, pos_p, shift0, wkv0, (), p)
    y_prompt = yp[:, N_META:]

    DB = x_sample.shape[0]
    past_len = page_table.shape[1] * PAGE_SIZE
    lat_past = cache_latent[page_table].reshape(DB, past_len, KV_RANK)
    krope_past = cache_krope[page_table].reshape(DB, past_len, QK_ROPE)
    pos_past = jnp.arange(past_len, dtype=jnp.int32)
    pos_s = PAST_LEN + jnp.arange(x_sample.shape[1], dtype=jnp.int32)
    y_sample, wkv_s, shift_s, lat_s, krope_s = trunk(
        x_sample, pos_s, state_shift, state_wkv, ((lat_past, krope_past, pos_past),), p)

    return (y_prompt, y_sample, wkv_p, shift_p, lat_p, krope_p, wkv_s, shift_s, lat_s, krope_s)
```

```python
import math
import os
import numpy as np
from contextlib import ExitStack
import concourse.bass as bass
import concourse.mybir as mybir
from concourse.bass_utils import run_bass_kernel_spmd

F32 = mybir.dt.float32
F32R = mybir.dt.float32r
BF16 = mybir.dt.bfloat16
I32 = mybir.dt.int32
AF = mybir.ActivationFunctionType
ALU = mybir.AluOpType
AX = mybir.AxisListType

D = 1024
NH = 16
HD = 64
MH = 8
KVR = 256
QR = 384
ROPE = 64
DFF = 4096
N_META = 16
GN_EPS = 64e-5
NORM_EPS = 1e-6
ATTN_SCALE = (128 + 64) ** -0.5
DEC_SEQ = 4
NEG = -30000.0


class Trk:
    __slots__ = ("lastw", "readers", "excl")

    def __init__(self, excl=False):
        self.lastw = None
        self.readers = []
        self.excl = excl


class Prog:
    ENGS = ("pe", "act", "dve", "pool", "sp")
    NDMA = 8

    def __init__(self, nc):
        self.nc = nc
        self.es = ExitStack()
        self.q = {e: [] for e in self.ENGS}
        self.cnt = {}
        self.sems = {}
        self.waited = {e: {} for e in self.ENGS}
        for e in self.ENGS:
            self.sems[e] = self.es.enter_context(nc.semaphore("s_" + e))
            self.cnt[e] = 0
        self.dma_i = {e: 0 for e in self.ENGS}
        for e in ("sp", "act", "pool"):
            for i in range(self.NDMA):
                k = "d_%s_%d" % (e, i)
                self.sems[k] = self.es.enter_context(nc.semaphore(k))
                self.cnt[k] = 0
        self.nops = 0

    def sb(self, name, shape, dtype=F32):
        return self.es.enter_context(self.nc.sbuf_tensor(name, list(shape), dtype))

    def ps(self, name, shape, dtype=F32):
        return self.es.enter_context(self.nc.psum_tensor(name, list(shape), dtype))

    def _deps(self, eng, reads, writes):
        deps = {}

        def add(d):
            if d is None:
                return
            k, v = d
            if eng == "pe" and k == "pe":
                return
            if deps.get(k, 0) < v:
                deps[k] = v
        for t in reads:
            add(t.lastw)
        for t in writes:
            add(t.lastw)
            for r in t.readers:
                add(r)
        out = []
        w = self.waited[eng]
        for k, v in deps.items():
            if w.get(k, 0) < v:
                w[k] = v
                out.append((k, v))
        return out

    def _mark(self, reads, writes, tok):
        for t in reads:
            t.readers.append(tok)
            if len(t.readers) > 64:
                t.readers = t.readers[-64:] if False else t.readers
        for t in writes:
            t.lastw = tok
            t.readers = []

    @staticmethod
    def _split(reads, writes):
        ex = [t for t in reads if t.excl]
        if ex:
            reads = [t for t in reads if not t.excl]
            writes = list(writes) + ex
        return reads, writes

    def op(self, eng, fn, reads=(), writes=()):
        reads, writes = self._split(reads, writes)
        waits = self._deps(eng, reads, writes)
        self.cnt[eng] += 1
        tok = (eng, self.cnt[eng])
        sems = self.sems
        semh = sems[eng]

        def run(e):
            for k, v in waits:
                e.wait_ge(sems[k], v)
            fn(e).then_inc(semh, 1)
        self.q[eng].append(run)
        self._mark(reads, writes, tok)
        self.nops += 1
        return tok

    def dma(self, eng, out, in_, reads=(), writes=(), fn=None):
        i = self.dma_i[eng]
        self.dma_i[eng] += 1
        k = "d_%s_%d" % (eng, i % self.NDMA)
        reads, writes = self._split(reads, writes)
        waits = self._deps(eng, reads, writes)
        prev = self.cnt[k]
        if prev and self.waited[eng].get(k, 0) < prev:
            self.waited[eng][k] = prev
            waits.append((k, prev))
        self.cnt[k] += 16
        tok = (k, self.cnt[k])
        sems = self.sems
        semh = sems[k]

        def run(e):
            for kk, v in waits:
                e.wait_ge(sems[kk], v)
            if fn is None:
                e.dma_start(out=out, in_=in_).then_inc(semh, 16)
            else:
                fn(e).then_inc(semh, 16)
        self.q[eng].append(run)
        self._mark(reads, writes, tok)
        self.nops += 1
        return tok

    def barrier(self):
        snap = [(k, v) for k, v in self.cnt.items() if v > 0]
        sems = self.sems
        for eng in self.ENGS:
            waits = []
            for k, v in snap:
                if eng == "pe" and k == "pe":
                    continue
                if self.waited[eng].get(k, 0) < v:
                    self.waited[eng][k] = v
                    waits.append((k, v))

            def run(e, waits=waits):
                for k, v in waits:
                    e.wait_ge(sems[k], v)
            self.q[eng].append(run)

    def finish(self, final_trks):
        waits = self._deps("sp", final_trks, final_trks)
        sems = self.sems

        def run(e):
            for k, v in waits:
                e.wait_ge(sems[k], v)
        self.q["sp"].append(run)
        nc = self.nc
        q = self.q
        with nc.allow_low_precision("bf16 matmul operands, fp32 accumulation"), nc.Block() as block:
            @block.tensor
            def _(e):
                for f in q["pe"]:
                    f(e)

            @block.scalar
            def _(e):
                for f in q["act"]:
                    f(e)

            @block.vector
            def _(e):
                for f in q["dve"]:
                    f(e)

            @block.gpsimd
            def _(e):
                for f in q["pool"]:
                    f(e)

            @block.sync
            def _(e):
                for f in q["sp"]:
                    f(e)
        self.es.close()


def _levels(C):
    return max(1, int(math.ceil(math.log2(C))))


def make_consts(cfg):
    T = cfg["T"]
    NB = cfg["NB"]
    past = cfg["PAST"]
    c = {}
    c["ident"] = np.eye(128, dtype=np.float32)
    rot = np.zeros((64, 64), np.float32)
    for m in range(32):
        rot[m + 32, m] = -1.0
        rot[m, m + 32] = 1.0
    c["rot"] = rot
    half = 32
    inv_freq = (10000.0 ** (-np.arange(half, dtype=np.float32) / half)).astype(np.float32)

    def tables(pos):
        ang = pos.astype(np.float32)[:, None] * inv_freq[None, :]
        return np.cos(ang).astype(np.float32), np.sin(ang).astype(np.float32)
    cp, sp_ = tables(np.arange(T))
    cs, ss = tables(past + (np.arange(NB * DEC_SEQ) % DEC_SEQ))
    c["rope_tok_p"] = np.concatenate([cp, sp_], axis=1)
    c["rope_tok_s"] = np.concatenate([cs, ss], axis=1)
    c["rope_fm_p"] = np.concatenate([cp.T, cp.T, sp_.T, sp_.T], axis=0).astype(np.float32)
    c["rope_fm_s"] = np.concatenate([cs.T, cs.T, ss.T, ss.T], axis=0).astype(np.float32)
    for C in (64, 16, 4):
        i = np.arange(C)[:, None]
        t = np.arange(C)[None, :]
        su = (i < t).astype(np.float32)
        sl = (i > t).astype(np.float32)
        iu = (i <= t).astype(np.float32)
        c["smask%d" % C] = np.concatenate([su, sl, su, iu, iu], axis=1)
    qi = np.arange(128)[:, None]
    ki = np.arange(128)[None, :]
    c["cmask"] = np.where(ki <= qi, 0.0, NEG).astype(np.float32)
    sm = np.full((32, NB, NB * DEC_SEQ), NEG, np.float32)
    for b in range(NB):
        for h in range(MH):
            for t in range(DEC_SEQ):
                for t2 in range(t + 1):
                    sm[h * DEC_SEQ + t, b, b * DEC_SEQ + t2] = 0.0
    c["smask_s"] = sm
    c["cmod"] = (np.arange(128) % 16).astype(np.float32).reshape(128, 1)
    return c


def build(cfg):
    SEQ = cfg["SEQ"]
    T = cfg["T"]
    NB = cfg["NB"]
    NPG = cfg["NPG"]
    NPHYS = cfg["NPHYS"]
    NS = NB * DEC_SEQ
    NT = 256
    NKB = (T + 127) // 128
    NGRP = NPG // 8

    nc = bass.Bass("TRN2", target_bir_lowering=False)

    def din(name, shape, dt=F32):
        return nc.dram_tensor(name, list(shape), dt, kind="ExternalInput").ap()

    def dout(name, shape, dt=F32):
        return nc.dram_tensor(name, list(shape), dt, kind="ExternalOutput").ap()

    xp = din("xp", [SEQ, D])
    meta = din("meta", [N_META, D])
    xs = din("xs", [NS, D])
    swkv = din("swkv", [NB, NH, HD, HD])
    sshift = din("sshift", [NB, D])
    c_lat = din("c_lat", [NPHYS * 16, 8 * KVR])
    c_kr = din("c_kr", [NPHYS * 16, 8 * ROPE])
    ptrep = din("ptrep", [NB, 128, NGRP], I32)
    vec128 = din("vec128", [99, 128])
    vec64 = din("vec64", [112, 64])
    latnorm = din("latnorm", [1, KVR])
    W = {}
    for nm, shp in [("rw_wr", [D, D]), ("rw_wk", [D, D]), ("rw_wv", [D, D]), ("rw_wo", [D, D]),
                    ("rw_w1", [D, 64]), ("rw_w2", [64, D]), ("rw_a1", [D, 64]), ("rw_a2", [64, D]),
                    ("rw_g1", [D, 128]), ("rw_g2", [128, D]),
                    ("ffn_up0", [D, DFF]), ("ffn_down0", [DFF, D]), ("ffn_up1", [D, DFF]), ("ffn_down1", [DFF, D]),
                    ("w_dkv", [D, KVR]), ("w_kr", [D, ROPE]), ("w_uk", [KVR, MH * 128]), ("w_uv", [KVR, MH * 128]),
                    ("w_dq", [D, QR]), ("w_uq", [QR, MH * 192]), ("w_o_mla", [D, D])]:
        W[nm] = din(nm, shp)
    C = {}
    for nm, shp in [("ident", [128, 128]), ("rot", [64, 64]), ("rope_tok_p", [T, 64]), ("rope_tok_s", [NS, 64]),
                    ("rope_fm_p", [128, T]), ("rope_fm_s", [128, NS]), ("smask64", [64, 320]), ("smask16", [16, 80]),
                    ("smask4", [4, 20]), ("cmask", [128, 128]), ("smask_s", [32, NB, NS]), ("cmod", [128, 1])]:
        C[nm] = din("c_" + nm, shp)
    o_yp = dout("o_yp", [SEQ, D])
    o_ys = dout("o_ys", [NS, D])
    o_wkvp = dout("o_wkvp", [NH, HD, HD])
    o_shiftp = dout("o_shiftp", [8, 128])
    o_latp = dout("o_latp", [T, KVR])
    o_krp = dout("o_krp", [T, ROPE])
    o_wkvs = dout("o_wkvs", [NB, NH, HD, HD])
    o_shifts = dout("o_shifts", [NB, D])
    o_lats = dout("o_lats", [NS, KVR])
    o_krs = dout("o_krs", [NS, ROPE])

    p = Prog(nc)
    out_trk = Trk()

    NFB = 5
    pbank = [p.ps("pb%d" % i, [128, 512], F32) for i in range(NFB)]
    pbt = [Trk(True) for _ in range(NFB)]
    pbi = [0]
    obank = p.ps("obank", [128, 512], F32)
    obankt = Trk(True)
    hbank = [p.ps("hb%d" % i, [128, 1024], BF16) for i in range(2)]
    hbt = [Trk(True) for _ in range(2)]
    hbi = [0]

    def bank():
        i = pbi[0] % NFB
        pbi[0] += 1
        return pbank[i], pbt[i]

    def bbank():
        i = hbi[0] % 2
        hbi[0] += 1
        return hbank[i], hbt[i]

    def MM(out, pairs, reads, wtrk, start=True, stop=True):
        n = len(pairs)

        def f(e):
            ins = None
            for i, (l, r) in enumerate(pairs):
                ins = e.matmul(out, l, r, start=(start and i == 0), stop=(stop and i == n - 1))
            return ins
        p.op("pe", f, reads=reads, writes=[wtrk])

    def TR(out, in_, ident_ap, reads, wtrk):
        p.op("pe", lambda e: e.transpose(out, in_, ident_ap), reads=reads, writes=[wtrk])

    def ACT(out, in_, func, reads, writes, bias=None, scale=None, accum=None):
        kw = {}
        if bias is not None:
            kw["bias"] = bias
        if scale is not None:
            kw["scale"] = scale
        if accum is not None:
            kw["accum_out"] = accum
        p.op("act", lambda e: e.activation(out, in_, func, **kw), reads=reads, writes=writes)

    def TS(eng, out, in0, s1, s2, op0, op1, reads, writes):
        if s2 is None:
            p.op(eng, lambda e: e.tensor_scalar(out, in0, s1, None, op0), reads=reads, writes=writes)
        else:
            p.op(eng, lambda e: e.tensor_scalar(out, in0, s1, s2, op0, op1), reads=reads, writes=writes)

    def TT(eng, out, in0, in1, op, reads, writes):
        p.op(eng, lambda e: e.tensor_tensor(out, in0, in1, op), reads=reads, writes=writes)

    def STT(out, in0, scalar, in1, op0, op1, reads, writes):
        p.op("dve", lambda e: e.scalar_tensor_tensor(out, in0, scalar, in1, op0, op1), reads=reads, writes=writes)

    def CP(eng, out, in_, reads, writes):
        if eng == "act":
            p.op("act", lambda e: e.copy(out, in_), reads=reads, writes=writes)
        else:
            p.op(eng, lambda e: e.tensor_copy(out, in_), reads=reads, writes=writes)

    def RECIP(out, in_, reads, writes):
        p.op("dve", lambda e: e.reciprocal(out, in_), reads=reads, writes=writes)

    def MEMSET(eng, ap, val, writes):
        p.op(eng, lambda e: e.memset(ap, val), writes=writes)

    def OUT(dst, src, reads):
        p.dma("sp", dst, src, reads=reads, writes=[out_trk])

    ct = Trk()
    ident = p.sb("ident", [128, 128], F32)
    identb = p.sb("identb", [128, 128], BF16)
    identr = p.sb("identr", [64, 64], F32R)
    ones_r = p.sb("ones_r", [128, 128], F32R)
    onesf = p.sb("onesf", [128, 64], F32)
    rotm = p.sb("rotm", [64, 64], BF16)
    cmask = p.sb("cmask", [128, 128], F32)
    smk = {Cc: p.sb("smask%d" % Cc, [Cc, 5 * Cc], F32) for Cc in (64, 16, 4)}
    smask_s = p.sb("smask_s", [32, NB, NS], F32)
    cmod = p.sb("cmod", [128, 1], F32)
    lnbc = p.sb("lnbc", [128, KVR], F32)
    v128 = p.sb("v128", [128, 99], F32)
    v64 = p.sb("v64", [64, 112], F32)
    p.dma("sp", ident[:], C["ident"], writes=[ct])
    p.dma("pool", rotm[:], C["rot"], writes=[ct])
    p.dma("sp", cmask[:], C["cmask"], writes=[ct])
    for Cc in (64, 16, 4):
        p.dma("sp", smk[Cc][:], C["smask%d" % Cc], writes=[ct])
    p.dma("sp", smask_s[:], C["smask_s"], writes=[ct])
    p.dma("sp", cmod[:], C["cmod"], writes=[ct])
    p.dma("sp", lnbc[:], latnorm.partition_broadcast(128), writes=[ct])
    CP("dve", identb[:], ident[:], [ct], [ct])
    CP("dve", identr[:], ident[0:64, 0:64], [ct], [ct])
    onesb = p.sb("onesb", [128, 128], F32)
    MEMSET("pool", onesb[:], 1.0, [ct])
    CP("dve", ones_r[:], onesb[:], [ct], [ct])
    MEMSET("pool", onesf[:], 1.0, [ct])
    stg = p.sb("stg", [128, 1024], F32)
    stgt = Trk()
    p.dma("sp", stg[0:99, 0:128], vec128, writes=[stgt])
    p.dma("sp", stg[0:112, 128:192], vec64, writes=[stgt])
    b_, bt_ = bank()
    TR(b_[:, 0:99], stg[0:99, 0:128], ident[0:99, 0:99], [stgt, ct], bt_)
    TR(b_[0:64, 128:240], stg[0:112, 128:192], ident[0:112, 0:112], [stgt, ct], bt_)
    CP("dve", v128[:], b_[:, 0:99], [bt_], [ct])
    CP("dve", v64[:], b_[0:64, 128:240], [bt_], [ct])
    nw0 = p.sb("nw0", [64, 32], F32)
    TS("dve", nw0[:], v64[:, 0:32], -1.0, None, ALU.mult, None, [ct], [ct])

    wres_t = Trk()
    w1s = p.sb("w1s", [128, 8, 64], BF16)
    a1s = p.sb("a1s", [128, 8, 64], BF16)
    g1s = p.sb("g1s", [128, 8, 128], BF16)
    w2s = p.sb("w2s", [64, D], BF16)
    a2s = p.sb("a2s", [64, D], BF16)
    g2s = p.sb("g2s", [128, D], BF16)
    wukT = p.sb("wukT", [128, MH, KVR], BF16)

    def kcv(ap):
        return ap.rearrange("(kc p) m -> p kc m", p=128)
    p.dma("pool", w1s[:], kcv(W["rw_w1"]), writes=[wres_t])
    p.dma("pool", a1s[:], kcv(W["rw_a1"]), writes=[wres_t])
    p.dma("pool", g1s[:], kcv(W["rw_g1"]), writes=[wres_t])
    p.dma("pool", w2s[:], W["rw_w2"], writes=[wres_t])
    p.dma("pool", a2s[:], W["rw_a2"], writes=[wres_t])
    p.dma("pool", g2s[:], W["rw_g2"], writes=[wres_t])

    NSLAB = 4
    slabs = [p.sb("slab%d" % i, [128, 4096], BF16) for i in range(NSLAB)]
    slabt = [Trk() for _ in range(NSLAB)]
    slabi = [0]

    def load_slab(parts):
        i = slabi[0] % NSLAB
        slabi[0] += 1
        for (vf, src) in parts:
            p.dma("pool", vf(slabs[i]), src, writes=[slabt[i]])
        return slabs[i], slabt[i]

    def v_k8(s):
        return s[:].rearrange("p (a b) -> p a b", a=8)

    def v_k4(s):
        return s[:].rearrange("p (a b) -> p a b", a=4)

    def v_h16(s):
        return s[0:64, :].rearrange("p (a b) -> p a b", a=16)

    sl_, slt_ = load_slab([(lambda s: s[:, 0:2048].rearrange("p (a b) -> p a b", a=2), kcv(W["w_uk"]))])
    wuk_nat = sl_[:, 0:2048].rearrange("p (a b) -> p a b", a=2)
    for h in range(MH):
        hb_, hbt_ = bbank()
        for rc in range(2):
            TR(hb_[:, rc * 128:(rc + 1) * 128], wuk_nat[:, rc, h * 128:(h + 1) * 128], identb[:], [slt_, ct], hbt_)
        CP("dve", wukT[:, h, :], hb_[:, 0:256], [hbt_], [wres_t])

    NMAX = NT
    X = p.sb("X", [128, 8, NMAX], F32)
    Xt = [Trk() for _ in range(8)]
    XNX = p.sb("XNX", [128, 16, NMAX], BF16)
    XNb = XNX[:, 0:8, :]
    XXb = XNX[:, 8:16, :]
    XNt = Trk()
    XXt = XNt
    XM = [p.sb("XM%d" % i, [128, 8, NMAX], BF16) for i in range(4)]
    XMt = [Trk() for _ in range(4)]
    OGs = XNX[0:64, :, :]
    OGt = XNt
    carry = p.sb("carry", [128, 8], F32)
    carryt = Trk()
    rstd = p.sb("rstd", [128, NMAX], F32)
    rstdt = Trk()
    sq = [p.sb("sq%d" % i, [128, NMAX], F32R) for i in range(2)]
    sqt = [Trk() for _ in range(2)]
    sqi = [0]
    HW = p.sb("HW", [64, NMAX], BF16)
    HA = p.sb("HA", [64, NMAX], BF16)
    HG = p.sb("HG", [128, NMAX], BF16)
    Hlt = Trk()
    tmpA = p.sb("tmpA", [128, NMAX], F32)
    tmpAt = Trk()
    NG = 19
    Gp = [p.sb("G%d" % i, [128, 256], F32) for i in range(NG)]
    Gt = [Trk() for _ in range(NG)]

    def g32(i, parts=128, n=None):
        return Gp[i][0:parts, 0:(n if n is not None else 256)]

    def gbf(i, parts=128):
        return Gp[i][0:parts, :].bitcast(BF16)

    GR = [p.sb("GR%d" % i, [64, 256], F32R) for i in range(5)]
    GRt = [Trk() for _ in range(5)]

    NCH = 4
    TM = p.sb("TM", [64, NCH, 192], F32R)
    TMt = [Trk() for _ in range(NCH)]
    MMs = p.sb("MMs", [64, NCH, 320], F32R)
    MMt = [Trk() for _ in range(NCH)]
    NN = [p.sb("NN%d" % i, [64, NCH, 128], F32R) for i in range(2)]
    NNt = [Trk() for _ in range(2)]
    TT_ = [p.sb("TT%d" % i, [64, NCH, 64], F32R) for i in range(2)]
    TTt = [Trk() for _ in range(2)]
    WT = p.sb("WT", [64, 64], F32R)
    WTt = Trk()
    UT = p.sb("UT", [64, 64], F32R)
    UTt = Trk()
    SNAT = stg[0:64, :].rearrange("p (a b) -> p a b", a=16)
    SNATt = stgt
    kvcol = p.sb("kvcol", [128, 8], F32)
    kvcolt = Trk()
    ropet = p.sb("ropet", [128, 64], F32)
    ropett = Trk()
    ropef = p.sb("ropef", [64, NMAX], F32)
    rope_s2 = p.sb("rope_s2", [64, NMAX], F32)
    ropeft = Trk()
    QPr = p.sb("QPr", [64, NMAX], BF16)
    QPf = p.sb("QPf", [64, NMAX], F32)
    QPt = Trk()
    QLh = p.sb("QLh", [128, 2, NMAX], BF16)
    QPEh = p.sb("QPEh", [64, NMAX], BF16)
    QLt = Trk()
    OLT = p.sb("OLT", [128, 2, NMAX], BF16)
    OLTt = Trk()
    sm_ = p.sb("sm_", [128, 16], F32)
    smt = Trk()
    idxf = p.sb("idxf", [128, NGRP], F32)
    idxi = p.sb("idxi", [128, NGRP], I32)
    idxr = p.sb("idxr", [128, NGRP], I32)
    idxt = Trk()
    QB = p.sb("QB", [128, 2, 32], BF16)
    QPB = p.sb("QPB", [64, 32], BF16)
    QBt = Trk()
    shT = p.sb("shT", [128, 8, NB], F32)
    shTt = Trk()
    LATTs = p.sb("LATTs", [128, 2, 128], BF16)
    KRTs = p.sb("KRTs", [64, 128], BF16)
    LATTOKs = p.sb("LATTOKs", [128, 1, KVR], BF16)
    KVst = Trk()

    AW = max(NKB * 320 + 2048, 6 * 2048 + NB * 128) + 64
    arena = p.sb("arena", [128, AW], F32)
    _off = [0]

    def carve(nwords, dtype=F32, parts=128):
        a = arena[0:parts, _off[0]:_off[0] + nwords]
        _off[0] += nwords
        if dtype is not F32:
            a = a.bitcast(dtype)
        return a
    _off[0] = 0
    LATT = carve(NKB * 128, BF16).rearrange("p (a b) -> p a b", a=2)
    KRT = carve(NKB * 64, BF16, 64)
    LATTOK = carve(NKB * 128, BF16).rearrange("p (a b) -> p a b", a=NKB)
    S32 = carve(NH * 64, F32, 64).rearrange("p (a b) -> p a b", a=NH)
    SR = p.sb("SR", [64, max(NH, NB), 64], F32R)
    KVt = Trk()
    S32t = [Trk() for _ in range(NH)]
    SRt = [Trk() for _ in range(NH)]
    _off[0] = 0
    stL = [carve(8 * KVR).rearrange("p (a b) -> p a b", a=8) for _ in range(2)]
    stK = [carve(8 * ROPE).rearrange("p (a b) -> p a b", a=8) for _ in range(2)]
    stt_ = [Trk() for _ in range(2)]
    stLb = carve(8 * KVR // 2, BF16).rearrange("p (a b) -> p a b", a=8)
    stKb = carve(8 * ROPE // 2, BF16).rearrange("p (a b) -> p a b", a=8)
    stbt = Trk()
    LTs = carve(1024, BF16).rearrange("p (a b) -> p a b", a=2)
    KTs = carve(512, BF16, 64)
    LTst = Trk()
    SS32 = carve(NB * 64, F32, 64).rearrange("p (a b) -> p a b", a=NB)
    SSR = SR
    SSt = [Trk() for _ in range(NB)]
    SSRt = [Trk() for _ in range(NB)]
    QLs = carve(MH * 2 * NS // 2, BF16).rearrange("p (h r n) -> p h r n", h=MH, r=2)
    OLTs = carve(MH * 2 * NS // 2, BF16).rearrange("p (h r n) -> p h r n", h=MH, r=2)
    QPEs = carve(MH * NS // 2, BF16, 64).rearrange("p (h n) -> p h n", h=MH)
    QLst = Trk()
    OLTst = Trk()

    epsc = p.sb("epsc", [128, 4], F32)
    MEMSET("pool", epsc[:, 0:1], NORM_EPS, [ct])
    MEMSET("pool", epsc[:, 1:2], GN_EPS, [ct])

    def load_x_block(rows_src, nt, col0):
        for (src, r0, n) in rows_src:
            p.dma("sp", stg[r0:r0 + n, :], src, writes=[stgt])
        for g in range(2):
            b, bt = bank()
            for j in range(4):
                kc = g * 4 + j
                TR(b[:, j * 128:j * 128 + nt], stg[0:nt, kc * 128:(kc + 1) * 128], ident[0:nt, 0:nt], [stgt, ct], bt)
            for j in range(4):
                kc = g * 4 + j
                CP("act" if j % 2 else "dve", X[:, kc, col0:col0 + nt], b[:, j * 128:j * 128 + nt], [bt], [Xt[kc]])

    def rms_rstd(N, srcs, nfeat):
        b, bt = bank()
        n = len(srcs)
        for i_, (ap, t) in enumerate(srcs):
            i = sqi[0] % 2
            sqi[0] += 1
            ACT(sq[i][:, :N], ap, AF.Square, [t], [sqt[i]])
            MM(b[:, :N], [(ones_r[:, :], sq[i][:, :N])], [sqt[i], ct], bt, start=(i_ == 0), stop=(i_ == n - 1))
        ACT(rstd[:, :N], b[:, :N], AF.Ln, [bt, ct], [rstdt], scale=1.0 / nfeat, bias=epsc[:, 0:1])
        ACT(rstd[:, :N], rstd[:, :N], AF.Exp, [rstdt], [rstdt], scale=-0.5)

    def norm_to_bf16(N, gcol0, dst, dstt):
        rms_rstd(N, [(X[:, kc, :N], Xt[kc]) for kc in range(8)], D)
        for kc in range(8):
            STT(dst[:, kc, :N], X[:, kc, :N], v128[:, gcol0 + kc:gcol0 + kc + 1], rstd[:, :N], ALU.mult, ALU.mult,
                [Xt[kc], rstdt, ct], [dstt])

    RI = dict(r=0, k=1, v=2, e1=3, L=4, Lex=5, a=6, kkraw=7, k2=8, P=9, Pex=10, Pinv=11, kka=12, G=13, t1=14, gsb=15, xnf=16)
    RI.update(logd=RI["e1"], kk=RI["kkraw"], bonus=RI["e1"], Bh=RI["Pinv"], Kh=RI["a"], cen=RI["Lex"], y=RI["Pex"])
    RR = dict(At=0, Bt=1, Kt=2, Rt=3, kksq=4)
    RR.update(rkk=RR["kksq"], osb=RR["At"], censq=RR["Bt"])

    def rwkv_layer(N, C_, chunks, mode):
        msk = smk[C_]
        rms_rstd(N, [(X[:, kc, :N], Xt[kc]) for kc in range(8)], D)
        xnf, xnft = g32(RI["xnf"], 128, N), Gt[RI["xnf"]]
        for kc in range(8):
            STT(xnf, X[:, kc, :N], v128[:, kc:kc + 1], rstd[:, :N], ALU.mult, ALU.mult, [Xt[kc], rstdt, ct], [xnft])
            CP("act", XNb[:, kc, :N], xnf, [xnft], [XNt])
            if mode == "p":
                if N > 1:
                    TT("dve", XXb[:, kc, 1:N], xnf[:, 0:N - 1], xnf[:, 1:N], ALU.subtract, [xnft], [XXt])
                TT("dve", XXb[:, kc, 0:1], carry[:, kc:kc + 1], xnf[:, 0:1], ALU.subtract, [xnft, carryt], [XXt])
                CP("dve", carry[:, kc:kc + 1], xnf[:, N - 1:N], [xnft, carryt], [carryt])
            else:
                xn4 = xnf.rearrange("p (b t) -> p b t", t=DEC_SEQ)
                xx4 = XXb[:, kc, :N].rearrange("p (b t) -> p b t", t=DEC_SEQ)
                TT("dve", xx4[:, :, 1:DEC_SEQ], xn4[:, :, 0:DEC_SEQ - 1], xn4[:, :, 1:DEC_SEQ], ALU.subtract, [xnft], [XXt])
                TT("dve", xx4[:, :, 0:1], shT[:, kc, :].unsqueeze(2), xn4[:, :, 0:1], ALU.subtract, [xnft, shTt], [XXt])
                CP("dve", shT[:, kc, :].unsqueeze(2), xn4[:, :, DEC_SEQ - 1:DEC_SEQ], [xnft, shTt, XXt], [shTt])

        def make_xm(m, dst, dstt):
            for kc in range(8):
                STT(dst[:, kc, :N], XXb[:, kc, :N], v128[:, 48 + m * 8 + kc:48 + m * 8 + kc + 1], XNb[:, kc, :N],
                    ALU.mult, ALU.add, [XXt, XNt, ct], [dstt])
        for (m, wsb, M_) in ((1, w1s, 64), (4, a1s, 64), (5, g1s, 128)):
            make_xm(m, XM[3], XMt[3])
            b, bt = bank()
            MM(b[0:M_, :N], [(wsb[:, kc, :], XM[3][:, kc, :N]) for kc in range(8)], [XMt[3], wres_t], bt)
            if m == 1:
                ACT(tmpA[0:64, :N], b[0:64, :N], AF.Exp, [bt], [tmpAt], scale=2.0)
                TS("dve", tmpA[0:64, :N], tmpA[0:64, :N], 1.0, None, ALU.add, None, [tmpAt], [tmpAt])
                RECIP(tmpA[0:64, :N], tmpA[0:64, :N], [tmpAt], [tmpAt])
                TS("dve", HW[:, :N], tmpA[0:64, :N], -2.0, 1.0, ALU.mult, ALU.add, [tmpAt], [Hlt])
            elif m == 4:
                CP("act", HA[:, :N], b[0:64, :N], [bt], [Hlt])
            else:
                ACT(tmpA[:, :N], b[:, :N], AF.Exp, [bt], [tmpAt], scale=-1.0)
                TS("dve", tmpA[:, :N], tmpA[:, :N], 1.0, None, ALU.add, None, [tmpAt], [tmpAt])
                RECIP(HG[:, :N], tmpA[:, :N], [tmpAt], [Hlt])
        for i, m in enumerate((0, 2, 3)):
            make_xm(m, XM[i], XMt[i])

        def T_(n):
            return g32(RI[n], 64, N), Gt[RI[n]]

        def R_(n):
            return GR[RR[n]][:, 0:N], GRt[RR[n]]

        def Tc(n, c0, cs):
            return Gp[RI[n]][0:64, c0:c0 + cs]

        def Rc(n, c0, cs):
            return GR[RR[n]][:, c0:c0 + cs]
        wnames = ("rw_wr", "rw_wk", "rw_wv")
        for h in range(NH):
            if h % 8 == 0:
                g = h // 8
                sl = [load_slab([(v_k8, kcv(W[nm])[:, :, g * 512:(g + 1) * 512])]) for nm in wnames]
            if mode == "s":
                p.dma("sp", SNAT[:, 0:NB, :], swkv[:, h, :, :].rearrange("b v k -> v b k"), writes=[SNATt])
                for g0 in range(0, NB, 8):
                    b, bt = bank()
                    nb_ = min(8, NB - g0)
                    for j in range(nb_):
                        TR(b[0:64, j * 64:(j + 1) * 64], SNAT[:, g0 + j, :], ident[0:64, 0:64], [SNATt, ct], bt)
                    for j in range(nb_):
                        CP("dve", SS32[:, g0 + j, :], b[0:64, j * 64:(j + 1) * 64], [bt], [SSt[g0 + j]])
                        CP("act", SSR[:, g0 + j, :], b[0:64, j * 64:(j + 1) * 64], [bt], [SSRt[g0 + j]])
            hc = (h % 8) * 64
            r, rt = T_("r")
            k, kt = T_("k")
            v, vt = T_("v")
            for i, (dst, dt_) in enumerate(((r, rt), (k, kt), (v, vt))):
                b, bt = bank()
                sv = v_k8(sl[i][0])
                MM(b[0:64, :N], [(sv[:, kc, hc:hc + 64], XM[i][:, kc, :N]) for kc in range(8)], [XMt[i], sl[i][1]], bt)
                CP("act" if i == 1 else "dve", dst, b[0:64, :N], [bt], [dt_])
            zb, zbt = bank()
            if 3 * N <= 512:
                zps, aps, gps = zb[0:64, 0:N], zb[0:64, N:2 * N], zb[0:64, 2 * N:3 * N]
                gbt_ = zbt
            else:
                gb_, gbt_ = bank()
                zps, aps, gps = zb[0:64, 0:N], zb[0:64, N:2 * N], gb_[0:64, 0:N]
            MM(zps, [(w2s[:, h * 64:(h + 1) * 64], HW[:, :N])], [Hlt, wres_t], zbt)
            MM(aps, [(a2s[:, h * 64:(h + 1) * 64], HA[:, :N])], [Hlt, wres_t], zbt)
            MM(gps, [(g2s[:, h * 64:(h + 1) * 64], HG[:, :N])], [Hlt, wres_t], gbt_)
            e1, e1t = T_("e1")
            L, Lt = T_("L")
            Lex, Lext = T_("Lex")
            a, at = T_("a")
            kkraw, kkrawt = T_("kkraw")
            k2, k2t = T_("k2")
            P_, Pt_ = T_("P")
            Pex, Pext = T_("Pex")
            Pinv, Pinvt = T_("Pinv")
            kka, kkat = T_("kka")
            G, Gt_ = T_("G")
            t1, t1t = T_("t1")
            gsb, gsbt = T_("gsb")
            At, Att = R_("At")
            Bt, Btt = R_("Bt")
            Kt, Ktt = R_("Kt")
            Rt, Rtt = R_("Rt")
            kksq, kksqt = R_("kksq")
            logd, logdt = e1, e1t
            kk, kkt = kkraw, kkrawt
            ACT(e1, zps, AF.Exp, [zbt, ct], [e1t], scale=-1.0, bias=nw0[:, h:h + 1])
            TS("dve", e1, e1, 1.0, None, ALU.add, None, [e1t], [e1t])
            RECIP(e1, e1, [e1t], [e1t])
            TS("pool", logd, e1, -math.exp(-0.5), None, ALU.mult, None, [e1t], [e1t])
            for (c0, cs) in chunks:
                p.op("dve", lambda e, c0=c0, cs=cs: e.tensor_tensor_scan(Tc("L", c0, cs), onesf[0:64, 0:cs], Tc("logd", c0, cs), 0.0, ALU.mult, ALU.add),
                     reads=[logdt, ct], writes=[Lt])
            TT("pool", Lex, L, logd, ALU.subtract, [Lt, logdt], [Lext])
            ACT(a, aps, AF.Exp, [zbt, ct], [at], scale=-1.0, bias=nw0[:, 16 + h:17 + h])
            TS("dve", a, a, 1.0, None, ALU.add, None, [at], [at])
            RECIP(a, a, [at], [at])
            CP("act", gsb, gps, [gbt_], [gsbt])
            TS("pool", kkraw, k, v64[:, 32 + h:33 + h], None, ALU.mult, None, [kt, ct], [kkrawt])
            ACT(kksq, kkraw, AF.Square, [kkrawt], [kksqt])
            b, bt = bank()
            MM(b[0:64, :N], [(ones_r[0:64, 0:64], kksq)], [kksqt, ct], bt)
            TS("dve", t1, b[0:64, :N], 1e-24, None, ALU.max, None, [bt], [t1t])
            ACT(t1, t1, AF.Ln, [t1t], [t1t])
            ACT(t1, t1, AF.Exp, [t1t], [t1t], scale=-0.5)
            TT("dve", kk, kkraw, t1, ALU.mult, [kkrawt, t1t], [kkt])
            TS("dve", t1, a, -1.0, v64[:, 48 + h:49 + h], ALU.add, ALU.mult, [at, ct, t1t], [t1t])
            STT(k2, t1, 1.0, k, ALU.add, ALU.mult, [t1t, kt], [k2t])
            ACT(P_, L, AF.Exp, [Lt], [Pt_])
            ACT(Pex, Lex, AF.Exp, [Lext], [Pext])
            ACT(Pinv, L, AF.Exp, [Lt], [Pinvt], scale=-1.0)
            STT(At, kk, -1.0, Pex, ALU.mult, ALU.mult, [kkt, Pext], [Att])
            TT("pool", kka, kk, a, ALU.mult, [kkt, at], [kkat])
            TT("dve", Bt, kka, Pinv, ALU.mult, [kkat, Pinvt], [Btt])
            TT("pool", Kt, k2, Pinv, ALU.mult, [k2t, Pinvt], [Ktt])
            TT("dve", Rt, r, P_, ALU.mult, [rt, Pt_], [Rtt])
            for (c0, cs) in chunks:
                ACT(Tc("G", c0, cs), Tc("L", c0, cs), AF.Exp, [Lt], [Gt_], scale=-1.0, bias=Tc("L", c0 + cs - 1, 1))
            Bh, Bht = T_("Bh")
            Kh, Kht = T_("Kh")
            TT("pool", Bh, kka, G, ALU.mult, [kkat, Gt_, Pinvt], [Bht])
            TT("dve", Kh, k2, G, ALU.mult, [k2t, Gt_, at], [Kht])
            rkk, rkkt = R_("rkk")
            bonus, bonust = T_("bonus")
            STT(rkk, r, v64[:, 64 + h:65 + h], k2, ALU.mult, ALU.mult, [rt, k2t, ct, kksqt], [rkkt])
            b, bt = bank()
            MM(b[0:64, :N], [(ones_r[0:64, 0:64], rkk)], [rkkt, ct], bt)
            TT("dve", bonus, v, b[0:64, :N], ALU.mult, [vt, bt, e1t], [bonust])
            ob, obt = obank, obankt
            for ug in range(0, len(chunks), NCH):
                cl = chunks[ug:ug + NCH]
                nch = len(cl)
                for ci, (c0, cs) in enumerate(cl):
                    b, bt = bank()
                    TR(b[0:cs, 0:64], Tc("Bh", c0, cs), ident[0:64, 0:64], [Bht, ct], bt)
                    TR(b[0:cs, 64:128], Tc("Kh", c0, cs), ident[0:64, 0:64], [Kht, ct], bt)
                    TR(b[0:cs, 128:192], Tc("v", c0, cs), ident[0:64, 0:64], [vt, ct], bt)
                    CP("act", TM[0:cs, ci, :], b[0:cs, 0:192], [bt], [TMt[ci]])
                upb = max(1, 512 // (5 * C_))
                for u0 in range(0, nch, upb):
                    b, bt = bank()
                    us = list(range(u0, min(nch, u0 + upb)))
                    for ui, ci in enumerate(us):
                        c0, cs = cl[ci]
                        o = ui * 5 * C_
                        A_, B_, K_, R__ = Rc("At", c0, cs), Rc("Bt", c0, cs), Rc("Kt", c0, cs), Rc("Rt", c0, cs)
                        for kind, (l_, r_) in enumerate(((B_, A_), (A_, B_), (K_, A_), (B_, R__), (K_, R__))):
                            MM(b[0:cs, o + kind * C_:o + kind * C_ + cs], [(l_, r_)], [Att, Btt, Ktt, Rtt], bt)
                    for ui, ci in enumerate(us):
                        c0, cs = cl[ci]
                        o = ui * 5 * C_
                        TT("dve", MMs[0:cs, ci, 0:5 * C_], b[0:cs, o:o + 5 * C_], msk[0:cs, :], ALU.mult, [bt, ct], [MMt[ci]])
                csz = cl[0][1]
                lv = _levels(csz)
                for ci in range(nch):
                    TT("dve", TT_[0][0:csz, ci, 0:csz], MMs[0:csz, ci, 0:csz], identr[0:csz, 0:csz], ALU.add, [MMt[ci], ct], [TTt[0]])
                cur = 0
                for l in range(1, lv):
                    last = (l == lv - 1)
                    b, bt = bank()
                    for ci in range(nch):
                        if l == 1:
                            Np, NpT, rd = MMs[0:csz, ci, 0:csz], MMs[0:csz, ci, C_:C_ + csz], [MMt[ci]]
                        else:
                            Np, NpT, rd = NN[(l - 1) % 2][0:csz, ci, 0:csz], NN[(l - 1) % 2][0:csz, ci, 64:64 + csz], [NNt[(l - 1) % 2]]
                        if not last:
                            MM(b[0:csz, ci * 128:ci * 128 + csz], [(NpT, Np)], rd, bt)
                        MM(b[0:csz, ci * 128 + 64:ci * 128 + 64 + csz], [(Np, NpT)], rd, bt)
                    for ci in range(nch):
                        if not last:
                            CP("act", NN[l % 2][0:csz, ci, 0:csz], b[0:csz, ci * 128:ci * 128 + csz], [bt], [NNt[l % 2]])
                        CP("act", NN[l % 2][0:csz, ci, 64:64 + csz], b[0:csz, ci * 128 + 64:ci * 128 + 64 + csz], [bt], [NNt[l % 2]])
                    b, bt = bank()
                    for ci in range(nch):
                        MM(b[0:csz, ci * 64:ci * 64 + csz], [(NN[l % 2][0:csz, ci, 64:64 + csz], TT_[cur][0:csz, ci, 0:csz])],
                           [NNt[l % 2], TTt[cur]], bt)
                    for ci in range(nch):
                        TT("dve", TT_[1 - cur][0:csz, ci, 0:csz], b[0:csz, ci * 64:ci * 64 + csz], TT_[cur][0:csz, ci, 0:csz], ALU.add,
                           [bt, TTt[cur]], [TTt[1 - cur]])
                    cur = 1 - cur
                Tfin, Tfint = TT_[cur], TTt[cur]
                for ci, (c0, cs) in enumerate(cl):
                    if mode == "p":
                        s32, s32t, sr, srt = S32[:, h, :], S32t[h], SR[:, h, :], SRt[h]
                    else:
                        bi = ug + ci
                        s32, s32t, sr, srt = SS32[:, bi, :], SSt[bi], SSR[:, bi, :], SSRt[bi]
                    A_, R__ = Rc("At", c0, cs), Rc("Rt", c0, cs)
                    Mak = MMs[0:cs, ci, 2 * C_:2 * C_ + cs]
                    Mbr = MMs[0:cs, ci, 3 * C_:3 * C_ + cs]
                    Mkr = MMs[0:cs, ci, 4 * C_:4 * C_ + cs]
                    BhT, KhT, VT = TM[0:cs, ci, 0:64], TM[0:cs, ci, 64:128], TM[0:cs, ci, 128:192]
                    b, bt = bank()
                    MM(b[0:cs, 0:64], [(A_, sr), (Mak, VT)], [Att, srt, MMt[ci], TMt[ci]], bt)
                    CP("act", WT[0:cs, :], b[0:cs, 0:64], [bt], [WTt])
                    MM(b[0:cs, 64:128], [(Tfin[0:cs, ci, 0:cs], WT[0:cs, :])], [Tfint, WTt], bt)
                    CP("act", UT[0:cs, :], b[0:cs, 64:128], [bt], [UTt])
                    MM(ob[0:64, c0:c0 + cs], [(sr, R__), (UT[0:cs, :], Mbr), (VT, Mkr)], [srt, Rtt, UTt, MMt[ci], TMt[ci]], obt)
                    MM(b[0:64, 128:192], [(BhT, UT[0:cs, :]), (KhT, VT)], [TMt[ci], UTt], bt)
                    STT(s32, s32, Tc("P", c0 + cs - 1, 1), b[0:64, 128:192], ALU.mult, ALU.add, [s32t, Pt_, bt], [s32t])
                    CP("act", sr, s32, [s32t], [srt])
            osb, osbt = R_("osb")
            censq, censqt = R_("censq")
            cen, cent = T_("cen")
            y, yt = T_("y")
            CP("act", osb, ob[0:64, :N], [obt, Att], [osbt])
            b, bt = bank()
            MM(b[0:64, :N], [(ones_r[0:64, 0:64], osb)], [osbt, ct], bt)
            STT(cen, b[0:64, :N], -1.0 / 64, osb.bitcast(F32), ALU.mult, ALU.add, [bt, osbt, Lext], [cent])
            ACT(censq, cen, AF.Square, [cent, Btt], [censqt])
            b, bt = bank()
            MM(b[0:64, :N], [(ones_r[0:64, 0:64], censq)], [censqt, ct], bt)
            ACT(t1, b[0:64, :N], AF.Ln, [bt, ct], [t1t], scale=1.0 / 64, bias=epsc[0:64, 1:2])
            ACT(t1, t1, AF.Exp, [t1t], [t1t], scale=-0.5)
            TT("dve", y, cen, t1, ALU.mult, [cent, t1t, Pext], [yt])
            TS("dve", y, y, v64[:, 80 + h:81 + h], v64[:, 96 + h:97 + h], ALU.mult, ALU.add, [yt, ct], [yt])
            TT("pool", y, y, bonus, ALU.add, [yt, bonust], [yt])
            TT("dve", OGs[:, h, :N], y, gsb, ALU.mult, [yt, gsbt], [OGt])
            if mode == "s":
                for g0 in range(0, NB, 8):
                    b, bt = bank()
                    nb_ = min(8, NB - g0)
                    for j in range(nb_):
                        TR(b[0:64, j * 64:(j + 1) * 64], SS32[:, g0 + j, :], ident[0:64, 0:64], [SSt[g0 + j], ct], bt)
                    CP("dve", SNAT[:, g0:g0 + nb_, :], b[0:64, 0:nb_ * 64].rearrange("p (a b) -> p a b", a=nb_), [bt, SNATt], [SNATt])
                OUT(o_wkvs[:, h, :, :].rearrange("b v k -> v b k"), SNAT[:, 0:NB, :], [SNATt])
        for g in range(4):
            sl_, slt_ = load_slab([(v_h16, W["rw_wo"].rearrange("(h p) m -> p h m", p=64)[:, :, g * 256:(g + 1) * 256])])
            sv = v_h16(sl_)
            for mm_ in range(2):
                m = g * 2 + mm_
                b, bt = bank()
                MM(b[:, :N], [(sv[:, h, mm_ * 128:(mm_ + 1) * 128], OGs[:, h, :N]) for h in range(NH)], [OGt, slt_], bt)
                TT("dve", X[:, m, :N], X[:, m, :N], b[:, :N], ALU.add, [Xt[m], bt], [Xt[m]])

    def ffn_layer(N, li):
        xb, xbt = XM[3], XMt[3]
        norm_to_bf16(N, 8 if li == 0 else 32, xb, xbt)
        up = W["ffn_up%d" % li]
        dn = W["ffn_down%d" % li]

        def loads(jb):
            su = load_slab([(v_k8, kcv(up)[:, :, jb * 512:(jb + 1) * 512])])
            sd = load_slab([(v_k4, dn[jb * 512:(jb + 1) * 512, :].rearrange("(kc p) m -> p kc m", p=128))])
            return su, sd
        nxt = loads(0)
        for jb in range(8):
            (su_, sut), (sd_, sdt) = nxt
            su, sd = v_k8(su_), v_k4(sd_)
            for oc in range(4):
                gi = (jb % 2) * 4 + oc
                hh, hht = gbf(gi)[:, :N], Gt[gi]
                hr_, hrt = gbf(8 + oc % 2)[:, :N], Gt[8 + oc % 2]
                b, bt = bank()
                MM(b[:, :N], [(su[:, kc, oc * 128:(oc + 1) * 128], xb[:, kc, :N]) for kc in range(8)], [xbt, sut], bt)
                ACT(hr_, b[:, :N], AF.Relu, [bt], [hrt])
                TT("pool", hh, hr_, hr_, ALU.mult, [hrt], [hht])
            if jb + 1 < 8:
                nxt = loads(jb + 1)
            for m in range(8):
                b, bt = bank()
                MM(b[:, :N], [(sd[:, kc, m * 128:(m + 1) * 128], gbf((jb % 2) * 4 + kc)[:, :N]) for kc in range(4)],
                   [Gt[(jb % 2) * 4 + kc] for kc in range(4)] + [sdt], bt)
                TT("dve", X[:, m, :N], X[:, m, :N], b[:, :N], ALU.add, [Xt[m], bt], [Xt[m]])

    def kv_path(N, tok0, rope_tok, lat_out, kr_out, blk0, LATT_, KRT_, LATTOK_, kvt_):
        xb, xbt = XM[3], XMt[3]
        norm_to_bf16(N, 16, xb, xbt)
        sl_, slt_ = load_slab([(lambda s: s[:, 0:8 * 320].rearrange("p (a b) -> p a b", a=8)[:, :, 0:KVR], kcv(W["w_dkv"])),
                               (lambda s: s[:, 0:8 * 320].rearrange("p (a b) -> p a b", a=8)[:, :, KVR:KVR + ROPE], kcv(W["w_kr"]))])
        wdkv = sl_[:, 0:8 * 320].rearrange("p (a b) -> p a b", a=8)
        kvA, kvAt = g32(0), Gt[0]
        kvB, kvBt = g32(1), Gt[1]
        latb, latbt = gbf(2), Gt[2]
        nblk = (N + 127) // 128
        for tb in range(nblk):
            c0 = tb * 128
            nt = min(128, N - c0)
            b, bt = bank()
            MM(b[0:nt, 0:KVR + ROPE], [(xb[:, kc, c0:c0 + nt], wdkv[:, kc, :]) for kc in range(8)], [xbt, slt_], bt)
            MEMSET("pool", kvcol[0:nt, 0:1], 0.0, [kvcolt])
            ACT(kvA[0:nt, :], b[0:nt, 0:KVR], AF.Square, [bt], [kvAt, kvcolt], accum=kvcol[0:nt, 0:1])
            ACT(kvcol[0:nt, 1:2], kvcol[0:nt, 0:1], AF.Ln, [kvcolt, ct], [kvcolt], scale=1.0 / KVR, bias=epsc[0:nt, 0:1])
            ACT(kvcol[0:nt, 1:2], kvcol[0:nt, 1:2], AF.Exp, [kvcolt], [kvcolt], scale=-0.5)
            STT(kvA[0:nt, :], b[0:nt, 0:KVR], kvcol[0:nt, 1:2], lnbc[0:nt, :], ALU.mult, ALU.mult, [bt, kvcolt, ct, kvAt], [kvAt])
            p.dma("sp", ropet[0:nt, :], rope_tok[tok0 + c0:tok0 + c0 + nt, :], writes=[ropett])
            x1 = b[0:nt, KVR:KVR + 32]
            x2 = b[0:nt, KVR + 32:KVR + 64]
            o1, o2 = kvB[0:nt, 0:32], kvB[0:nt, 32:64]
            t_a, t_b = kvB[0:nt, 64:96], kvB[0:nt, 96:128]
            TT("dve", t_a, x1, ropet[0:nt, 0:32], ALU.mult, [bt, ropett, kvBt], [kvBt])
            TT("dve", t_b, x2, ropet[0:nt, 32:64], ALU.mult, [bt, ropett, kvBt], [kvBt])
            TT("dve", o1, t_a, t_b, ALU.subtract, [kvBt], [kvBt])
            TT("dve", t_a, x1, ropet[0:nt, 32:64], ALU.mult, [bt, ropett, kvBt], [kvBt])
            TT("dve", t_b, x2, ropet[0:nt, 0:32], ALU.mult, [bt, ropett, kvBt], [kvBt])
            TT("dve", o2, t_a, t_b, ALU.add, [kvBt], [kvBt])
            OUT(lat_out[tok0 + c0:tok0 + c0 + nt, :], kvA[0:nt, :], [kvAt])
            OUT(kr_out[tok0 + c0:tok0 + c0 + nt, :], kvB[0:nt, 0:ROPE], [kvBt])
            kb = blk0 + tb
            CP("act", LATTOK_[0:nt, kb, :], kvA[0:nt, :], [kvAt], [kvt_])
            CP("pool", latb[0:nt, 0:ROPE], kvB[0:nt, 0:ROPE], [kvBt], [latbt])
            hb_, hbt_ = bbank()
            for rc in range(2):
                TR(hb_[:, rc * 128:rc * 128 + nt], LATTOK_[0:nt, kb, rc * 128:(rc + 1) * 128], identb[0:nt, 0:nt], [kvt_, ct], hbt_)
            TR(hb_[0:64, 256:256 + nt], latb[0:nt, 0:ROPE], identb[0:nt, 0:nt], [latbt, ct], hbt_)
            for rc in range(2):
                CP("dve" if rc else "act", LATT_[:, rc, kb * 128:kb * 128 + nt], hb_[:, rc * 128:rc * 128 + nt], [hbt_], [kvt_])
            CP("dve", KRT_[:, kb * 128:kb * 128 + nt], hb_[0:64, 256:256 + nt], [hbt_], [kvt_])

    GI_CQ = (0, 1, 2)
    GI_CQN = (3, 4, 5)
    GI_QN, GI_PB, GI_PT, GI_ACC, GI_OLB = 6, 7, 8, 9, 10
    GI_OH = tuple(range(11, 19))

    def mla_queries(N, tok0, rope_fm, head_cb):
        xb, xbt = XM[3], XMt[3]
        norm_to_bf16(N, 24, xb, xbt)
        p.dma("sp", ropef[:, :N], rope_fm[0:64, tok0:tok0 + N], writes=[ropeft])
        p.dma("sp", rope_s2[:, :N], rope_fm[64:128, tok0:tok0 + N], writes=[ropeft])
        sl_, slt_ = load_slab([(lambda s: s[:, 0:8 * QR].rearrange("p (a b) -> p a b", a=8), kcv(W["w_dq"]))])
        wdq = sl_[:, 0:8 * QR].rearrange("p (a b) -> p a b", a=8)
        for m in range(3):
            b, bt = bank()
            MM(b[:, :N], [(wdq[:, kc, m * 128:(m + 1) * 128], xb[:, kc, :N]) for kc in range(8)], [xbt, slt_], bt)
            CP("act", g32(GI_CQ[m], 128, N), b[:, :N], [bt], [Gt[GI_CQ[m]]])
        rms_rstd(N, [(g32(GI_CQ[m], 128, N), Gt[GI_CQ[m]]) for m in range(3)], QR)
        for m in range(3):
            STT(gbf(GI_CQN[m])[:, :N], g32(GI_CQ[m], 128, N), v128[:, 96 + m:97 + m], rstd[:, :N], ALU.mult, ALU.mult,
                [Gt[GI_CQ[m]], rstdt, ct], [Gt[GI_CQN[m]]])
        cqn_t = [Gt[GI_CQN[m]] for m in range(3)]
        QN, QNt = gbf(GI_QN), Gt[GI_QN]
        for hg in range(2):
            sq_, sqt_ = load_slab([(lambda s: s[:, 0:3 * 768].rearrange("p (a b) -> p a b", a=3),
                                    kcv(W["w_uq"])[:, :, hg * 768:(hg + 1) * 768])])
            wuq = sq_[:, 0:3 * 768].rearrange("p (a b) -> p a b", a=3)
            for hh_ in range(4):
                h = hg * 4 + hh_
                b, bt = bank()
                MM(b[:, :N], [(wuq[:, kc, hh_ * 192:hh_ * 192 + 128], gbf(GI_CQN[kc])[:, :N]) for kc in range(3)], cqn_t + [sqt_], bt)
                CP("act", QN[:, :N], b[:, :N], [bt], [QNt])
                b2, bt2 = bank()
                MM(b2[0:64, :N], [(wuq[:, kc, hh_ * 192 + 128:hh_ * 192 + 192], gbf(GI_CQN[kc])[:, :N]) for kc in range(3)], cqn_t + [sqt_], bt2)
                CP("act", QPr[:, :N], b2[0:64, :N], [bt2], [QPt])
                TT("dve", QPf[:, :N], b2[0:64, :N], ropef[:, :N], ALU.mult, [bt2, ropeft], [QPt])
                b3, bt3 = bank()
                MM(b3[0:64, :N], [(rotm[:, :], QPr[:, :N])], [QPt, ct], bt3)
                TT("dve", tmpA[0:64, :N], b3[0:64, :N], rope_s2[:, :N], ALU.mult, [bt3, ropeft], [tmpAt])
                head_cb(h, QN, QNt)

    def load_wuv():
        sl_, slt_ = load_slab([(lambda s: s[:, 0:2048].rearrange("p (a b) -> p a b", a=2), kcv(W["w_uv"]))])
        return sl_[:, 0:2048].rearrange("p (a b) -> p a b", a=2), slt_

    def prompt_attention(N, tok0):
        wuv, wuvt = load_wuv()
        Pb, Pbt = gbf(GI_PB), Gt[GI_PB]
        PT = gbf(GI_PT).rearrange("p (a b) -> p a b", a=4)
        PTt = Gt[GI_PT]
        acc, acct = g32(GI_ACC), Gt[GI_ACC]
        olb, olbt = gbf(GI_OLB), Gt[GI_OLB]

        def per_head(h, QN, QNt):
            STT(QPEh[:, :N], QPf[:, :N], 1.0, tmpA[0:64, :N], ALU.mult, ALU.add, [QPt, tmpAt], [QLt])
            TS("pool", QPEh[:, :N], QPEh[:, :N], ATTN_SCALE, None, ALU.mult, None, [QLt], [QLt])
            for rc in range(2):
                b, bt = bank()
                MM(b[:, :N], [(wukT[:, h, rc * 128:(rc + 1) * 128], QN[:, :N])], [QNt, wres_t], bt)
                ACT(QLh[:, rc, :N], b[:, :N], AF.Copy, [bt], [QLt], scale=ATTN_SCALE)
            nqb = (N + 127) // 128
            for qb in range(nqb):
                q0 = qb * 128
                nq = min(128, N - q0)
                kend = tok0 + q0 + nq
                M_, L_, BM, MN, NM, CR, RS, RL = [sm_[0:nq, i:i + 1] for i in range(8)]
                MEMSET("pool", sm_[0:nq, 0:1], NEG, [smt])
                MEMSET("pool", sm_[0:nq, 1:2], 0.0, [smt])
                MEMSET("pool", acc[0:nq, :], 0.0, [acct])
                nseg = (kend + 511) // 512
                for s in range(nseg):
                    k0 = s * 512
                    kl = min(512, kend - k0)
                    b, bt = bank()
                    MM(b[0:nq, 0:kl], [(QLh[:, 0, q0:q0 + nq], LATT[:, 0, k0:k0 + kl]), (QLh[:, 1, q0:q0 + nq], LATT[:, 1, k0:k0 + kl]),
                                       (QPEh[:, q0:q0 + nq], KRT[:, k0:k0 + kl])], [QLt, KVt], bt)
                    dpos = tok0 + q0 - k0
                    if 0 <= dpos < 512:
                        TT("dve", b[0:nq, dpos:dpos + nq], b[0:nq, dpos:dpos + nq], cmask[0:nq, 0:nq], ALU.add, [bt, ct], [bt])
                    p.op("dve", lambda e, b=b, kl=kl, BM=BM, nq=nq: e.reduce_max(BM, b[0:nq, 0:kl], AX.X), reads=[bt, smt], writes=[smt])
                    TT("dve", MN, M_, BM, ALU.max, [smt], [smt])
                    TS("dve", NM, MN, -1.0, None, ALU.mult, None, [smt], [smt])
                    MEMSET("pool", RS, 0.0, [smt])
                    ACT(CR, M_, AF.Exp, [smt], [smt], bias=NM)
                    ACT(Pb[0:nq, 0:kl], b[0:nq, 0:kl], AF.Exp, [bt, smt], [Pbt, smt], bias=NM, accum=RS)
                    STT(L_, L_, CR, RS, ALU.mult, ALU.add, [smt], [smt])
                    CP("pool", M_, MN, [smt], [smt])
                    nkb = (kl + 127) // 128
                    hb_, hbt_ = bbank()
                    for j in range(nkb):
                        kn = min(128, kl - j * 128)
                        TR(hb_[0:kn, j * 128:j * 128 + nq], Pb[0:nq, j * 128:j * 128 + kn], identb[0:nq, 0:nq], [Pbt, ct], hbt_)
                    for j in range(nkb):
                        kn = min(128, kl - j * 128)
                        CP("act" if j % 2 else "dve", PT[0:kn, j, 0:nq], hb_[0:kn, j * 128:j * 128 + nq], [hbt_], [PTt])
                    b2, bt2 = bank()
                    prs = []
                    for j in range(nkb):
                        kn = min(128, kl - j * 128)
                        prs.append((PT[0:kn, j, 0:nq], LATTOK[0:kn, k0 // 128 + j, :]))
                    MM(b2[0:nq, 0:KVR], prs, [PTt, KVt], bt2)
                    STT(acc[0:nq, :], acc[0:nq, :], CR, b2[0:nq, 0:KVR], ALU.mult, ALU.add, [acct, smt, bt2], [acct])
                RECIP(RL, L_, [smt], [smt])
                TS("dve", olb[0:nq, 0:KVR], acc[0:nq, :], RL, None, ALU.mult, None, [acct, smt], [olbt])
                hb_, hbt_ = bbank()
                for rc in range(2):
                    TR(hb_[:, rc * 128:rc * 128 + nq], olb[0:nq, rc * 128:(rc + 1) * 128], identb[0:nq, 0:nq], [olbt, ct], hbt_)
                for rc in range(2):
                    CP("act" if rc else "dve", OLT[:, rc, q0:q0 + nq], hb_[:, rc * 128:rc * 128 + nq], [hbt_], [OLTt])
            b, bt = bank()
            MM(b[:, :N], [(wuv[:, rc, h * 128:(h + 1) * 128], OLT[:, rc, :N]) for rc in range(2)], [OLTt, wuvt], bt)
            CP("act", gbf(GI_OH[h])[:, :N], b[:, :N], [bt], [Gt[GI_OH[h]]])
        return per_head

    def mla_out(N):
        for g in range(2):
            sl_, slt_ = load_slab([(v_k8, kcv(W["w_o_mla"])[:, :, g * 512:(g + 1) * 512])])
            sv = v_k8(sl_)
            for mm_ in range(4):
                m = g * 4 + mm_
                b, bt = bank()
                MM(b[:, :N], [(sv[:, h, mm_ * 128:(mm_ + 1) * 128], gbf(GI_OH[h])[:, :N]) for h in range(MH)],
                   [Gt[GI_OH[h]] for h in range(MH)] + [slt_], bt)
                TT("dve", X[:, m, :N], X[:, m, :N], b[:, :N], ALU.add, [Xt[m], bt], [Xt[m]])

    def final_out(N, dst_rows):
        rms_rstd(N, [(X[:, kc, :N], Xt[kc]) for kc in range(8)], D)
        for kc in range(8):
            STT(g32(kc, 128, N), X[:, kc, :N], v128[:, 40 + kc:41 + kc], rstd[:, :N], ALU.mult, ALU.mult, [Xt[kc], rstdt, ct], [Gt[kc]])
        nblk = (N + 127) // 128
        for tb in range(nblk):
            c0 = tb * 128
            nt = min(128, N - c0)
            for g in range(2):
                b, bt = bank()
                for j in range(4):
                    kc = g * 4 + j
                    TR(b[0:nt, j * 128:(j + 1) * 128], Gp[kc][:, c0:c0 + nt], ident[:, :], [Gt[kc], ct], bt)
                CP("act" if g else "dve", stg[0:nt, g * 512:(g + 1) * 512], b[0:nt, :], [bt, stgt], [stgt])
            for (dst, r0, n) in dst_rows[tb]:
                OUT(dst, stg[r0:r0 + n, :], [stgt])

    tiles = []
    t0 = 0
    while t0 < T:
        n = min(NT, T - t0)
        tiles.append((t0, n))
        t0 += n
    MEMSET("pool", carry[:, :], 0.0, [carryt])
    for h in range(NH):
        MEMSET("pool", S32[:, h, :], 0.0, [S32t[h]])
        CP("dve", SR[:, h, :], S32[:, h, :], [S32t[h]], [SRt[h]])
    PH = os.environ.get("MK_PH", "rwfkmgo")
    MAXT = int(os.environ.get("MK_MAXT", "999"))
    SPH = os.environ.get("MK_SPH", "rfkmago")
    if cfg.get("DO_PROMPT", True) and "P" not in os.environ.get("MK_SKIP", ""):
        for ti, (tok0, N) in enumerate(tiles):
            if ti >= MAXT:
                break
            nblk = (N + 127) // 128
            for tb in range(nblk):
                g0 = tok0 + tb * 128
                nt = min(128, N - tb * 128)
                rows = []
                if g0 < N_META:
                    rows.append((meta[g0:N_META, :], 0, N_META - g0))
                    rows.append((xp[0:nt - (N_META - g0), :], N_META - g0, nt - (N_META - g0)))
                else:
                    rows.append((xp[g0 - N_META:g0 - N_META + nt, :], 0, nt))
                load_x_block(rows, nt, tb * 128)
            Cc = 64 if N >= 64 else N
            chunks = [(c0, Cc) for c0 in range(0, N, Cc)]
            if "r" in PH:
                rwkv_layer(N, Cc, chunks, "p")
            if (ti == len(tiles) - 1 or ti == MAXT - 1) and "w" in PH:
                b, bt = bank()
                TR(b[0:8, 0:128], carry[:, :], ident[:, :], [carryt, ct], bt)
                CP("dve", stg[0:8, 0:128], b[0:8, 0:128], [bt, stgt], [stgt])
                OUT(o_shiftp, stg[0:8, 0:128], [stgt])
                for g0 in range(0, NH, 8):
                    b, bt = bank()
                    for j in range(8):
                        TR(b[0:64, j * 64:(j + 1) * 64], S32[:, g0 + j, :], ident[0:64, 0:64], [S32t[g0 + j], ct], bt)
                    CP("dve", SNAT[:, 0:8, :], b[0:64, 0:512].rearrange("p (a b) -> p a b", a=8), [bt, SNATt], [SNATt])
                    OUT(o_wkvp[g0:g0 + 8, :, :].rearrange("h v k -> v h k"), SNAT[:, 0:8, :], [SNATt])
            if "f" in PH:
                ffn_layer(N, 0)
            if "k" in PH:
                kv_path(N, tok0, C["rope_tok_p"], o_latp, o_krp, tok0 // 128, LATT, KRT, LATTOK, KVt)
            if "m" in PH:
                mla_queries(N, tok0, C["rope_fm_p"], prompt_attention(N, tok0))
                mla_out(N)
            if "g" in PH:
                ffn_layer(N, 1)
            dst_rows = []
            for tb in range(nblk):
                g0 = tok0 + tb * 128
                nt = min(128, N - tb * 128)
                if g0 < N_META:
                    dst_rows.append([(o_yp[0:nt - (N_META - g0), :], N_META - g0, nt - (N_META - g0))])
                else:
                    dst_rows.append([(o_yp[g0 - N_META:g0 - N_META + nt, :], 0, nt)])
            if "o" in PH:
                final_out(N, dst_rows)

    p.barrier()

    def sample_q_cb(N):
        def per_head(h, QN, QNt):
            STT(QPEs[:, h, :N], QPf[:, :N], 1.0, tmpA[0:64, :N], ALU.mult, ALU.add, [QPt, tmpAt], [QLst])
            TS("pool", QPEs[:, h, :N], QPEs[:, h, :N], ATTN_SCALE, None, ALU.mult, None, [QLst], [QLst])
            for rc in range(2):
                b, bt = bank()
                MM(b[:, :N], [(wukT[:, h, rc * 128:(rc + 1) * 128], QN[:, :N])], [QNt, wres_t], bt)
                ACT(QLs[:, h, rc, :N], b[:, :N], AF.Copy, [bt], [QLst], scale=ATTN_SCALE)
        return per_head

    def sample_attend_all(N):
        nq = 32
        M_, L_, BM, MN, NM, CR, RS, RL = [sm_[0:nq, i:i + 1] for i in range(8)]
        Pb, Pbt = gbf(GI_PB), Gt[GI_PB]
        PT = gbf(GI_PT).rearrange("p (a b) -> p a b", a=4)
        PTt = Gt[GI_PT]
        acc, acct = g32(GI_ACC), Gt[GI_ACC]
        olb, olbt = gbf(GI_OLB), Gt[GI_OLB]

        def softmax_block(b, bt, kl, vblocks):
            p.op("dve", lambda e: e.reduce_max(BM, b[0:nq, 0:kl], AX.X), reads=[bt, smt], writes=[smt])
            TT("dve", MN, M_, BM, ALU.max, [smt], [smt])
            TS("dve", NM, MN, -1.0, None, ALU.mult, None, [smt], [smt])
            MEMSET("pool", RS, 0.0, [smt])
            ACT(CR, M_, AF.Exp, [smt], [smt], bias=NM)
            ACT(Pb[0:nq, 0:kl], b[0:nq, 0:kl], AF.Exp, [bt, smt], [Pbt, smt], bias=NM, accum=RS)
            STT(L_, L_, CR, RS, ALU.mult, ALU.add, [smt], [smt])
            CP("pool", M_, MN, [smt], [smt])
            nkb = len(vblocks)
            hb_, hbt_ = bbank()
            o = 0
            for j, (kn, vap, rd) in enumerate(vblocks):
                TR(hb_[0:kn, j * 32:j * 32 + nq], Pb[0:nq, o:o + kn], identb[0:nq, 0:nq], [Pbt, ct], hbt_)
                o += kn
            kn0 = vblocks[0][0]
            CP("dve", PT[0:kn0, 0:nkb, 0:nq], hb_[0:kn0, 0:nkb * 32].rearrange("p (a b) -> p a b", a=nkb), [hbt_], [PTt])
            b2, bt2 = bank()
            rds = [PTt]
            for (_, _, rd) in vblocks:
                rds += rd
            MM(b2[0:nq, 0:KVR], [(PT[0:kn, j, 0:nq], vap) for j, (kn, vap, rd) in enumerate(vblocks)], rds, bt2)
            STT(acc[0:nq, :], acc[0:nq, :], CR, b2[0:nq, 0:KVR], ALU.mult, ALU.add, [acct, smt, bt2], [acct])

        gi = 0
        for bi in range(NB):
            p.dma("sp", idxr[:, :], ptrep[bi], writes=[idxt])
            CP("dve", idxf[:, :], idxr[:, :], [idxt], [idxt])
            TS("dve", idxf[:, :], idxf[:, :], 16.0, cmod[:, 0:1], ALU.mult, ALU.add, [idxt, ct], [idxt])
            CP("dve", idxi[:, :], idxf[:, :], [idxt], [idxt])
            for rc in range(2):
                CP("dve", QB[:, rc, :].rearrange("p (h t) -> p h t", t=DEC_SEQ), QLs[:, :, rc, bi * DEC_SEQ:(bi + 1) * DEC_SEQ], [QLst], [QBt])
            CP("dve", QPB[:, :].rearrange("p (h t) -> p h t", t=DEC_SEQ), QPEs[:, :, bi * DEC_SEQ:(bi + 1) * DEC_SEQ], [QLst], [QBt])
            MEMSET("pool", sm_[0:nq, 0:1], NEG, [smt])
            MEMSET("pool", sm_[0:nq, 1:2], 0.0, [smt])
            MEMSET("pool", acc[0:nq, :], 0.0, [acct])
            for g in range(NGRP):
                si = gi % 2
                gi += 1
                p.dma("pool", None, None, reads=[idxt], writes=[stt_[si]],
                      fn=lambda e, si=si, g=g: e.indirect_dma_start(
                          out=stL[si].rearrange("p a b -> p (a b)"), out_offset=None, in_=c_lat,
                          in_offset=bass.IndirectOffsetOnAxis(ap=idxi[:, g:g + 1], axis=0)))
                p.dma("pool", None, None, reads=[idxt], writes=[stt_[si]],
                      fn=lambda e, si=si, g=g: e.indirect_dma_start(
                          out=stK[si].rearrange("p a b -> p (a b)"), out_offset=None, in_=c_kr,
                          in_offset=bass.IndirectOffsetOnAxis(ap=idxi[:, g:g + 1], axis=0)))
                CP("pool", stLb[:, 0:4, :], stL[si][:, 0:4, :], [stt_[si]], [stbt])
                CP("act", stLb[:, 4:8, :], stL[si][:, 4:8, :], [stt_[si]], [stbt])
                CP("dve", stKb[:, :, :], stK[si][:, :, :], [stt_[si]], [stbt])
                for rc in range(2):
                    hb_, hbt_ = bbank()
                    for j in range(8):
                        TR(hb_[:, j * 128:(j + 1) * 128], stLb[:, j, rc * 128:(rc + 1) * 128], identb[:, :], [stbt, ct], hbt_)
                    CP("act" if rc else "dve", LTs[:, rc, :], hb_[:, :], [hbt_], [LTst])
                hb_, hbt_ = bbank()
                for j in range(8):
                    TR(hb_[0:64, j * 128:(j + 1) * 128], stKb[:, j, :], identb[:, :], [stbt, ct], hbt_)
                CP("dve", KTs[:, :], hb_[0:64, :], [hbt_], [LTst])
                for s in range(2):
                    b, bt = bank()
                    k0 = s * 512
                    MM(b[0:nq, 0:512], [(QB[:, 0, :], LTs[:, 0, k0:k0 + 512]), (QB[:, 1, :], LTs[:, 1, k0:k0 + 512]),
                                        (QPB[:, :], KTs[:, k0:k0 + 512])], [QBt, LTst], bt)
                    softmax_block(b, bt, 512, [(128, stLb[:, s * 4 + j, :], [stbt]) for j in range(4)])
            b, bt = bank()
            MM(b[0:nq, 0:NS], [(QB[:, 0, :], LATTs[:, 0, 0:NS]), (QB[:, 1, :], LATTs[:, 1, 0:NS]), (QPB[:, :], KRTs[:, 0:NS])],
               [QBt, KVst], bt)
            TT("dve", b[0:nq, 0:NS], b[0:nq, 0:NS], smask_s[:, bi, :], ALU.add, [bt, ct], [bt])
            softmax_block(b, bt, NS, [(NS, LATTOKs[0:NS, 0, :], [KVst])])
            RECIP(RL, L_, [smt], [smt])
            TS("dve", olb[0:nq, 0:KVR], acc[0:nq, :], RL, None, ALU.mult, None, [acct, smt], [olbt])
            hb_, hbt_ = bbank()
            for rc in range(2):
                TR(hb_[:, rc * 32:rc * 32 + nq], olb[0:nq, rc * 128:(rc + 1) * 128], identb[0:nq, 0:nq], [olbt, ct], hbt_)
            for rc in range(2):
                CP("dve", OLTs[:, :, rc, bi * DEC_SEQ:(bi + 1) * DEC_SEQ], hb_[:, rc * 32:rc * 32 + nq].rearrange("p (h t) -> p h t", t=DEC_SEQ),
                   [hbt_], [OLTst])
        wuv, wuvt = load_wuv()
        for h in range(MH):
            b, bt = bank()
            MM(b[:, :N], [(wuv[:, rc, h * 128:(h + 1) * 128], OLTs[:, h, rc, :N]) for rc in range(2)], [OLTst, wuvt], bt)
            CP("act", gbf(GI_OH[h])[:, :N], b[:, :N], [bt], [Gt[GI_OH[h]]])

    if cfg.get("DO_SAMPLE", True) and "S" not in os.environ.get("MK_SKIP", ""):
        N = NS
        load_x_block([(xs[:, :], 0, NS)], NS, 0)
        p.dma("sp", stg[0:NB, :], sshift, writes=[stgt])
        b, bt = bank()
        for kc in range(8):
            TR(b[:, kc * NB:(kc + 1) * NB], stg[0:NB, kc * 128:(kc + 1) * 128], ident[0:NB, 0:NB], [stgt, ct], bt)
        CP("dve", shT[:, :, :], b[:, 0:8 * NB].rearrange("p (a b) -> p a b", a=8), [bt], [shTt])
        chunks = [(bi * DEC_SEQ, DEC_SEQ) for bi in range(NB)]
        if "r" in SPH:
            rwkv_layer(N, DEC_SEQ, chunks, "s")
        for g in range(2):
            b, bt = bank()
            for j in range(4):
                kc = g * 4 + j
                TR(b[0:NB, j * 128:(j + 1) * 128], shT[:, kc, :], ident[:, :], [shTt, ct], bt)
            CP("dve", stg[0:NB, g * 512:(g + 1) * 512], b[0:NB, :], [bt, stgt], [stgt])
        OUT(o_shifts, stg[0:NB, :], [stgt])
        if "f" in SPH:
            ffn_layer(N, 0)
        if "k" in SPH:
            kv_path(N, 0, C["rope_tok_s"], o_lats, o_krs, 0, LATTs, KRTs, LATTOKs, KVst)
        if "m" in SPH:
            mla_queries(N, 0, C["rope_fm_s"], sample_q_cb(N))
        if "a" in SPH:
            sample_attend_all(N)
            mla_out(N)
        if "g" in SPH:
            ffn_layer(N, 1)
        final_out(N, [[(o_ys[:, :], 0, NS)]])

    p.finish([out_trk])
    return nc


def _run(inputs, cfg, n_cores=8):
    f = lambda a: np.ascontiguousarray(np.asarray(a))
    NB = cfg["NB"]
    NPG = cfg["NPG"]
    nseq = inputs["x_prompt"].shape[0]
    consts = make_consts(cfg)
    shared = {}
    for nm in ("rw_wr", "rw_wk", "rw_wv", "rw_wo", "rw_w1", "rw_w2", "rw_a1", "rw_a2", "rw_g1", "rw_g2"):
        shared[nm] = f(inputs[nm][0])
    for li in range(2):
        shared["ffn_up%d" % li] = f(inputs["ffn_up"][li])
        shared["ffn_down%d" % li] = f(inputs["ffn_down"][li])
    shared["w_dkv"] = f(inputs["w_dkv"])
    shared["w_kr"] = f(inputs["w_kr"])
    shared["w_uk"] = f(np.asarray(inputs["w_uk"]).reshape(KVR, MH * 128))
    shared["w_uv"] = f(np.asarray(inputs["w_uv"]).reshape(KVR, MH * 128))
    shared["w_dq"] = f(inputs["w_dq"][0])
    shared["w_uq"] = f(np.asarray(inputs["w_uq"][0]).reshape(QR, MH * 192))
    shared["w_o_mla"] = f(inputs["w_o_mla"][0])
    v128 = np.concatenate([
        np.asarray(inputs["norm_mix"][0]).reshape(8, 128), np.asarray(inputs["norm_ffn"][0]).reshape(8, 128),
        np.asarray(inputs["kv_norm"]).reshape(8, 128), np.asarray(inputs["norm_mix"][1]).reshape(8, 128),
        np.asarray(inputs["norm_ffn"][1]).reshape(8, 128), np.asarray(inputs["norm_final"]).reshape(8, 128),
        np.asarray(inputs["rw_mu"][0]).reshape(48, 128), np.asarray(inputs["q_norm"][0]).reshape(3, 128)], axis=0)
    shared["vec128"] = f(v128.astype(np.float32))
    v64 = np.concatenate([np.asarray(inputs[k][0]).reshape(16, 64) for k in
                          ("rw_w0", "rw_a0", "rw_kk", "rw_ka", "rw_rk", "rw_lnx_g", "rw_lnx_b")], axis=0)
    shared["vec64"] = f(v64.astype(np.float32))
    shared["latnorm"] = f(np.asarray(inputs["lat_norm"]).reshape(1, KVR))
    shared["meta"] = f(inputs["meta_tokens"])
    nphys = inputs["cache_latent"].shape[0]
    shared["c_lat"] = f(np.asarray(inputs["cache_latent"]).reshape(nphys * 16, 8 * KVR))
    shared["c_kr"] = f(np.asarray(inputs["cache_krope"]).reshape(nphys * 16, 8 * ROPE))
    for k, v in consts.items():
        shared["c_" + k] = f(v)
    pt = np.asarray(inputs["page_table"]).astype(np.int32)
    ngrp = NPG // 8
    in_maps = []
    for c in range(n_cores):
        m = dict(shared)
        m["xp"] = f(inputs["x_prompt"][c % nseq])
        bs = slice(c * NB, (c + 1) * NB)
        m["xs"] = f(np.asarray(inputs["x_sample"][bs]).reshape(NB * DEC_SEQ, D))
        m["swkv"] = f(inputs["state_wkv"][0][bs])
        m["sshift"] = f(inputs["state_shift"][0][bs])
        ptc = pt[bs].reshape(NB, ngrp, 8)
        rep = np.repeat(ptc.transpose(0, 2, 1)[:, :, None, :], 16, axis=2)
        m["ptrep"] = f(rep.reshape(NB, 128, ngrp).astype(np.int32))
        in_maps.append(m)
    nc = build(cfg)
    res = run_bass_kernel_spmd(nc, in_maps, core_ids=list(range(n_cores)))
    return res.results


def _assemble(results, cfg, nseq, n_cores=8):
    NB = cfg["NB"]
    r = results
    y_prompt = np.stack([r[b]["o_yp"] for b in range(nseq)], axis=0)
    y_sample = np.concatenate([r[c]["o_ys"].reshape(NB, DEC_SEQ, D) for c in range(n_cores)], axis=0)
    wkv_p = np.stack([r[b]["o_wkvp"] for b in range(nseq)], axis=0)[None]
    shift_p = np.stack([r[b]["o_shiftp"].reshape(D) for b in range(nseq)], axis=0)[None]
    lat_p = np.stack([r[b]["o_latp"] for b in range(nseq)], axis=0)
    kr_p = np.stack([r[b]["o_krp"] for b in range(nseq)], axis=0)
    wkv_s = np.concatenate([r[c]["o_wkvs"] for c in range(n_cores)], axis=0)[None]
    shift_s = np.concatenate([r[c]["o_shifts"] for c in range(n_cores)], axis=0)[None]
    lat_s = np.concatenate([r[c]["o_lats"].reshape(NB, DEC_SEQ, KVR) for c in range(n_cores)], axis=0)
    kr_s = np.concatenate([r[c]["o_krs"].reshape(NB, DEC_SEQ, ROPE) for c in range(n_cores)], axis=0)
    outs = (y_prompt, y_sample, wkv_p, shift_p, lat_p, kr_p, wkv_s, shift_s, lat_s, kr_s)
    return tuple(np.ascontiguousarray(o.astype(np.float32)) for o in outs)


def kernel(**inputs):
    seq = inputs["x_prompt"].shape[1]
    nseq = inputs["x_prompt"].shape[0]
    db = inputs["x_sample"].shape[0]
    npg = inputs["page_table"].shape[1]
    cfg = dict(SEQ=seq, T=seq + N_META, NB=db // 8, NPG=npg, NPHYS=inputs["cache_latent"].shape[0], PAST=npg * 128)
    results = _run(inputs, cfg)
    return _assemble(results, cfg, nseq)
```

```python
import math
import os
import numpy as np
from contextlib import ExitStack
import concourse.bass as bass
import concourse.mybir as mybir
from concourse.bass_utils import run_bass_kernel_spmd

F32 = mybir.dt.float32
F32R = mybir.dt.float32r
BF16 = mybir.dt.bfloat16
I32 = mybir.dt.int32
AF = mybir.ActivationFunctionType
ALU = mybir.AluOpType
AX = mybir.AxisListType

D = 1024
NH = 16
HD = 64
MH = 8
KVR = 256
QR = 384
ROPE = 64
DFF = 4096
N_META = 16
GN_EPS = 64e-5
NORM_EPS = 1e-6
ATTN_SCALE = (128 + 64) ** -0.5
DEC_SEQ = 4
NEG = -30000.0


class Trk:
    __slots__ = ("lastw", "readers", "excl")

    def __init__(self, excl=False):
        self.lastw = None
        self.readers = []
        self.excl = excl


class Prog:
    ENGS = ("pe", "act", "dve", "pool", "sp")
    NDMA = 8

    def __init__(self, nc):
        self.nc = nc
        self.es = ExitStack()
        self.q = {e: [] for e in self.ENGS}
        self.cnt = {}
        self.sems = {}
        self.waited = {e: {} for e in self.ENGS}
        for e in self.ENGS:
            self.sems[e] = self.es.enter_context(nc.semaphore("s_" + e))
            self.cnt[e] = 0
        self.dma_i = {e: 0 for e in self.ENGS}
        for e in ("sp", "act", "pool"):
            for i in range(self.NDMA):
                k = "d_%s_%d" % (e, i)
                self.sems[k] = self.es.enter_context(nc.semaphore(k))
                self.cnt[k] = 0
        self.nops = 0

    def sb(self, name, shape, dtype=F32):
        return self.es.enter_context(self.nc.sbuf_tensor(name, list(shape), dtype))

    def ps(self, name, shape, dtype=F32):
        return self.es.enter_context(self.nc.psum_tensor(name, list(shape), dtype))

    def _deps(self, eng, reads, writes):
        deps = {}

        def add(d):
            if d is None:
                return
            k, v = d
            if eng == "pe" and k == "pe":
                return
            if deps.get(k, 0) < v:
                deps[k] = v
        for t in reads:
            add(t.lastw)
        for t in writes:
            add(t.lastw)
            for r in t.readers:
                add(r)
        out = []
        w = self.waited[eng]
        for k, v in deps.items():
            if w.get(k, 0) < v:
                w[k] = v
                out.append((k, v))
        return out

    def _mark(self, reads, writes, tok):
        for t in reads:
            t.readers.append(tok)
            if len(t.readers) > 64:
                t.readers = t.readers[-64:] if False else t.readers
        for t in writes:
            t.lastw = tok
            t.readers = []

    @staticmethod
    def _split(reads, writes):
        ex = [t for t in reads if t.excl]
        if ex:
            reads = [t for t in reads if not t.excl]
            writes = list(writes) + ex
        return reads, writes

    def op(self, eng, fn, reads=(), writes=()):
        reads, writes = self._split(reads, writes)
        waits = self._deps(eng, reads, writes)
        self.cnt[eng] += 1
        tok = (eng, self.cnt[eng])
        sems = self.sems
        semh = sems[eng]

        def run(e):
            for k, v in waits:
                e.wait_ge(sems[k], v)
            fn(e).then_inc(semh, 1)
        self.q[eng].append(run)
        self._mark(reads, writes, tok)
        self.nops += 1
        return tok

    def dma(self, eng, out, in_, reads=(), writes=(), fn=None):
        i = self.dma_i[eng]
        self.dma_i[eng] += 1
        k = "d_%s_%d" % (eng, i % self.NDMA)
        reads, writes = self._split(reads, writes)
        waits = self._deps(eng, reads, writes)
        prev = self.cnt[k]
        if prev and self.waited[eng].get(k, 0) < prev:
            self.waited[eng][k] = prev
            waits.append((k, prev))
        self.cnt[k] += 16
        tok = (k, self.cnt[k])
        sems = self.sems
        semh = sems[k]

        def run(e):
            for kk, v in waits:
                e.wait_ge(sems[kk], v)
            if fn is None:
                e.dma_start(out=out, in_=in_).then_inc(semh, 16)
            else:
                fn(e).then_inc(semh, 16)
        self.q[eng].append(run)
        self._mark(reads, writes, tok)
        self.nops += 1
        return tok

    def barrier(self):
        snap = [(k, v) for k, v in self.cnt.items() if v > 0]
        sems = self.sems
        for eng in self.ENGS:
            waits = []
            for k, v in snap:
                if eng == "pe" and k == "pe":
                    continue
                if self.waited[eng].get(k, 0) < v:
                    self.waited[eng][k] = v
                    waits.append((k, v))

            def run(e, waits=waits):
                for k, v in waits:
                    e.wait_ge(sems[k], v)
            self.q[eng].append(run)

    def finish(self, final_trks):
        waits = self._deps("sp", final_trks, final_trks)
        sems = self.sems

        def run(e):
            for k, v in waits:
                e.wait_ge(sems[k], v)
        self.q["sp"].append(run)
        nc = self.nc
        q = self.q
        with nc.allow_low_precision("bf16 matmul operands, fp32 accumulation"), nc.Block() as block:
            @block.tensor
            def _(e):
                for f in q["pe"]:
                    f(e)

            @block.scalar
            def _(e):
                for f in q["act"]:
                    f(e)

            @block.vector
            def _(e):
                for f in q["dve"]:
                    f(e)

            @block.gpsimd
            def _(e):
                for f in q["pool"]:
                    f(e)

            @block.sync
            def _(e):
                for f in q["sp"]:
                    f(e)
        self.es.close()


def _levels(C):
    return max(1, int(math.ceil(math.log2(C))))


def make_consts(cfg):
    T = cfg["T"]
    NB = cfg["NB"]
    past = cfg["PAST"]
    c = {}
    c["ident"] = np.eye(128, dtype=np.float32)
    rot = np.zeros((64, 64), np.float32)
    for m in range(32):
        rot[m + 32, m] = -1.0
        rot[m, m + 32] = 1.0
    c["rot"] = rot
    half = 32
    inv_freq = (10000.0 ** (-np.arange(half, dtype=np.float32) / half)).astype(np.float32)

    def tables(pos):
        ang = pos.astype(np.float32)[:, None] * inv_freq[None, :]
        return np.cos(ang).astype(np.float32), np.sin(ang).astype(np.float32)
    cp, sp_ = tables(np.arange(T))
    cs, ss = tables(past + (np.arange(NB * DEC_SEQ) % DEC_SEQ))
    c["rope_tok_p"] = np.concatenate([cp, sp_], axis=1)
    c["rope_tok_s"] = np.concatenate([cs, ss], axis=1)
    c["rope_fm_p"] = np.concatenate([cp.T, cp.T, sp_.T, sp_.T], axis=0).astype(np.float32)
    c["rope_fm_s"] = np.concatenate([cs.T, cs.T, ss.T, ss.T], axis=0).astype(np.float32)
    for C in (64, 16, 4):
        i = np.arange(C)[:, None]
        t = np.arange(C)[None, :]
        su = (i < t).astype(np.float32)
        sl = (i > t).astype(np.float32)
        iu = (i <= t).astype(np.float32)
        c["smask%d" % C] = np.concatenate([su, sl, su, iu, iu], axis=1)
    qi = np.arange(128)[:, None]
    ki = np.arange(128)[None, :]
    c["cmask"] = np.where(ki <= qi, 0.0, NEG).astype(np.float32)
    sm = np.full((32, NB, NB * DEC_SEQ), NEG, np.float32)
    for b in range(NB):
        for h in range(MH):
            for t in range(DEC_SEQ):
                for t2 in range(t + 1):
                    sm[h * DEC_SEQ + t, b, b * DEC_SEQ + t2] = 0.0
    c["smask_s"] = sm
    c["cmod"] = (np.arange(128) % 16).astype(np.float32).reshape(128, 1)
    return c


def build(cfg):
    SEQ = cfg["SEQ"]
    T = cfg["T"]
    NB = cfg["NB"]
    NPG = cfg["NPG"]
    NPHYS = cfg["NPHYS"]
    NS = NB * DEC_SEQ
    NT = 256
    NKB = (T + 127) // 128
    NGRP = NPG // 8

    nc = bass.Bass("TRN2", target_bir_lowering=False)

    def din(name, shape, dt=F32):
        return nc.dram_tensor(name, list(shape), dt, kind="ExternalInput").ap()

    def dout(name, shape, dt=F32):
        return nc.dram_tensor(name, list(shape), dt, kind="ExternalOutput").ap()

    xp = din("xp", [SEQ, D])
    meta = din("meta", [N_META, D])
    xs = din("xs", [NS, D])
    swkv = din("swkv", [NB, NH, HD, HD])
    sshift = din("sshift", [NB, D])
    c_lat = din("c_lat", [NPHYS * 16, 8 * KVR])
    c_kr = din("c_kr", [NPHYS * 16, 8 * ROPE])
    ptrep = din("ptrep", [NB, 128, NGRP], I32)
    vec128 = din("vec128", [99, 128])
    vec64 = din("vec64", [112, 64])
    latnorm = din("latnorm", [1, KVR])
    W = {}
    for nm, shp in [("rw_wr", [D, D]), ("rw_wk", [D, D]), ("rw_wv", [D, D]), ("rw_wo", [D, D]),
                    ("rw_w1", [D, 64]), ("rw_w2", [64, D]), ("rw_a1", [D, 64]), ("rw_a2", [64, D]),
                    ("rw_g1", [D, 128]), ("rw_g2", [128, D]),
                    ("ffn_up0", [D, DFF]), ("ffn_down0", [DFF, D]), ("ffn_up1", [D, DFF]), ("ffn_down1", [DFF, D]),
                    ("w_dkv", [D, KVR]), ("w_kr", [D, ROPE]), ("w_uk", [KVR, MH * 128]), ("w_uv", [KVR, MH * 128]),
                    ("w_dq", [D, QR]), ("w_uq", [QR, MH * 192]), ("w_o_mla", [D, D])]:
        W[nm] = din(nm, shp)
    C = {}
    for nm, shp in [("ident", [128, 128]), ("rot", [64, 64]), ("rope_tok_p", [T, 64]), ("rope_tok_s", [NS, 64]),
                    ("rope_fm_p", [128, T]), ("rope_fm_s", [128, NS]), ("smask64", [64, 320]), ("smask16", [16, 80]),
                    ("smask4", [4, 20]), ("cmask", [128, 128]), ("smask_s", [32, NB, NS]), ("cmod", [128, 1])]:
        C[nm] = din("c_" + nm, shp)
    o_yp = dout("o_yp", [SEQ, D])
    o_ys = dout("o_ys", [NS, D])
    o_wkvp = dout("o_wkvp", [NH, HD, HD])
    o_shiftp = dout("o_shiftp", [8, 128])
    o_latp = dout("o_latp", [T, KVR])
    o_krp = dout("o_krp", [T, ROPE])
    o_wkvs = dout("o_wkvs", [NB, NH, HD, HD])
    o_shifts = dout("o_shifts", [NB, D])
    o_lats = dout("o_lats", [NS, KVR])
    o_krs = dout("o_krs", [NS, ROPE])

    p = Prog(nc)
    out_trk = Trk()

    NFB = 5
    pbank = [p.ps("pb%d" % i, [128, 512], F32) for i in range(NFB)]
    pbt = [Trk(True) for _ in range(NFB)]
    pbi = [0]
    obank = p.ps("obank", [128, 512], F32)
    obankt = Trk(True)
    hbank = [p.ps("hb%d" % i, [128, 1024], BF16) for i in range(2)]
    hbt = [Trk(True) for _ in range(2)]
    hbi = [0]

    def bank():
        i = pbi[0] % NFB
        pbi[0] += 1
        return pbank[i], pbt[i]

    def bbank():
        i = hbi[0] % 2
        hbi[0] += 1
        return hbank[i], hbt[i]

    def MM(out, pairs, reads, wtrk, start=True, stop=True):
        n = len(pairs)

        def f(e):
            ins = None
            for i, (l, r) in enumerate(pairs):
                ins = e.matmul(out, l, r, start=(start and i == 0), stop=(stop and i == n - 1))
            return ins
        p.op("pe", f, reads=reads, writes=[wtrk])

    def TR(out, in_, ident_ap, reads, wtrk):
        p.op("pe", lambda e: e.transpose(out, in_, ident_ap), reads=reads, writes=[wtrk])

    def ACT(out, in_, func, reads, writes, bias=None, scale=None, accum=None):
        kw = {}
        if bias is not None:
            kw["bias"] = bias
        if scale is not None:
            kw["scale"] = scale
        if accum is not None:
            kw["accum_out"] = accum
        p.op("act", lambda e: e.activation(out, in_, func, **kw), reads=reads, writes=writes)

    def TS(eng, out, in0, s1, s2, op0, op1, reads, writes):
        if s2 is None:
            p.op(eng, lambda e: e.tensor_scalar(out, in0, s1, None, op0), reads=reads, writes=writes)
        else:
            p.op(eng, lambda e: e.tensor_scalar(out, in0, s1, s2, op0, op1), reads=reads, writes=writes)

    def TT(eng, out, in0, in1, op, reads, writes):
        p.op(eng, lambda e: e.tensor_tensor(out, in0, in1, op), reads=reads, writes=writes)

    def STT(out, in0, scalar, in1, op0, op1, reads, writes):
        p.op("dve", lambda e: e.scalar_tensor_tensor(out, in0, scalar, in1, op0, op1), reads=reads, writes=writes)

    def CP(eng, out, in_, reads, writes):
        if eng == "act":
            p.op("act", lambda e: e.copy(out, in_), reads=reads, writes=writes)
        else:
            p.op(eng, lambda e: e.tensor_copy(out, in_), reads=reads, writes=writes)

    def RECIP(out, in_, reads, writes):
        p.op("dve", lambda e: e.reciprocal(out, in_), reads=reads, writes=writes)

    def MEMSET(eng, ap, val, writes):
        p.op(eng, lambda e: e.memset(ap, val), writes=writes)

    def OUT(dst, src, reads):
        p.dma("sp", dst, src, reads=reads, writes=[out_trk])

    ct = Trk()
    ident = p.sb("ident", [128, 128], F32)
    identb = p.sb("identb", [128, 128], BF16)
    identr = p.sb("identr", [64, 64], F32R)
    ones_r = p.sb("ones_r", [128, 128], F32R)
    onesf = p.sb("onesf", [128, 64], F32)
    rotm = p.sb("rotm", [64, 64], BF16)
    cmask = p.sb("cmask", [128, 128], F32)
    smk = {Cc: p.sb("smask%d" % Cc, [Cc, 5 * Cc], F32) for Cc in (64, 16, 4)}
    smask_s = p.sb("smask_s", [32, NB, NS], F32)
    cmod = p.sb("cmod", [128, 1], F32)
    lnbc = p.sb("lnbc", [128, KVR], F32)
    v128 = p.sb("v128", [128, 99], F32)
    v64 = p.sb("v64", [64, 112], F32)
    p.dma("sp", ident[:], C["ident"], writes=[ct])
    p.dma("pool", rotm[:], C["rot"], writes=[ct])
    p.dma("sp", cmask[:], C["cmask"], writes=[ct])
    for Cc in (64, 16, 4):
        p.dma("sp", smk[Cc][:], C["smask%d" % Cc], writes=[ct])
    p.dma("sp", smask_s[:], C["smask_s"], writes=[ct])
    p.dma("sp", cmod[:], C["cmod"], writes=[ct])
    p.dma("sp", lnbc[:], latnorm.partition_broadcast(128), writes=[ct])
    CP("dve", identb[:], ident[:], [ct], [ct])
    CP("dve", identr[:], ident[0:64, 0:64], [ct], [ct])
    onesb = p.sb("onesb", [128, 128], F32)
    MEMSET("pool", onesb[:], 1.0, [ct])
    CP("dve", ones_r[:], onesb[:], [ct], [ct])
    MEMSET("pool", onesf[:], 1.0, [ct])
    stg = p.sb("stg", [128, 1024], F32)
    stgt = Trk()
    p.dma("sp", stg[0:99, 0:128], vec128, writes=[stgt])
    p.dma("sp", stg[0:112, 128:192], vec64, writes=[stgt])
    b_, bt_ = bank()
    TR(b_[:, 0:99], stg[0:99, 0:128], ident[0:99, 0:99], [stgt, ct], bt_)
    TR(b_[0:64, 128:240], stg[0:112, 128:192], ident[0:112, 0:112], [stgt, ct], bt_)
    CP("dve", v128[:], b_[:, 0:99], [bt_], [ct])
    CP("dve", v64[:], b_[0:64, 128:240], [bt_], [ct])
    nw0 = p.sb("nw0", [64, 32], F32)
    TS("dve", nw0[:], v64[:, 0:32], -1.0, None, ALU.mult, None, [ct], [ct])

    wres_t = Trk()
    w1s = p.sb("w1s", [128, 8, 64], BF16)
    a1s = p.sb("a1s", [128, 8, 64], BF16)
    g1s = p.sb("g1s", [128, 8, 128], BF16)
    w2s = p.sb("w2s", [64, D], BF16)
    a2s = p.sb("a2s", [64, D], BF16)
    g2s = p.sb("g2s", [128, D], BF16)
    wukT = p.sb("wukT", [128, MH, KVR], BF16)

    def kcv(ap):
        return ap.rearrange("(kc p) m -> p kc m", p=128)
    p.dma("pool", w1s[:], kcv(W["rw_w1"]), writes=[wres_t])
    p.dma("pool", a1s[:], kcv(W["rw_a1"]), writes=[wres_t])
    p.dma("pool", g1s[:], kcv(W["rw_g1"]), writes=[wres_t])
    p.dma("pool", w2s[:], W["rw_w2"], writes=[wres_t])
    p.dma("pool", a2s[:], W["rw_a2"], writes=[wres_t])
    p.dma("pool", g2s[:], W["rw_g2"], writes=[wres_t])

    NSLAB = 4
    slabs = [p.sb("slab%d" % i, [128, 4096], BF16) for i in range(NSLAB)]
    slabt = [Trk() for _ in range(NSLAB)]
    slabi = [0]

    def load_slab(parts):
        i = slabi[0] % NSLAB
        slabi[0] += 1
        for (vf, src) in parts:
            p.dma("pool", vf(slabs[i]), src, writes=[slabt[i]])
        return slabs[i], slabt[i]

    def v_k8(s):
        return s[:].rearrange("p (a b) -> p a b", a=8)

    def v_k4(s):
        return s[:].rearrange("p (a b) -> p a b", a=4)

    def v_h16(s):
        return s[0:64, :].rearrange("p (a b) -> p a b", a=16)

    sl_, slt_ = load_slab([(lambda s: s[:, 0:2048].rearrange("p (a b) -> p a b", a=2), kcv(W["w_uk"]))])
    wuk_nat = sl_[:, 0:2048].rearrange("p (a b) -> p a b", a=2)
    for h in range(MH):
        hb_, hbt_ = bbank()
        for rc in range(2):
            TR(hb_[:, rc * 128:(rc + 1) * 128], wuk_nat[:, rc, h * 128:(h + 1) * 128], identb[:], [slt_, ct], hbt_)
        CP("dve", wukT[:, h, :], hb_[:, 0:256], [hbt_], [wres_t])

    NMAX = NT
    X = p.sb("X", [128, 8, NMAX], F32)
    Xt = [Trk() for _ in range(8)]
    XNX = p.sb("XNX", [128, 16, NMAX], BF16)
    XNb = XNX[:, 0:8, :]
    XXb = XNX[:, 8:16, :]
    XNt = Trk()
    XXt = XNt
    XM = [p.sb("XM%d" % i, [128, 8, NMAX], BF16) for i in range(4)]
    XMt = [Trk() for _ in range(4)]
    OGs = XNX[0:64, :, :]
    OGt = XNt
    carry = p.sb("carry", [128, 8], F32)
    carryt = Trk()
    rstd = p.sb("rstd", [128, NMAX], F32)
    rstdt = Trk()
    sq = [p.sb("sq%d" % i, [128, NMAX], BF16) for i in range(2)]
    ones_b = p.sb("ones_b", [128, 128], BF16)
    CP("dve", ones_b[:], onesb[:], [ct], [ct])
    sqt = [Trk() for _ in range(2)]
    sqi = [0]
    HW = p.sb("HW", [64, NMAX], BF16)
    HA = p.sb("HA", [64, NMAX], BF16)
    HG = p.sb("HG", [128, NMAX], BF16)
    Hlt = Trk()
    tmpA = p.sb("tmpA", [128, NMAX], F32)
    tmpAt = Trk()
    NG = 20
    Gp = [p.sb("G%d" % i, [128, 256], F32) for i in range(NG)]
    Gt = [Trk() for _ in range(NG)]

    def g32(i, parts=128, n=None):
        return Gp[i][0:parts, 0:(n if n is not None else 256)]

    def gbf(i, parts=128):
        return Gp[i][0:parts, :].bitcast(BF16)

    GR = [p.sb("GR%d" % i, [64, 256], F32R) for i in range(5)]
    GRt = [Trk() for _ in range(5)]

    NCH = 4
    TM = p.sb("TM", [64, NCH, 192], F32R)
    TMt = [Trk() for _ in range(NCH)]
    MMs = p.sb("MMs", [64, NCH, 320], F32R)
    MMt = [Trk() for _ in range(NCH)]
    NN = [p.sb("NN%d" % i, [64, NCH, 128], F32R) for i in range(2)]
    NNt = [Trk() for _ in range(2)]
    TT_ = [p.sb("TT%d" % i, [64, NCH, 64], F32R) for i in range(2)]
    TTt = [Trk() for _ in range(2)]
    WT = p.sb("WT", [64, 64], F32R)
    WTt = Trk()
    UT = p.sb("UT", [64, 64], F32R)
    UTt = Trk()
    SNAT = stg[0:64, :].rearrange("p (a b) -> p a b", a=16)
    SNATt = stgt
    kvcol = p.sb("kvcol", [128, 8], F32)
    kvcolt = Trk()
    ropet = p.sb("ropet", [128, 64], F32)
    ropett = Trk()
    ropef = p.sb("ropef", [64, NMAX], F32)
    rope_s2 = p.sb("rope_s2", [64, NMAX], F32)
    ropeft = Trk()
    QPr = p.sb("QPr", [64, NMAX], BF16)
    QPf = p.sb("QPf", [64, NMAX], F32)
    QPt = Trk()
    QLh = p.sb("QLh", [128, 2, NMAX], BF16)
    QPEh = p.sb("QPEh", [64, NMAX], BF16)
    QLt = Trk()
    OLT = p.sb("OLT", [128, 2, NMAX], BF16)
    OLTt = Trk()
    sm2 = p.sb("sm2", [128, 16], F32)
    sm2t = Trk()
    sm3 = p.sb("sm3", [128, 16], F32)
    sm3t = Trk()
    sm_ = p.sb("sm_", [128, 16], F32)
    smt = Trk()
    idxf = p.sb("idxf", [128, NGRP], F32)
    idxi = p.sb("idxi", [128, NGRP], I32)
    idxr = p.sb("idxr", [128, NGRP], I32)
    idxt = Trk()
    QB = p.sb("QB", [128, 2, 32], BF16)
    QPB = p.sb("QPB", [64, 32], BF16)
    QBt = Trk()
    shT = p.sb("shT", [128, 8, NB], F32)
    shTt = Trk()
    LATTs = p.sb("LATTs", [128, 2, 128], BF16)
    KRTs = p.sb("KRTs", [64, 128], BF16)
    LATTOKs = p.sb("LATTOKs", [128, 1, KVR], BF16)
    KVst = Trk()

    AW = max(NKB * 320 + 2048, 6 * 2048 + NB * 128) + 64
    arena = p.sb("arena", [128, AW], F32)
    _off = [0]

    def carve(nwords, dtype=F32, parts=128):
        a = arena[0:parts, _off[0]:_off[0] + nwords]
        _off[0] += nwords
        if dtype is not F32:
            a = a.bitcast(dtype)
        return a
    _off[0] = 0
    LATT = carve(NKB * 128, BF16).rearrange("p (a b) -> p a b", a=2)
    KRT = carve(NKB * 64, BF16, 64)
    LATTOK = carve(NKB * 128, BF16).rearrange("p (a b) -> p a b", a=NKB)
    S32 = carve(NH * 64, F32, 64).rearrange("p (a b) -> p a b", a=NH)
    SR = p.sb("SR", [64, max(NH, NB), 64], F32R)
    KVt = Trk()
    S32t = [Trk() for _ in range(NH)]
    SRt = [Trk() for _ in range(NH)]
    _off[0] = 0
    stL = [carve(8 * KVR).rearrange("p (a b) -> p a b", a=8) for _ in range(2)]
    stK = [carve(8 * ROPE).rearrange("p (a b) -> p a b", a=8) for _ in range(2)]
    stt_ = [Trk() for _ in range(2)]
    stLb = carve(8 * KVR // 2, BF16).rearrange("p (a b) -> p a b", a=8)
    stKb = carve(8 * ROPE // 2, BF16).rearrange("p (a b) -> p a b", a=8)
    stbt = Trk()
    LTs = carve(1024, BF16).rearrange("p (a b) -> p a b", a=2)
    KTs = carve(512, BF16, 64)
    LTst = Trk()
    SS32 = carve(NB * 64, F32, 64).rearrange("p (a b) -> p a b", a=NB)
    SSR = SR
    SSt = [Trk() for _ in range(NB)]
    SSRt = [Trk() for _ in range(NB)]
    QLs = carve(MH * 2 * NS // 2, BF16).rearrange("p (h r n) -> p h r n", h=MH, r=2)
    OLTs = carve(MH * 2 * NS // 2, BF16).rearrange("p (h r n) -> p h r n", h=MH, r=2)
    QPEs = carve(MH * NS // 2, BF16, 64).rearrange("p (h n) -> p h n", h=MH)
    QLst = Trk()
    OLTst = Trk()

    epsc = p.sb("epsc", [128, 4], F32)
    MEMSET("pool", epsc[:, 0:1], NORM_EPS, [ct])
    MEMSET("pool", epsc[:, 1:2], GN_EPS, [ct])

    def load_x_block(rows_src, nt, col0):
        for (src, r0, n) in rows_src:
            p.dma("sp", stg[r0:r0 + n, :], src, writes=[stgt])
        for g in range(2):
            b, bt = bank()
            for j in range(4):
                kc = g * 4 + j
                TR(b[:, j * 128:j * 128 + nt], stg[0:nt, kc * 128:(kc + 1) * 128], ident[0:nt, 0:nt], [stgt, ct], bt)
            for j in range(4):
                kc = g * 4 + j
                CP("act" if j % 2 else "dve", X[:, kc, col0:col0 + nt], b[:, j * 128:j * 128 + nt], [bt], [Xt[kc]])

    def rms_rstd(N, srcs, nfeat):
        b, bt = bank()
        n = len(srcs)
        for i_, (ap, t) in enumerate(srcs):
            i = sqi[0] % 2
            sqi[0] += 1
            ACT(sq[i][:, :N], ap, AF.Square, [t], [sqt[i]])
            MM(b[:, :N], [(ones_b[:, :], sq[i][:, :N])], [sqt[i], ct], bt, start=(i_ == 0), stop=(i_ == n - 1))
        ACT(rstd[:, :N], b[:, :N], AF.Ln, [bt, ct], [rstdt], scale=1.0 / nfeat, bias=epsc[:, 0:1])
        ACT(rstd[:, :N], rstd[:, :N], AF.Exp, [rstdt], [rstdt], scale=-0.5)

    def norm_to_bf16(N, gcol0, dst, dstt):
        rms_rstd(N, [(X[:, kc, :N], Xt[kc]) for kc in range(8)], D)
        for kc in range(8):
            STT(dst[:, kc, :N], X[:, kc, :N], v128[:, gcol0 + kc:gcol0 + kc + 1], rstd[:, :N], ALU.mult, ALU.mult,
                [Xt[kc], rstdt, ct], [dstt])

    RI = dict(r=0, k=1, v=2, e1=3, L=4, Lex=5, a=6, kkraw=7, k2=8, P=9, Pex=10, Pinv=11, kka=12, G=13, t1=14, gsb=15, xnf=16)
    RI.update(logd=RI["e1"], kk=RI["kkraw"], bonus=RI["e1"], Bh=RI["Pinv"], Kh=RI["a"], cen=RI["Lex"], y=RI["Pex"])
    RR = dict(At=0, Bt=1, Kt=2, Rt=3, kksq=4)
    RR.update(rkk=RR["kksq"], osb=RR["At"], censq=RR["Bt"])

    def rwkv_layer(N, C_, chunks, mode):
        msk = smk[C_]
        rms_rstd(N, [(X[:, kc, :N], Xt[kc]) for kc in range(8)], D)
        xnf, xnft = g32(RI["xnf"], 128, N), Gt[RI["xnf"]]
        for kc in range(8):
            STT(xnf, X[:, kc, :N], v128[:, kc:kc + 1], rstd[:, :N], ALU.mult, ALU.mult, [Xt[kc], rstdt, ct], [xnft])
            CP("act", XNb[:, kc, :N], xnf, [xnft], [XNt])
            if mode == "p":
                if N > 1:
                    TT("dve", XXb[:, kc, 1:N], xnf[:, 0:N - 1], xnf[:, 1:N], ALU.subtract, [xnft], [XXt])
                TT("dve", XXb[:, kc, 0:1], carry[:, kc:kc + 1], xnf[:, 0:1], ALU.subtract, [xnft, carryt], [XXt])
                CP("dve", carry[:, kc:kc + 1], xnf[:, N - 1:N], [xnft, carryt], [carryt])
            else:
                xn4 = xnf.rearrange("p (b t) -> p b t", t=DEC_SEQ)
                xx4 = XXb[:, kc, :N].rearrange("p (b t) -> p b t", t=DEC_SEQ)
                TT("dve", xx4[:, :, 1:DEC_SEQ], xn4[:, :, 0:DEC_SEQ - 1], xn4[:, :, 1:DEC_SEQ], ALU.subtract, [xnft], [XXt])
                TT("dve", xx4[:, :, 0:1], shT[:, kc, :].unsqueeze(2), xn4[:, :, 0:1], ALU.subtract, [xnft, shTt], [XXt])
                CP("dve", shT[:, kc, :].unsqueeze(2), xn4[:, :, DEC_SEQ - 1:DEC_SEQ], [xnft, shTt, XXt], [shTt])

        def make_xm(m, dst, dstt):
            for kc in range(8):
                STT(dst[:, kc, :N], XXb[:, kc, :N], v128[:, 48 + m * 8 + kc:48 + m * 8 + kc + 1], XNb[:, kc, :N],
                    ALU.mult, ALU.add, [XXt, XNt, ct], [dstt])
        for (m, wsb, M_) in ((1, w1s, 64), (4, a1s, 64), (5, g1s, 128)):
            make_xm(m, XM[3], XMt[3])
            b, bt = bank()
            MM(b[0:M_, :N], [(wsb[:, kc, :], XM[3][:, kc, :N]) for kc in range(8)], [XMt[3], wres_t], bt)
            if m == 1:
                ACT(tmpA[0:64, :N], b[0:64, :N], AF.Exp, [bt], [tmpAt], scale=2.0)
                TS("dve", tmpA[0:64, :N], tmpA[0:64, :N], 1.0, None, ALU.add, None, [tmpAt], [tmpAt])
                RECIP(tmpA[0:64, :N], tmpA[0:64, :N], [tmpAt], [tmpAt])
                TS("dve", HW[:, :N], tmpA[0:64, :N], -2.0, 1.0, ALU.mult, ALU.add, [tmpAt], [Hlt])
            elif m == 4:
                CP("act", HA[:, :N], b[0:64, :N], [bt], [Hlt])
            else:
                ACT(tmpA[:, :N], b[:, :N], AF.Exp, [bt], [tmpAt], scale=-1.0)
                TS("dve", tmpA[:, :N], tmpA[:, :N], 1.0, None, ALU.add, None, [tmpAt], [tmpAt])
                RECIP(HG[:, :N], tmpA[:, :N], [tmpAt], [Hlt])
        for i, m in enumerate((0, 2, 3)):
            make_xm(m, XM[i], XMt[i])

        def T_(n):
            return g32(RI[n], 64, N), Gt[RI[n]]

        def R_(n):
            return GR[RR[n]][:, 0:N], GRt[RR[n]]

        def Tc(n, c0, cs):
            return Gp[RI[n]][0:64, c0:c0 + cs]

        def Rc(n, c0, cs):
            return GR[RR[n]][:, c0:c0 + cs]
        wnames = ("rw_wr", "rw_wk", "rw_wv")
        for h in range(NH):
            if h % 8 == 0:
                g = h // 8
                sl = [load_slab([(v_k8, kcv(W[nm])[:, :, g * 512:(g + 1) * 512])]) for nm in wnames]
            if mode == "s":
                p.dma("sp", SNAT[:, 0:NB, :], swkv[:, h, :, :].rearrange("b v k -> v b k"), writes=[SNATt])
                for g0 in range(0, NB, 8):
                    b, bt = bank()
                    nb_ = min(8, NB - g0)
                    for j in range(nb_):
                        TR(b[0:64, j * 64:(j + 1) * 64], SNAT[:, g0 + j, :], ident[0:64, 0:64], [SNATt, ct], bt)
                    CP("act", SSR[:, g0:g0 + nb_, :], b[0:64, 0:nb_ * 64].rearrange("p (a b) -> p a b", a=nb_), [bt], [SSRt[g0 + j] for j in range(nb_)])
            hc = (h % 8) * 64
            r, rt = T_("r")
            k, kt = T_("k")
            v, vt = T_("v")
            for i, (dst, dt_) in enumerate(((r, rt), (k, kt), (v, vt))):
                b, bt = bank()
                sv = v_k8(sl[i][0])
                MM(b[0:64, :N], [(sv[:, kc, hc:hc + 64], XM[i][:, kc, :N]) for kc in range(8)], [XMt[i], sl[i][1]], bt)
                CP("act" if i == 1 else "dve", dst, b[0:64, :N], [bt], [dt_])
            zb, zbt = bank()
            if 3 * N <= 512:
                zps, aps, gps = zb[0:64, 0:N], zb[0:64, N:2 * N], zb[0:64, 2 * N:3 * N]
                gbt_ = zbt
            else:
                gb_, gbt_ = bank()
                zps, aps, gps = zb[0:64, 0:N], zb[0:64, N:2 * N], gb_[0:64, 0:N]
            MM(zps, [(w2s[:, h * 64:(h + 1) * 64], HW[:, :N])], [Hlt, wres_t], zbt)
            MM(aps, [(a2s[:, h * 64:(h + 1) * 64], HA[:, :N])], [Hlt, wres_t], zbt)
            MM(gps, [(g2s[:, h * 64:(h + 1) * 64], HG[:, :N])], [Hlt, wres_t], gbt_)
            e1, e1t = T_("e1")
            L, Lt = T_("L")
            Lex, Lext = T_("Lex")
            a, at = T_("a")
            kkraw, kkrawt = T_("kkraw")
            k2, k2t = T_("k2")
            P_, Pt_ = T_("P")
            Pex, Pext = T_("Pex")
            Pinv, Pinvt = T_("Pinv")
            kka, kkat = T_("kka")
            G, Gt_ = T_("G")
            t1, t1t = T_("t1")
            gsb, gsbt = T_("gsb")
            At, Att = R_("At")
            Bt, Btt = R_("Bt")
            Kt, Ktt = R_("Kt")
            Rt, Rtt = R_("Rt")
            kksq, kksqt = R_("kksq")
            logd, logdt = e1, e1t
            kk, kkt = kkraw, kkrawt
            ACT(e1, zps, AF.Exp, [zbt, ct], [e1t], scale=-1.0, bias=nw0[:, h:h + 1])
            TS("dve", e1, e1, 1.0, None, ALU.add, None, [e1t], [e1t])
            RECIP(e1, e1, [e1t], [e1t])
            ACT(logd, e1, AF.Copy, [e1t], [e1t], scale=-math.exp(-0.5))
            for (c0, cs) in chunks:
                p.op("dve", lambda e, c0=c0, cs=cs: e.tensor_tensor_scan(Tc("L", c0, cs), onesf[0:64, 0:cs], Tc("logd", c0, cs), 0.0, ALU.mult, ALU.add),
                     reads=[logdt, ct], writes=[Lt])
            TT("pool", Lex, L, logd, ALU.subtract, [Lt, logdt], [Lext])
            ACT(a, aps, AF.Exp, [zbt, ct], [at], scale=-1.0, bias=nw0[:, 16 + h:17 + h])
            TS("dve", a, a, 1.0, None, ALU.add, None, [at], [at])
            RECIP(a, a, [at], [at])
            CP("act", gsb, gps, [gbt_], [gsbt])
            ACT(kkraw, k, AF.Copy, [kt, ct], [kkrawt], scale=v64[:, 32 + h:33 + h])
            ACT(kksq, kkraw, AF.Square, [kkrawt], [kksqt])
            b, bt = bank()
            MM(b[0:64, :N], [(ones_r[0:64, 0:64], kksq)], [kksqt, ct], bt)
            TS("dve", t1, b[0:64, :N], 1e-24, None, ALU.max, None, [bt], [t1t])
            ACT(t1, t1, AF.Ln, [t1t], [t1t])
            ACT(t1, t1, AF.Exp, [t1t], [t1t], scale=-0.5)
            TT("dve", kk, kkraw, t1, ALU.mult, [kkrawt, t1t], [kkt])
            TS("dve", t1, a, -1.0, v64[:, 48 + h:49 + h], ALU.add, ALU.mult, [at, ct, t1t], [t1t])
            STT(k2, t1, 1.0, k, ALU.add, ALU.mult, [t1t, kt], [k2t])
            ACT(P_, L, AF.Exp, [Lt], [Pt_])
            ACT(Pex, Lex, AF.Exp, [Lext], [Pext])
            ACT(Pinv, L, AF.Exp, [Lt], [Pinvt], scale=-1.0)
            STT(At, kk, -1.0, Pex, ALU.mult, ALU.mult, [kkt, Pext], [Att])
            TT("pool", kka, kk, a, ALU.mult, [kkt, at], [kkat])
            TT("dve", Bt, kka, Pinv, ALU.mult, [kkat, Pinvt], [Btt])
            TT("pool", Kt, k2, Pinv, ALU.mult, [k2t, Pinvt], [Ktt])
            TT("dve", Rt, r, P_, ALU.mult, [rt, Pt_], [Rtt])
            for (c0, cs) in chunks:
                ACT(Tc("G", c0, cs), Tc("L", c0, cs), AF.Exp, [Lt], [Gt_], scale=-1.0, bias=Tc("L", c0 + cs - 1, 1))
            Bh, Bht = T_("Bh")
            Kh, Kht = T_("Kh")
            TT("pool", Bh, kka, G, ALU.mult, [kkat, Gt_, Pinvt], [Bht])
            TT("dve", Kh, k2, G, ALU.mult, [k2t, Gt_, at], [Kht])
            rkk, rkkt = R_("rkk")
            bonus, bonust = T_("bonus")
            STT(rkk, r, v64[:, 64 + h:65 + h], k2, ALU.mult, ALU.mult, [rt, k2t, ct, kksqt], [rkkt])
            b, bt = bank()
            MM(b[0:64, :N], [(ones_r[0:64, 0:64], rkk)], [rkkt, ct], bt)
            TT("dve", bonus, v, b[0:64, :N], ALU.mult, [vt, bt, e1t], [bonust])
            ob, obt = obank, obankt
            for ug in range(0, len(chunks), NCH):
                cl = chunks[ug:ug + NCH]
                nch = len(cl)
                for ci, (c0, cs) in enumerate(cl):
                    b, bt = bank()
                    TR(b[0:cs, 0:64], Tc("Bh", c0, cs), ident[0:64, 0:64], [Bht, ct], bt)
                    TR(b[0:cs, 64:128], Tc("Kh", c0, cs), ident[0:64, 0:64], [Kht, ct], bt)
                    TR(b[0:cs, 128:192], Tc("v", c0, cs), ident[0:64, 0:64], [vt, ct], bt)
                    CP("act", TM[0:cs, ci, :], b[0:cs, 0:192], [bt], [TMt[ci]])
                upb = max(1, 512 // (5 * C_))
                for u0 in range(0, nch, upb):
                    b, bt = bank()
                    us = list(range(u0, min(nch, u0 + upb)))
                    for ui, ci in enumerate(us):
                        c0, cs = cl[ci]
                        o = ui * 5 * C_
                        A_, B_, K_, R__ = Rc("At", c0, cs), Rc("Bt", c0, cs), Rc("Kt", c0, cs), Rc("Rt", c0, cs)
                        for kind, (l_, r_) in enumerate(((B_, A_), (A_, B_), (K_, A_), (B_, R__), (K_, R__))):
                            MM(b[0:cs, o + kind * C_:o + kind * C_ + cs], [(l_, r_)], [Att, Btt, Ktt, Rtt], bt)
                    for ui, ci in enumerate(us):
                        c0, cs = cl[ci]
                        o = ui * 5 * C_
                        TT("dve", MMs[0:cs, ci, 0:5 * C_], b[0:cs, o:o + 5 * C_], msk[0:cs, :], ALU.mult, [bt, ct], [MMt[ci]])
                csz = cl[0][1]
                lv = _levels(csz)
                allM = [MMt[ci] for ci in range(nch)]
                TT("dve", TT_[0][0:csz, 0:nch, 0:csz], MMs[0:csz, 0:nch, 0:csz],
                   identr[0:csz, 0:csz].unsqueeze(1).broadcast_to([csz, nch, csz]), ALU.add, allM + [ct], [TTt[0]])
                cur = 0
                for l in range(1, lv):
                    last = (l == lv - 1)
                    b, bt = bank()
                    for ci in range(nch):
                        if l == 1:
                            Np, NpT, rd = MMs[0:csz, ci, 0:csz], MMs[0:csz, ci, C_:C_ + csz], [MMt[ci]]
                        else:
                            Np, NpT, rd = NN[(l - 1) % 2][0:csz, ci, 0:csz], NN[(l - 1) % 2][0:csz, ci, 64:64 + csz], [NNt[(l - 1) % 2]]
                        if not last:
                            MM(b[0:csz, ci * 128:ci * 128 + csz], [(NpT, Np)], rd, bt)
                        MM(b[0:csz, ci * 128 + 64:ci * 128 + 64 + csz], [(Np, NpT)], rd, bt)
                    bv = b[0:csz, 0:nch * 128].rearrange("p (a b) -> p a b", a=nch)
                    if not last:
                        CP("act", NN[l % 2][0:csz, 0:nch, 0:csz], bv[:, :, 0:csz], [bt], [NNt[l % 2]])
                    CP("act", NN[l % 2][0:csz, 0:nch, 64:64 + csz], bv[:, :, 64:64 + csz], [bt], [NNt[l % 2]])
                    b, bt = bank()
                    for ci in range(nch):
                        MM(b[0:csz, ci * 64:ci * 64 + csz], [(NN[l % 2][0:csz, ci, 64:64 + csz], TT_[cur][0:csz, ci, 0:csz])],
                           [NNt[l % 2], TTt[cur]], bt)
                    bv = b[0:csz, 0:nch * 64].rearrange("p (a b) -> p a b", a=nch)
                    TT("dve", TT_[1 - cur][0:csz, 0:nch, 0:csz], bv[:, :, 0:csz], TT_[cur][0:csz, 0:nch, 0:csz], ALU.add,
                       [bt, TTt[cur]], [TTt[1 - cur]])
                    cur = 1 - cur
                Tfin, Tfint = TT_[cur], TTt[cur]
                for ci, (c0, cs) in enumerate(cl):
                    if mode == "p":
                        s32, s32t, sr, srt = S32[:, h, :], S32t[h], SR[:, h, :], SRt[h]
                    else:
                        bi = ug + ci
                        s32, s32t, sr, srt = SS32[:, bi, :], SSt[bi], SSR[:, bi, :], SSRt[bi]
                    A_, R__ = Rc("At", c0, cs), Rc("Rt", c0, cs)
                    Mak = MMs[0:cs, ci, 2 * C_:2 * C_ + cs]
                    Mbr = MMs[0:cs, ci, 3 * C_:3 * C_ + cs]
                    Mkr = MMs[0:cs, ci, 4 * C_:4 * C_ + cs]
                    BhT, KhT, VT = TM[0:cs, ci, 0:64], TM[0:cs, ci, 64:128], TM[0:cs, ci, 128:192]
                    b, bt = bank()
                    MM(b[0:cs, 0:64], [(A_, sr), (Mak, VT)], [Att, srt, MMt[ci], TMt[ci]], bt)
                    CP("act", WT[0:cs, :], b[0:cs, 0:64], [bt], [WTt])
                    MM(b[0:cs, 64:128], [(Tfin[0:cs, ci, 0:cs], WT[0:cs, :])], [Tfint, WTt], bt)
                    CP("act", UT[0:cs, :], b[0:cs, 64:128], [bt], [UTt])
                    MM(ob[0:64, c0:c0 + cs], [(sr, R__), (UT[0:cs, :], Mbr), (VT, Mkr)], [srt, Rtt, UTt, MMt[ci], TMt[ci]], obt)
                    MM(b[0:64, 128:192], [(BhT, UT[0:cs, :]), (KhT, VT)], [TMt[ci], UTt], bt)
                    STT(sr, sr.bitcast(F32), Tc("P", c0 + cs - 1, 1), b[0:64, 128:192], ALU.mult, ALU.add, [srt, Pt_, bt], [srt])
            osb, osbt = R_("osb")
            censq, censqt = R_("censq")
            cen, cent = T_("cen")
            y, yt = T_("y")
            CP("act", osb, ob[0:64, :N], [obt, Att], [osbt])
            b, bt = bank()
            MM(b[0:64, :N], [(ones_r[0:64, 0:64], osb)], [osbt, ct], bt)
            STT(cen, b[0:64, :N], -1.0 / 64, osb.bitcast(F32), ALU.mult, ALU.add, [bt, osbt, Lext], [cent])
            ACT(censq, cen, AF.Square, [cent, Btt], [censqt])
            b, bt = bank()
            MM(b[0:64, :N], [(ones_r[0:64, 0:64], censq)], [censqt, ct], bt)
            ACT(t1, b[0:64, :N], AF.Ln, [bt, ct], [t1t], scale=1.0 / 64, bias=epsc[0:64, 1:2])
            ACT(t1, t1, AF.Exp, [t1t], [t1t], scale=-0.5)
            TT("dve", y, cen, t1, ALU.mult, [cent, t1t, Pext], [yt])
            TS("dve", y, y, v64[:, 80 + h:81 + h], v64[:, 96 + h:97 + h], ALU.mult, ALU.add, [yt, ct], [yt])
            TT("pool", y, y, bonus, ALU.add, [yt, bonust], [yt])
            TT("dve", OGs[:, h, :N], y, gsb, ALU.mult, [yt, gsbt], [OGt])
            if mode == "s":
                for g0 in range(0, NB, 8):
                    b, bt = bank()
                    nb_ = min(8, NB - g0)
                    for j in range(nb_):
                        TR(b[0:64, j * 64:(j + 1) * 64], SSR[:, g0 + j, :].bitcast(F32), ident[0:64, 0:64], [SSRt[g0 + j], ct], bt)
                    CP("dve", SNAT[:, g0:g0 + nb_, :], b[0:64, 0:nb_ * 64].rearrange("p (a b) -> p a b", a=nb_), [bt, SNATt], [SNATt])
                OUT(o_wkvs[:, h, :, :].rearrange("b v k -> v b k"), SNAT[:, 0:NB, :], [SNATt])
        for g in range(4):
            sl_, slt_ = load_slab([(v_h16, W["rw_wo"].rearrange("(h p) m -> p h m", p=64)[:, :, g * 256:(g + 1) * 256])])
            sv = v_h16(sl_)
            for mm_ in range(2):
                m = g * 2 + mm_
                b, bt = bank()
                MM(b[:, :N], [(sv[:, h, mm_ * 128:(mm_ + 1) * 128], OGs[:, h, :N]) for h in range(NH)], [OGt, slt_], bt)
                TT("dve", X[:, m, :N], X[:, m, :N], b[:, :N], ALU.add, [Xt[m], bt], [Xt[m]])

    def ffn_layer(N, li):
        xb, xbt = XM[3], XMt[3]
        norm_to_bf16(N, 8 if li == 0 else 32, xb, xbt)
        up = W["ffn_up%d" % li]
        dn = W["ffn_down%d" % li]

        def loads(jb):
            su = load_slab([(v_k8, kcv(up)[:, :, jb * 512:(jb + 1) * 512])])
            sd = load_slab([(v_k4, dn[jb * 512:(jb + 1) * 512, :].rearrange("(kc p) m -> p kc m", p=128))])
            return su, sd
        nxt = loads(0)
        for jb in range(8):
            (su_, sut), (sd_, sdt) = nxt
            su, sd = v_k8(su_), v_k4(sd_)
            for oc in range(4):
                gi = (jb % 2) * 4 + oc
                hh, hht = gbf(gi)[:, :N], Gt[gi]
                hr_, hrt = gbf(8 + oc % 2)[:, :N], Gt[8 + oc % 2]
                b, bt = bank()
                MM(b[:, :N], [(su[:, kc, oc * 128:(oc + 1) * 128], xb[:, kc, :N]) for kc in range(8)], [xbt, sut], bt)
                ACT(hr_, b[:, :N], AF.Relu, [bt], [hrt])
                TT("pool", hh, hr_, hr_, ALU.mult, [hrt], [hht])
            if jb + 1 < 8:
                nxt = loads(jb + 1)
            for m in range(8):
                b, bt = bank()
                MM(b[:, :N], [(sd[:, kc, m * 128:(m + 1) * 128], gbf((jb % 2) * 4 + kc)[:, :N]) for kc in range(4)],
                   [Gt[(jb % 2) * 4 + kc] for kc in range(4)] + [sdt], bt)
                TT("dve", X[:, m, :N], X[:, m, :N], b[:, :N], ALU.add, [Xt[m], bt], [Xt[m]])

    def kv_path(N, tok0, rope_tok, lat_out, kr_out, blk0, LATT_, KRT_, LATTOK_, kvt_):
        xb, xbt = XM[3], XMt[3]
        norm_to_bf16(N, 16, xb, xbt)
        sl_, slt_ = load_slab([(lambda s: s[:, 0:8 * 320].rearrange("p (a b) -> p a b", a=8)[:, :, 0:KVR], kcv(W["w_dkv"])),
                               (lambda s: s[:, 0:8 * 320].rearrange("p (a b) -> p a b", a=8)[:, :, KVR:KVR + ROPE], kcv(W["w_kr"]))])
        wdkv = sl_[:, 0:8 * 320].rearrange("p (a b) -> p a b", a=8)
        kvA, kvAt = g32(0), Gt[0]
        kvB, kvBt = g32(1), Gt[1]
        latb, latbt = gbf(2), Gt[2]
        nblk = (N + 127) // 128
        for tb in range(nblk):
            c0 = tb * 128
            nt = min(128, N - c0)
            b, bt = bank()
            MM(b[0:nt, 0:KVR + ROPE], [(xb[:, kc, c0:c0 + nt], wdkv[:, kc, :]) for kc in range(8)], [xbt, slt_], bt)
            MEMSET("pool", kvcol[0:nt, 0:1], 0.0, [kvcolt])
            ACT(kvA[0:nt, :], b[0:nt, 0:KVR], AF.Square, [bt], [kvAt, kvcolt], accum=kvcol[0:nt, 0:1])
            ACT(kvcol[0:nt, 1:2], kvcol[0:nt, 0:1], AF.Ln, [kvcolt, ct], [kvcolt], scale=1.0 / KVR, bias=epsc[0:nt, 0:1])
            ACT(kvcol[0:nt, 1:2], kvcol[0:nt, 1:2], AF.Exp, [kvcolt], [kvcolt], scale=-0.5)
            STT(kvA[0:nt, :], b[0:nt, 0:KVR], kvcol[0:nt, 1:2], lnbc[0:nt, :], ALU.mult, ALU.mult, [bt, kvcolt, ct, kvAt], [kvAt])
            p.dma("sp", ropet[0:nt, :], rope_tok[tok0 + c0:tok0 + c0 + nt, :], writes=[ropett])
            x1 = b[0:nt, KVR:KVR + 32]
            x2 = b[0:nt, KVR + 32:KVR + 64]
            o1, o2 = kvB[0:nt, 0:32], kvB[0:nt, 32:64]
            t_a, t_b = kvB[0:nt, 64:96], kvB[0:nt, 96:128]
            TT("dve", t_a, x1, ropet[0:nt, 0:32], ALU.mult, [bt, ropett, kvBt], [kvBt])
            TT("dve", t_b, x2, ropet[0:nt, 32:64], ALU.mult, [bt, ropett, kvBt], [kvBt])
            TT("dve", o1, t_a, t_b, ALU.subtract, [kvBt], [kvBt])
            TT("dve", t_a, x1, ropet[0:nt, 32:64], ALU.mult, [bt, ropett, kvBt], [kvBt])
            TT("dve", t_b, x2, ropet[0:nt, 0:32], ALU.mult, [bt, ropett, kvBt], [kvBt])
            TT("dve", o2, t_a, t_b, ALU.add, [kvBt], [kvBt])
            OUT(lat_out[tok0 + c0:tok0 + c0 + nt, :], kvA[0:nt, :], [kvAt])
            OUT(kr_out[tok0 + c0:tok0 + c0 + nt, :], kvB[0:nt, 0:ROPE], [kvBt])
            kb = blk0 + tb
            CP("act", LATTOK_[0:nt, kb, :], kvA[0:nt, :], [kvAt], [kvt_])
            CP("pool", latb[0:nt, 0:ROPE], kvB[0:nt, 0:ROPE], [kvBt], [latbt])
            hb_, hbt_ = bbank()
            for rc in range(2):
                TR(hb_[:, rc * 128:rc * 128 + nt], LATTOK_[0:nt, kb, rc * 128:(rc + 1) * 128], identb[0:nt, 0:nt], [kvt_, ct], hbt_)
            TR(hb_[0:64, 256:256 + nt], latb[0:nt, 0:ROPE], identb[0:nt, 0:nt], [latbt, ct], hbt_)
            for rc in range(2):
                CP("dve" if rc else "act", LATT_[:, rc, kb * 128:kb * 128 + nt], hb_[:, rc * 128:rc * 128 + nt], [hbt_], [kvt_])
            CP("dve", KRT_[:, kb * 128:kb * 128 + nt], hb_[0:64, 256:256 + nt], [hbt_], [kvt_])

    GI_CQ = (0, 1, 2)
    GI_CQN = (3, 4, 5)
    GI_QN, GI_PB, GI_PT, GI_ACC, GI_OLB = 6, 7, 8, 9, 10
    GI_OH = tuple(range(11, 19))
    GI_X2 = 19

    def mla_queries(N, tok0, rope_fm, head_cb):
        xb, xbt = XM[3], XMt[3]
        norm_to_bf16(N, 24, xb, xbt)
        p.dma("sp", ropef[:, :N], rope_fm[0:64, tok0:tok0 + N], writes=[ropeft])
        p.dma("sp", rope_s2[:, :N], rope_fm[64:128, tok0:tok0 + N], writes=[ropeft])
        sl_, slt_ = load_slab([(lambda s: s[:, 0:8 * QR].rearrange("p (a b) -> p a b", a=8), kcv(W["w_dq"]))])
        wdq = sl_[:, 0:8 * QR].rearrange("p (a b) -> p a b", a=8)
        for m in range(3):
            b, bt = bank()
            MM(b[:, :N], [(wdq[:, kc, m * 128:(m + 1) * 128], xb[:, kc, :N]) for kc in range(8)], [xbt, slt_], bt)
            CP("act", g32(GI_CQ[m], 128, N), b[:, :N], [bt], [Gt[GI_CQ[m]]])
        rms_rstd(N, [(g32(GI_CQ[m], 128, N), Gt[GI_CQ[m]]) for m in range(3)], QR)
        for m in range(3):
            STT(gbf(GI_CQN[m])[:, :N], g32(GI_CQ[m], 128, N), v128[:, 96 + m:97 + m], rstd[:, :N], ALU.mult, ALU.mult,
                [Gt[GI_CQ[m]], rstdt, ct], [Gt[GI_CQN[m]]])
        cqn_t = [Gt[GI_CQN[m]] for m in range(3)]
        QN, QNt = gbf(GI_QN), Gt[GI_QN]
        for hg in range(2):
            sq_, sqt_ = load_slab([(lambda s: s[:, 0:3 * 768].rearrange("p (a b) -> p a b", a=3),
                                    kcv(W["w_uq"])[:, :, hg * 768:(hg + 1) * 768])])
            wuq = sq_[:, 0:3 * 768].rearrange("p (a b) -> p a b", a=3)
            for hh_ in range(4):
                h = hg * 4 + hh_
                b, bt = bank()
                MM(b[:, :N], [(wuq[:, kc, hh_ * 192:hh_ * 192 + 128], gbf(GI_CQN[kc])[:, :N]) for kc in range(3)], cqn_t + [sqt_], bt)
                CP("act", QN[:, :N], b[:, :N], [bt], [QNt])
                b2, bt2 = bank()
                MM(b2[0:64, :N], [(wuq[:, kc, hh_ * 192 + 128:hh_ * 192 + 192], gbf(GI_CQN[kc])[:, :N]) for kc in range(3)], cqn_t + [sqt_], bt2)
                CP("act", QPr[:, :N], b2[0:64, :N], [bt2], [QPt])
                TT("dve", QPf[:, :N], b2[0:64, :N], ropef[:, :N], ALU.mult, [bt2, ropeft], [QPt])
                b3, bt3 = bank()
                MM(b3[0:64, :N], [(rotm[:, :], QPr[:, :N])], [QPt, ct], bt3)
                TT("dve", tmpA[0:64, :N], b3[0:64, :N], rope_s2[:, :N], ALU.mult, [bt3, ropeft], [tmpAt])
                head_cb(h, QN, QNt)

    def load_wuv():
        sl_, slt_ = load_slab([(lambda s: s[:, 0:2048].rearrange("p (a b) -> p a b", a=2), kcv(W["w_uv"]))])
        return sl_[:, 0:2048].rearrange("p (a b) -> p a b", a=2), slt_

    def prompt_attention(N, tok0):
        wuv, wuvt = load_wuv()
        Pbs = [(gbf(GI_PB), Gt[GI_PB]), (gbf(GI_ACC), Gt[GI_ACC])]
        PTs = [(gbf(GI_PT).rearrange("p (a b) -> p a b", a=4), Gt[GI_PT]), (gbf(GI_X2).rearrange("p (a b) -> p a b", a=4), Gt[GI_X2])]
        olb, olbt = gbf(GI_OLB), Gt[GI_OLB]

        def per_head(h, QN, QNt):
            STT(QPEh[:, :N], QPf[:, :N], 1.0, tmpA[0:64, :N], ALU.mult, ALU.add, [QPt, tmpAt], [QLt])
            ACT(QPEh[:, :N], QPEh[:, :N], AF.Copy, [QLt], [QLt], scale=ATTN_SCALE)
            for rc in range(2):
                b, bt = bank()
                MM(b[:, :N], [(wukT[:, h, rc * 128:(rc + 1) * 128], QN[:, :N])], [QNt, wres_t], bt)
                ACT(QLh[:, rc, :N], b[:, :N], AF.Copy, [bt], [QLt], scale=ATTN_SCALE)
            nqb = (N + 127) // 128
            for qb in range(nqb):
                q0 = qb * 128
                nq = min(128, N - q0)
                kend = tok0 + q0 + nq
                nseg = (kend + 511) // 512
                MXc = sm_[0:nq, 0:1]
                NM = sm_[0:nq, 1:2]
                Lc = sm_[0:nq, 2:3]
                RL = sm_[0:nq, 3:4]

                def scores(s):
                    k0 = s * 512
                    kl = min(512, kend - k0)
                    b, bt = bank()
                    MM(b[0:nq, 0:kl], [(QLh[:, 0, q0:q0 + nq], LATT[:, 0, k0:k0 + kl]), (QLh[:, 1, q0:q0 + nq], LATT[:, 1, k0:k0 + kl]),
                                       (QPEh[:, q0:q0 + nq], KRT[:, k0:k0 + kl])], [QLt, KVt], bt)
                    dpos = tok0 + q0 - k0
                    if 0 <= dpos < 512:
                        TT("dve", b[0:nq, dpos:dpos + nq], b[0:nq, dpos:dpos + nq], cmask[0:nq, 0:nq], ALU.add, [bt, ct], [bt])
                    return b, bt, k0, kl
                for s in range(nseg):
                    b, bt, k0, kl = scores(s)
                    p.op("dve", lambda e, b=b, kl=kl, s=s, nq=nq: e.reduce_max(sm2[0:nq, s:s + 1], b[0:nq, 0:kl], AX.X), reads=[bt], writes=[sm2t])
                p.op("dve", lambda e, nq=nq, nseg=nseg, MXc=MXc: e.reduce_max(MXc, sm2[0:nq, 0:nseg], AX.X), reads=[sm2t], writes=[smt])
                TS("dve", NM, MXc, -1.0, None, ALU.mult, None, [smt], [smt])
                MEMSET("pool", sm3[0:nq, 0:nseg], 0.0, [sm3t])
                for s in range(nseg):
                    b, bt, k0, kl = scores(s)
                    Pb, Pbt = Pbs[s % 2]
                    PT, PTt = PTs[s % 2]
                    ACT(Pb[0:nq, 0:kl], b[0:nq, 0:kl], AF.Exp, [bt, smt], [Pbt, sm3t], bias=NM, accum=sm3[0:nq, s:s + 1])
                    nkb = (kl + 127) // 128
                    hb_, hbt_ = bbank()
                    for j in range(nkb):
                        kn = min(128, kl - j * 128)
                        TR(hb_[0:kn, j * 128:j * 128 + nq], Pb[0:nq, j * 128:j * 128 + kn], identb[0:nq, 0:nq], [Pbt, ct], hbt_)
                    if kl == 512:
                        CP("act" if s % 2 else "dve", PT[:, 0:4, 0:nq], hb_[:, 0:512].rearrange("p (a b) -> p a b", a=4)[:, :, 0:nq], [hbt_], [PTt])
                    else:
                        for j in range(nkb):
                            kn = min(128, kl - j * 128)
                            CP("act" if j % 2 else "dve", PT[0:kn, j, 0:nq], hb_[0:kn, j * 128:j * 128 + nq], [hbt_], [PTt])
                    prs = []
                    for j in range(nkb):
                        kn = min(128, kl - j * 128)
                        prs.append((PT[0:kn, j, 0:nq], LATTOK[0:kn, k0 // 128 + j, :]))
                    MM(obank[0:nq, 0:KVR], prs, [PTt, KVt], obankt, start=(s == 0), stop=(s == nseg - 1))
                p.op("dve", lambda e, nq=nq, nseg=nseg, Lc=Lc: e.reduce_sum(Lc, sm3[0:nq, 0:nseg], AX.X), reads=[sm3t], writes=[smt])
                RECIP(RL, Lc, [smt], [smt])
                TS("dve", olb[0:nq, 0:KVR], obank[0:nq, 0:KVR], RL, None, ALU.mult, None, [obankt, smt], [olbt])
                hb_, hbt_ = bbank()
                for rc in range(2):
                    TR(hb_[:, rc * 128:rc * 128 + nq], olb[0:nq, rc * 128:(rc + 1) * 128], identb[0:nq, 0:nq], [olbt, ct], hbt_)
                for rc in range(2):
                    CP("act" if rc else "dve", OLT[:, rc, q0:q0 + nq], hb_[:, rc * 128:rc * 128 + nq], [hbt_], [OLTt])
            b, bt = bank()
            MM(b[:, :N], [(wuv[:, rc, h * 128:(h + 1) * 128], OLT[:, rc, :N]) for rc in range(2)], [OLTt, wuvt], bt)
            CP("act", gbf(GI_OH[h])[:, :N], b[:, :N], [bt], [Gt[GI_OH[h]]])
        return per_head

    def mla_out(N):
        for g in range(2):
            sl_, slt_ = load_slab([(v_k8, kcv(W["w_o_mla"])[:, :, g * 512:(g + 1) * 512])])
            sv = v_k8(sl_)
            for mm_ in range(4):
                m = g * 4 + mm_
                b, bt = bank()
                MM(b[:, :N], [(sv[:, h, mm_ * 128:(mm_ + 1) * 128], gbf(GI_OH[h])[:, :N]) for h in range(MH)],
                   [Gt[GI_OH[h]] for h in range(MH)] + [slt_], bt)
                TT("dve", X[:, m, :N], X[:, m, :N], b[:, :N], ALU.add, [Xt[m], bt], [Xt[m]])

    def final_out(N, dst_rows):
        rms_rstd(N, [(X[:, kc, :N], Xt[kc]) for kc in range(8)], D)
        for kc in range(8):
            STT(g32(kc, 128, N), X[:, kc, :N], v128[:, 40 + kc:41 + kc], rstd[:, :N], ALU.mult, ALU.mult, [Xt[kc], rstdt, ct], [Gt[kc]])
        nblk = (N + 127) // 128
        for tb in range(nblk):
            c0 = tb * 128
            nt = min(128, N - c0)
            for g in range(2):
                b, bt = bank()
                for j in range(4):
                    kc = g * 4 + j
                    TR(b[0:nt, j * 128:(j + 1) * 128], Gp[kc][:, c0:c0 + nt], ident[:, :], [Gt[kc], ct], bt)
                CP("act" if g else "dve", stg[0:nt, g * 512:(g + 1) * 512], b[0:nt, :], [bt, stgt], [stgt])
            for (dst, r0, n) in dst_rows[tb]:
                OUT(dst, stg[r0:r0 + n, :], [stgt])

    tiles = []
    t0 = 0
    while t0 < T:
        n = min(NT, T - t0)
        tiles.append((t0, n))
        t0 += n
    MEMSET("pool", carry[:, :], 0.0, [carryt])
    zer = p.sb("zer", [64, 64], F32)
    MEMSET("pool", zer[:, :], 0.0, [ct])
    for h in range(NH):
        CP("dve", SR[:, h, :], zer[:, :], [ct], [SRt[h]])
    PH = os.environ.get("MK_PH", "rwfkmgo")
    MAXT = int(os.environ.get("MK_MAXT", "999"))
    SPH = os.environ.get("MK_SPH", "rfkmago")
    if cfg.get("DO_PROMPT", True) and "P" not in os.environ.get("MK_SKIP", ""):
        for ti, (tok0, N) in enumerate(tiles):
            if ti >= MAXT:
                break
            nblk = (N + 127) // 128
            for tb in range(nblk):
                g0 = tok0 + tb * 128
                nt = min(128, N - tb * 128)
                rows = []
                if g0 < N_META:
                    rows.append((meta[g0:N_META, :], 0, N_META - g0))
                    rows.append((xp[0:nt - (N_META - g0), :], N_META - g0, nt - (N_META - g0)))
                else:
                    rows.append((xp[g0 - N_META:g0 - N_META + nt, :], 0, nt))
                load_x_block(rows, nt, tb * 128)
            Cc = 64 if N >= 64 else N
            chunks = [(c0, Cc) for c0 in range(0, N, Cc)]
            if "r" in PH:
                rwkv_layer(N, Cc, chunks, "p")
            if (ti == len(tiles) - 1 or ti == MAXT - 1) and "w" in PH:
                b, bt = bank()
                TR(b[0:8, 0:128], carry[:, :], ident[:, :], [carryt, ct], bt)
                CP("dve", stg[0:8, 0:128], b[0:8, 0:128], [bt, stgt], [stgt])
                OUT(o_shiftp, stg[0:8, 0:128], [stgt])
                for g0 in range(0, NH, 8):
                    b, bt = bank()
                    for j in range(8):
                        TR(b[0:64, j * 64:(j + 1) * 64], SR[:, g0 + j, :].bitcast(F32), ident[0:64, 0:64], [SRt[g0 + j], ct], bt)
                    CP("dve", SNAT[:, 0:8, :], b[0:64, 0:512].rearrange("p (a b) -> p a b", a=8), [bt, SNATt], [SNATt])
                    OUT(o_wkvp[g0:g0 + 8, :, :].rearrange("h v k -> v h k"), SNAT[:, 0:8, :], [SNATt])
            if "f" in PH:
                ffn_layer(N, 0)
            if "k" in PH:
                kv_path(N, tok0, C["rope_tok_p"], o_latp, o_krp, tok0 // 128, LATT, KRT, LATTOK, KVt)
            if "m" in PH:
                mla_queries(N, tok0, C["rope_fm_p"], prompt_attention(N, tok0))
                mla_out(N)
            if "g" in PH:
                ffn_layer(N, 1)
            dst_rows = []
            for tb in range(nblk):
                g0 = tok0 + tb * 128
                nt = min(128, N - tb * 128)
                if g0 < N_META:
                    dst_rows.append([(o_yp[0:nt - (N_META - g0), :], N_META - g0, nt - (N_META - g0))])
                else:
                    dst_rows.append([(o_yp[g0 - N_META:g0 - N_META + nt, :], 0, nt)])
            if "o" in PH:
                final_out(N, dst_rows)

    p.barrier()

    def sample_q_cb(N):
        def per_head(h, QN, QNt):
            STT(QPEs[:, h, :N], QPf[:, :N], 1.0, tmpA[0:64, :N], ALU.mult, ALU.add, [QPt, tmpAt], [QLst])
            ACT(QPEs[:, h, :N], QPEs[:, h, :N], AF.Copy, [QLst], [QLst], scale=ATTN_SCALE)
            for rc in range(2):
                b, bt = bank()
                MM(b[:, :N], [(wukT[:, h, rc * 128:(rc + 1) * 128], QN[:, :N])], [QNt, wres_t], bt)
                ACT(QLs[:, h, rc, :N], b[:, :N], AF.Copy, [bt], [QLst], scale=ATTN_SCALE)
        return per_head

    def sample_attend_all(N):
        nq = 32
        M_, L_, BM, MN, NM, CR, RS, RL = [sm_[0:nq, i:i + 1] for i in range(8)]
        Pb, Pbt = gbf(GI_PB), Gt[GI_PB]
        PT = gbf(GI_PT).rearrange("p (a b) -> p a b", a=4)
        PTt = Gt[GI_PT]
        acc, acct = g32(GI_ACC), Gt[GI_ACC]
        olb, olbt = gbf(GI_OLB), Gt[GI_OLB]

        def softmax_block(b, bt, kl, vblocks):
            p.op("dve", lambda e: e.reduce_max(BM, b[0:nq, 0:kl], AX.X), reads=[bt, smt], writes=[smt])
            TT("dve", MN, M_, BM, ALU.max, [smt], [smt])
            TS("dve", NM, MN, -1.0, None, ALU.mult, None, [smt], [smt])
            MEMSET("pool", RS, 0.0, [smt])
            ACT(CR, M_, AF.Exp, [smt], [smt], bias=NM)
            ACT(Pb[0:nq, 0:kl], b[0:nq, 0:kl], AF.Exp, [bt, smt], [Pbt, smt], bias=NM, accum=RS)
            STT(L_, L_, CR, RS, ALU.mult, ALU.add, [smt], [smt])
            CP("pool", M_, MN, [smt], [smt])
            nkb = len(vblocks)
            hb_, hbt_ = bbank()
            o = 0
            for j, (kn, vap, rd) in enumerate(vblocks):
                TR(hb_[0:kn, j * 32:j * 32 + nq], Pb[0:nq, o:o + kn], identb[0:nq, 0:nq], [Pbt, ct], hbt_)
                o += kn
            kn0 = vblocks[0][0]
            CP("dve", PT[0:kn0, 0:nkb, 0:nq], hb_[0:kn0, 0:nkb * 32].rearrange("p (a b) -> p a b", a=nkb), [hbt_], [PTt])
            b2, bt2 = bank()
            rds = [PTt]
            for (_, _, rd) in vblocks:
                rds += rd
            MM(b2[0:nq, 0:KVR], [(PT[0:kn, j, 0:nq], vap) for j, (kn, vap, rd) in enumerate(vblocks)], rds, bt2)
            STT(acc[0:nq, :], acc[0:nq, :], CR, b2[0:nq, 0:KVR], ALU.mult, ALU.add, [acct, smt, bt2], [acct])

        gi = 0
        for bi in range(NB):
            p.dma("sp", idxr[:, :], ptrep[bi], writes=[idxt])
            CP("dve", idxf[:, :], idxr[:, :], [idxt], [idxt])
            TS("dve", idxf[:, :], idxf[:, :], 16.0, cmod[:, 0:1], ALU.mult, ALU.add, [idxt, ct], [idxt])
            CP("dve", idxi[:, :], idxf[:, :], [idxt], [idxt])
            for rc in range(2):
                CP("dve", QB[:, rc, :].rearrange("p (h t) -> p h t", t=DEC_SEQ), QLs[:, :, rc, bi * DEC_SEQ:(bi + 1) * DEC_SEQ], [QLst], [QBt])
            CP("dve", QPB[:, :].rearrange("p (h t) -> p h t", t=DEC_SEQ), QPEs[:, :, bi * DEC_SEQ:(bi + 1) * DEC_SEQ], [QLst], [QBt])
            MEMSET("pool", sm_[0:nq, 0:1], NEG, [smt])
            MEMSET("pool", sm_[0:nq, 1:2], 0.0, [smt])
            MEMSET("pool", acc[0:nq, :], 0.0, [acct])
            for g in range(NGRP):
                si = gi % 2
                gi += 1
                p.dma("pool", None, None, reads=[idxt], writes=[stt_[si]],
                      fn=lambda e, si=si, g=g: e.indirect_dma_start(
                          out=stL[si].rearrange("p a b -> p (a b)"), out_offset=None, in_=c_lat,
                          in_offset=bass.IndirectOffsetOnAxis(ap=idxi[:, g:g + 1], axis=0)))
                p.dma("pool", None, None, reads=[idxt], writes=[stt_[si]],
                      fn=lambda e, si=si, g=g: e.indirect_dma_start(
                          out=stK[si].rearrange("p a b -> p (a b)"), out_offset=None, in_=c_kr,
                          in_offset=bass.IndirectOffsetOnAxis(ap=idxi[:, g:g + 1], axis=0)))
                CP("pool", stLb[:, 0:4, :], stL[si][:, 0:4, :], [stt_[si]], [stbt])
                CP("act", stLb[:, 4:8, :], stL[si][:, 4:8, :], [stt_[si]], [stbt])
                CP("dve", stKb[:, :, :], stK[si][:, :, :], [stt_[si]], [stbt])
                for rc in range(2):
                    hb_, hbt_ = bbank()
                    for j in range(8):
                        TR(hb_[:, j * 128:(j + 1) * 128], stLb[:, j, rc * 128:(rc + 1) * 128], identb[:, :], [stbt, ct], hbt_)
                    CP("act" if rc else "dve", LTs[:, rc, :], hb_[:, :], [hbt_], [LTst])
                hb_, hbt_ = bbank()
                for j in range(8):
                    TR(hb_[0:64, j * 128:(j + 1) * 128], stKb[:, j, :], identb[:, :], [stbt, ct], hbt_)
                CP("dve", KTs[:, :], hb_[0:64, :], [hbt_], [LTst])
                for s in range(2):
                    b, bt = bank()
                    k0 = s * 512
                    MM(b[0:nq, 0:512], [(QB[:, 0, :], LTs[:, 0, k0:k0 + 512]), (QB[:, 1, :], LTs[:, 1, k0:k0 + 512]),
                                        (QPB[:, :], KTs[:, k0:k0 + 512])], [QBt, LTst], bt)
                    softmax_block(b, bt, 512, [(128, stLb[:, s * 4 + j, :], [stbt]) for j in range(4)])
            b, bt = bank()
            MM(b[0:nq, 0:NS], [(QB[:, 0, :], LATTs[:, 0, 0:NS]), (QB[:, 1, :], LATTs[:, 1, 0:NS]), (QPB[:, :], KRTs[:, 0:NS])],
               [QBt, KVst], bt)
            TT("dve", b[0:nq, 0:NS], b[0:nq, 0:NS], smask_s[:, bi, :], ALU.add, [bt, ct], [bt])
            softmax_block(b, bt, NS, [(NS, LATTOKs[0:NS, 0, :], [KVst])])
            RECIP(RL, L_, [smt], [smt])
            TS("dve", olb[0:nq, 0:KVR], acc[0:nq, :], RL, None, ALU.mult, None, [acct, smt], [olbt])
            hb_, hbt_ = bbank()
            for rc in range(2):
                TR(hb_[:, rc * 32:rc * 32 + nq], olb[0:nq, rc * 128:(rc + 1) * 128], identb[0:nq, 0:nq], [olbt, ct], hbt_)
            for rc in range(2):
                CP("dve", OLTs[:, :, rc, bi * DEC_SEQ:(bi + 1) * DEC_SEQ], hb_[:, rc * 32:rc * 32 + nq].rearrange("p (h t) -> p h t", t=DEC_SEQ),
                   [hbt_], [OLTst])
        wuv, wuvt = load_wuv()
        for h in range(MH):
            b, bt = bank()
            MM(b[:, :N], [(wuv[:, rc, h * 128:(h + 1) * 128], OLTs[:, h, rc, :N]) for rc in range(2)], [OLTst, wuvt], bt)
            CP("act", gbf(GI_OH[h])[:, :N], b[:, :N], [bt], [Gt[GI_OH[h]]])

    if cfg.get("DO_SAMPLE", True) and "S" not in os.environ.get("MK_SKIP", ""):
        N = NS
        load_x_block([(xs[:, :], 0, NS)], NS, 0)
        p.dma("sp", stg[0:NB, :], sshift, writes=[stgt])
        b, bt = bank()
        for kc in range(8):
            TR(b[:, kc * NB:(kc + 1) * NB], stg[0:NB, kc * 128:(kc + 1) * 128], ident[0:NB, 0:NB], [stgt, ct], bt)
        CP("dve", shT[:, :, :], b[:, 0:8 * NB].rearrange("p (a b) -> p a b", a=8), [bt], [shTt])
        chunks = [(bi * DEC_SEQ, DEC_SEQ) for bi in range(NB)]
        if "r" in SPH:
            rwkv_layer(N, DEC_SEQ, chunks, "s")
        for g in range(2):
            b, bt = bank()
            for j in range(4):
                kc = g * 4 + j
                TR(b[0:NB, j * 128:(j + 1) * 128], shT[:, kc, :], ident[:, :], [shTt, ct], bt)
            CP("dve", stg[0:NB, g * 512:(g + 1) * 512], b[0:NB, :], [bt, stgt], [stgt])
        OUT(o_shifts, stg[0:NB, :], [stgt])
        if "f" in SPH:
            ffn_layer(N, 0)
        if "k" in SPH:
            kv_path(N, 0, C["rope_tok_s"], o_lats, o_krs, 0, LATTs, KRTs, LATTOKs, KVst)
        if "m" in SPH:
            mla_queries(N, 0, C["rope_fm_s"], sample_q_cb(N))
        if "a" in SPH:
            sample_attend_all(N)
            mla_out(N)
        if "g" in SPH:
            ffn_layer(N, 1)
        final_out(N, [[(o_ys[:, :], 0, NS)]])

    p.finish([out_trk])
    return nc


def _run(inputs, cfg, n_cores=8):
    f = lambda a: np.ascontiguousarray(np.asarray(a))
    NB = cfg["NB"]
    NPG = cfg["NPG"]
    nseq = inputs["x_prompt"].shape[0]
    consts = make_consts(cfg)
    shared = {}
    for nm in ("rw_wr", "rw_wk", "rw_wv", "rw_wo", "rw_w1", "rw_w2", "rw_a1", "rw_a2", "rw_g1", "rw_g2"):
        shared[nm] = f(inputs[nm][0])
    for li in range(2):
        shared["ffn_up%d" % li] = f(inputs["ffn_up"][li])
        shared["ffn_down%d" % li] = f(inputs["ffn_down"][li])
    shared["w_dkv"] = f(inputs["w_dkv"])
    shared["w_kr"] = f(inputs["w_kr"])
    shared["w_uk"] = f(np.asarray(inputs["w_uk"]).reshape(KVR, MH * 128))
    shared["w_uv"] = f(np.asarray(inputs["w_uv"]).reshape(KVR, MH * 128))
    shared["w_dq"] = f(inputs["w_dq"][0])
    shared["w_uq"] = f(np.asarray(inputs["w_uq"][0]).reshape(QR, MH * 192))
    shared["w_o_mla"] = f(inputs["w_o_mla"][0])
    v128 = np.concatenate([
        np.asarray(inputs["norm_mix"][0]).reshape(8, 128), np.asarray(inputs["norm_ffn"][0]).reshape(8, 128),
        np.asarray(inputs["kv_norm"]).reshape(8, 128), np.asarray(inputs["norm_mix"][1]).reshape(8, 128),
        np.asarray(inputs["norm_ffn"][1]).reshape(8, 128), np.asarray(inputs["norm_final"]).reshape(8, 128),
        np.asarray(inputs["rw_mu"][0]).reshape(48, 128), np.asarray(inputs["q_norm"][0]).reshape(3, 128)], axis=0)
    shared["vec128"] = f(v128.astype(np.float32))
    v64 = np.concatenate([np.asarray(inputs[k][0]).reshape(16, 64) for k in
                          ("rw_w0", "rw_a0", "rw_kk", "rw_ka", "rw_rk", "rw_lnx_g", "rw_lnx_b")], axis=0)
    shared["vec64"] = f(v64.astype(np.float32))
    shared["latnorm"] = f(np.asarray(inputs["lat_norm"]).reshape(1, KVR))
    shared["meta"] = f(inputs["meta_tokens"])
    nphys = inputs["cache_latent"].shape[0]
    shared["c_lat"] = f(np.asarray(inputs["cache_latent"]).reshape(nphys * 16, 8 * KVR))
    shared["c_kr"] = f(np.asarray(inputs["cache_krope"]).reshape(nphys * 16, 8 * ROPE))
    for k, v in consts.items():
        shared["c_" + k] = f(v)
    pt = np.asarray(inputs["page_table"]).astype(np.int32)
    ngrp = NPG // 8
    in_maps = []
    for c in range(n_cores):
        m = dict(shared)
        m["xp"] = f(inputs["x_prompt"][c % nseq])
        bs = slice(c * NB, (c + 1) * NB)
        m["xs"] = f(np.asarray(inputs["x_sample"][bs]).reshape(NB * DEC_SEQ, D))
        m["swkv"] = f(inputs["state_wkv"][0][bs])
        m["sshift"] = f(inputs["state_shift"][0][bs])
        ptc = pt[bs].reshape(NB, ngrp, 8)
        rep = np.repeat(ptc.transpose(0, 2, 1)[:, :, None, :], 16, axis=2)
        m["ptrep"] = f(rep.reshape(NB, 128, ngrp).astype(np.int32))
        in_maps.append(m)
    nc = build(cfg)
    res = run_bass_kernel_spmd(nc, in_maps, core_ids=list(range(n_cores)))
    return res.results


def _assemble(results, cfg, nseq, n_cores=8):
    NB = cfg["NB"]
    r = results
    y_prompt = np.stack([r[b]["o_yp"] for b in range(nseq)], axis=0)
    y_sample = np.concatenate([r[c]["o_ys"].reshape(NB, DEC_SEQ, D) for c in range(n_cores)], axis=0)
    wkv_p = np.stack([r[b]["o_wkvp"] for b in range(nseq)], axis=0)[None]
    shift_p = np.stack([r[b]["o_shiftp"].reshape(D) for b in range(nseq)], axis=0)[None]
    lat_p = np.stack([r[b]["o_latp"] for b in range(nseq)], axis=0)
    kr_p = np.stack([r[b]["o_krp"] for b in range(nseq)], axis=0)
    wkv_s = np.concatenate([r[c]["o_wkvs"] for c in range(n_cores)], axis=0)[None]
    shift_s = np.concatenate([r[c]["o_shifts"] for c in range(n_cores)], axis=0)[None]
    lat_s = np.concatenate([r[c]["o_lats"].reshape(NB, DEC_SEQ, KVR) for c in range(n_cores)], axis=0)
    kr_s = np.concatenate([r[c]["o_krs"].reshape(NB, DEC_SEQ, ROPE) for c in range(n_cores)], axis=0)
    outs = (y_prompt, y_sample, wkv_p, shift_p, lat_p, kr_p, wkv_s, shift_s, lat_s, kr_s)
    return tuple(np.ascontiguousarray(o.astype(np.float32)) for o in outs)


def kernel(**inputs):
    seq = inputs["x_prompt"].shape[1]
    nseq = inputs["x_prompt"].shape[0]
    db = inputs["x_sample"].shape[0]
    npg = inputs["page_table"].shape[1]
    cfg = dict(SEQ=seq, T=seq + N_META, NB=db // 8, NPG=npg, NPHYS=inputs["cache_latent"].shape[0], PAST=npg * 128)
    results = _run(inputs, cfg)
    return _assemble(results, cfg, nseq)
```

```python
import math
import os
import numpy as np
from contextlib import ExitStack
import concourse.bass as bass
import concourse.mybir as mybir
from concourse.bass_utils import run_bass_kernel_spmd

F32 = mybir.dt.float32
F32R = mybir.dt.float32r
BF16 = mybir.dt.bfloat16
I32 = mybir.dt.int32
AF = mybir.ActivationFunctionType
ALU = mybir.AluOpType
AX = mybir.AxisListType

D = 1024
NH = 16
HD = 64
MH = 8
KVR = 256
QR = 384
ROPE = 64
DFF = 4096
N_META = 16
GN_EPS = 64e-5
NORM_EPS = 1e-6
ATTN_SCALE = (128 + 64) ** -0.5
DEC_SEQ = 4
NEG = -30000.0


class Trk:
    __slots__ = ("lastw", "readers", "excl")

    def __init__(self, excl=False):
        self.lastw = None
        self.readers = []
        self.excl = excl


class Prog:
    ENGS = ("pe", "act", "dve", "pool", "sp")
    NDMA = 8

    def __init__(self, nc):
        self.nc = nc
        self.es = ExitStack()
        self.q = {e: [] for e in self.ENGS}
        self.cnt = {}
        self.sems = {}
        self.waited = {e: {} for e in self.ENGS}
        for e in self.ENGS:
            self.sems[e] = self.es.enter_context(nc.semaphore("s_" + e))
            self.cnt[e] = 0
        self.dma_i = {e: 0 for e in self.ENGS}
        for e in ("sp", "act", "pool"):
            for i in range(self.NDMA):
                k = "d_%s_%d" % (e, i)
                self.sems[k] = self.es.enter_context(nc.semaphore(k))
                self.cnt[k] = 0
        self.nops = 0

    def sb(self, name, shape, dtype=F32):
        return self.es.enter_context(self.nc.sbuf_tensor(name, list(shape), dtype))

    def ps(self, name, shape, dtype=F32):
        return self.es.enter_context(self.nc.psum_tensor(name, list(shape), dtype))

    def _deps(self, eng, reads, writes):
        deps = {}

        def add(d):
            if d is None:
                return
            k, v = d
            if eng == "pe" and k == "pe":
                return
            if deps.get(k, 0) < v:
                deps[k] = v
        for t in reads:
            add(t.lastw)
        for t in writes:
            add(t.lastw)
            for r in t.readers:
                add(r)
        out = []
        w = self.waited[eng]
        for k, v in deps.items():
            if w.get(k, 0) < v:
                w[k] = v
                out.append((k, v))
        return out

    def _mark(self, reads, writes, tok):
        for t in reads:
            t.readers.append(tok)
            if len(t.readers) > 64:
                t.readers = t.readers[-64:] if False else t.readers
        for t in writes:
            t.lastw = tok
            t.readers = []

    @staticmethod
    def _split(reads, writes):
        ex = [t for t in reads if t.excl]
        if ex:
            reads = [t for t in reads if not t.excl]
            writes = list(writes) + ex
        return reads, writes

    def op(self, eng, fn, reads=(), writes=()):
        reads, writes = self._split(reads, writes)
        waits = self._deps(eng, reads, writes)
        self.cnt[eng] += 1
        tok = (eng, self.cnt[eng])
        sems = self.sems
        semh = sems[eng]

        def run(e):
            for k, v in waits:
                e.wait_ge(sems[k], v)
            fn(e).then_inc(semh, 1)
        self.q[eng].append(run)
        self._mark(reads, writes, tok)
        self.nops += 1
        return tok

    def dma(self, eng, out, in_, reads=(), writes=(), fn=None):
        i = self.dma_i[eng]
        self.dma_i[eng] += 1
        k = "d_%s_%d" % (eng, i % self.NDMA)
        reads, writes = self._split(reads, writes)
        waits = self._deps(eng, reads, writes)
        prev = self.cnt[k]
        if prev and self.waited[eng].get(k, 0) < prev:
            self.waited[eng][k] = prev
            waits.append((k, prev))
        self.cnt[k] += 16
        tok = (k, self.cnt[k])
        sems = self.sems
        semh = sems[k]

        def run(e):
            for kk, v in waits:
                e.wait_ge(sems[kk], v)
            if fn is None:
                e.dma_start(out=out, in_=in_).then_inc(semh, 16)
            else:
                fn(e).then_inc(semh, 16)
        self.q[eng].append(run)
        self._mark(reads, writes, tok)
        self.nops += 1
        return tok

    def barrier(self):
        snap = [(k, v) for k, v in self.cnt.items() if v > 0]
        sems = self.sems
        for eng in self.ENGS:
            waits = []
            for k, v in snap:
                if eng == "pe" and k == "pe":
                    continue
                if self.waited[eng].get(k, 0) < v:
                    self.waited[eng][k] = v
                    waits.append((k, v))

            def run(e, waits=waits):
                for k, v in waits:
                    e.wait_ge(sems[k], v)
            self.q[eng].append(run)

    def finish(self, final_trks):
        waits = self._deps("sp", final_trks, final_trks)
        sems = self.sems

        def run(e):
            for k, v in waits:
                e.wait_ge(sems[k], v)
        self.q["sp"].append(run)
        nc = self.nc
        q = self.q
        with nc.allow_low_precision("bf16 matmul operands, fp32 accumulation"), nc.Block() as block:
            @block.tensor
            def _(e):
                for f in q["pe"]:
                    f(e)

            @block.scalar
            def _(e):
                for f in q["act"]:
                    f(e)

            @block.vector
            def _(e):
                for f in q["dve"]:
                    f(e)

            @block.gpsimd
            def _(e):
                for f in q["pool"]:
                    f(e)

            @block.sync
            def _(e):
                for f in q["sp"]:
                    f(e)
        self.es.close()


def _levels(C):
    return max(1, int(math.ceil(math.log2(C))))


def make_consts(cfg):
    T = cfg["T"]
    NB = cfg["NB"]
    past = cfg["PAST"]
    c = {}
    c["ident"] = np.eye(128, dtype=np.float32)
    rot = np.zeros((64, 64), np.float32)
    for m in range(32):
        rot[m + 32, m] = -1.0
        rot[m, m + 32] = 1.0
    c["rot"] = rot
    half = 32
    inv_freq = (10000.0 ** (-np.arange(half, dtype=np.float32) / half)).astype(np.float32)

    def tables(pos):
        ang = pos.astype(np.float32)[:, None] * inv_freq[None, :]
        return np.cos(ang).astype(np.float32), np.sin(ang).astype(np.float32)
    cp, sp_ = tables(np.arange(T))
    cs, ss = tables(past + (np.arange(NB * DEC_SEQ) % DEC_SEQ))
    c["rope_tok_p"] = np.concatenate([cp, sp_], axis=1)
    c["rope_tok_s"] = np.concatenate([cs, ss], axis=1)
    c["rope_fm_p"] = np.concatenate([cp.T, cp.T, sp_.T, sp_.T], axis=0).astype(np.float32)
    c["rope_fm_s"] = np.concatenate([cs.T, cs.T, ss.T, ss.T], axis=0).astype(np.float32)
    for C in (64, 16, 4):
        i = np.arange(C)[:, None]
        t = np.arange(C)[None, :]
        su = (i < t).astype(np.float32)
        sl = (i > t).astype(np.float32)
        iu = (i <= t).astype(np.float32)
        c["smask%d" % C] = np.concatenate([su, sl, su, iu, iu], axis=1)
    qi = np.arange(128)[:, None]
    ki = np.arange(128)[None, :]
    c["cmask"] = np.where(ki <= qi, 0.0, NEG).astype(np.float32)
    sm = np.full((32, NB, NB * DEC_SEQ), NEG, np.float32)
    for b in range(NB):
        for h in range(MH):
            for t in range(DEC_SEQ):
                for t2 in range(t + 1):
                    sm[h * DEC_SEQ + t, b, b * DEC_SEQ + t2] = 0.0
    c["smask_s"] = sm
    c["cmod"] = (np.arange(128) % 16).astype(np.float32).reshape(128, 1)
    return c


def build(cfg):
    SEQ = cfg["SEQ"]
    T = cfg["T"]
    NB = cfg["NB"]
    NPG = cfg["NPG"]
    NPHYS = cfg["NPHYS"]
    NS = NB * DEC_SEQ
    NT = 256
    NKB = (T + 127) // 128
    NGRP = NPG // 8

    nc = bass.Bass("TRN2", target_bir_lowering=False)

    def din(name, shape, dt=F32):
        return nc.dram_tensor(name, list(shape), dt, kind="ExternalInput").ap()

    def dout(name, shape, dt=F32):
        return nc.dram_tensor(name, list(shape), dt, kind="ExternalOutput").ap()

    xp = din("xp", [SEQ, D])
    meta = din("meta", [N_META, D])
    xs = din("xs", [NS, D])
    swkv = din("swkv", [NB, NH, HD, HD])
    sshift = din("sshift", [NB, D])
    c_lat = din("c_lat", [NPHYS * 16, 8 * KVR])
    c_kr = din("c_kr", [NPHYS * 16, 8 * ROPE])
    ptrep = din("ptrep", [NB, 128, NGRP], I32)
    vec128 = din("vec128", [99, 128])
    vec64 = din("vec64", [112, 64])
    latnorm = din("latnorm", [1, KVR])
    W = {}
    for nm, shp in [("rw_wr", [D, D]), ("rw_wk", [D, D]), ("rw_wv", [D, D]), ("rw_wo", [D, D]),
                    ("rw_w1", [D, 64]), ("rw_w2", [64, D]), ("rw_a1", [D, 64]), ("rw_a2", [64, D]),
                    ("rw_g1", [D, 128]), ("rw_g2", [128, D]),
                    ("ffn_up0", [D, DFF]), ("ffn_down0", [DFF, D]), ("ffn_up1", [D, DFF]), ("ffn_down1", [DFF, D]),
                    ("w_dkv", [D, KVR]), ("w_kr", [D, ROPE]), ("w_uk", [KVR, MH * 128]), ("w_uv", [KVR, MH * 128]),
                    ("w_dq", [D, QR]), ("w_uq", [QR, MH * 192]), ("w_o_mla", [D, D])]:
        W[nm] = din(nm, shp)
    C = {}
    for nm, shp in [("ident", [128, 128]), ("rot", [64, 64]), ("rope_tok_p", [T, 64]), ("rope_tok_s", [NS, 64]),
                    ("rope_fm_p", [128, T]), ("rope_fm_s", [128, NS]), ("smask64", [64, 320]), ("smask16", [16, 80]),
                    ("smask4", [4, 20]), ("cmask", [128, 128]), ("smask_s", [32, NB, NS]), ("cmod", [128, 1])]:
        C[nm] = din("c_" + nm, shp)
    o_yp = dout("o_yp", [SEQ, D])
    o_ys = dout("o_ys", [NS, D])
    o_wkvp = dout("o_wkvp", [NH, HD, HD])
    o_shiftp = dout("o_shiftp", [8, 128])
    o_latp = dout("o_latp", [T, KVR])
    o_krp = dout("o_krp", [T, ROPE])
    o_wkvs = dout("o_wkvs", [NB, NH, HD, HD])
    o_shifts = dout("o_shifts", [NB, D])
    o_lats = dout("o_lats", [NS, KVR])
    o_krs = dout("o_krs", [NS, ROPE])

    p = Prog(nc)
    out_trk = Trk()

    NFB = 5
    pbank = [p.ps("pb%d" % i, [128, 512], F32) for i in range(NFB)]
    pbt = [Trk(True) for _ in range(NFB)]
    pbi = [0]
    obank = p.ps("obank", [128, 512], F32)
    obankt = Trk(True)
    hbank = [p.ps("hb%d" % i, [128, 1024], BF16) for i in range(2)]
    hbt = [Trk(True) for _ in range(2)]
    hbi = [0]

    def bank():
        i = pbi[0] % NFB
        pbi[0] += 1
        return pbank[i], pbt[i]

    def bbank():
        i = hbi[0] % 2
        hbi[0] += 1
        return hbank[i], hbt[i]

    def MM(out, pairs, reads, wtrk, start=True, stop=True):
        n = len(pairs)

        def f(e):
            ins = None
            for i, (l, r) in enumerate(pairs):
                ins = e.matmul(out, l, r, start=(start and i == 0), stop=(stop and i == n - 1))
            return ins
        p.op("pe", f, reads=reads, writes=[wtrk])

    def TR(out, in_, ident_ap, reads, wtrk):
        p.op("pe", lambda e: e.transpose(out, in_, ident_ap), reads=reads, writes=[wtrk])

    def ACT(out, in_, func, reads, writes, bias=None, scale=None, accum=None):
        kw = {}
        if bias is not None:
            kw["bias"] = bias
        if scale is not None:
            kw["scale"] = scale
        if accum is not None:
            kw["accum_out"] = accum
        p.op("act", lambda e: e.activation(out, in_, func, **kw), reads=reads, writes=writes)

    def TS(eng, out, in0, s1, s2, op0, op1, reads, writes):
        if s2 is None:
            p.op(eng, lambda e: e.tensor_scalar(out, in0, s1, None, op0), reads=reads, writes=writes)
        else:
            p.op(eng, lambda e: e.tensor_scalar(out, in0, s1, s2, op0, op1), reads=reads, writes=writes)

    def TT(eng, out, in0, in1, op, reads, writes):
        p.op(eng, lambda e: e.tensor_tensor(out, in0, in1, op), reads=reads, writes=writes)

    def STT(out, in0, scalar, in1, op0, op1, reads, writes):
        p.op("dve", lambda e: e.scalar_tensor_tensor(out, in0, scalar, in1, op0, op1), reads=reads, writes=writes)

    def CP(eng, out, in_, reads, writes):
        if eng == "act":
            p.op("act", lambda e: e.copy(out, in_), reads=reads, writes=writes)
        else:
            p.op(eng, lambda e: e.tensor_copy(out, in_), reads=reads, writes=writes)

    def RECIP(out, in_, reads, writes):
        p.op("dve", lambda e: e.reciprocal(out, in_), reads=reads, writes=writes)

    def MEMSET(eng, ap, val, writes):
        p.op(eng, lambda e: e.memset(ap, val), writes=writes)

    def OUT(dst, src, reads):
        p.dma("sp", dst, src, reads=reads, writes=[out_trk])

    ct = Trk()
    ident = p.sb("ident", [128, 128], F32)
    identb = p.sb("identb", [128, 128], BF16)
    identr = p.sb("identr", [64, 64], F32R)
    ones_r = p.sb("ones_r", [128, 128], F32R)
    onesf = p.sb("onesf", [128, 64], F32)
    rotm = p.sb("rotm", [64, 64], BF16)
    cmask = p.sb("cmask", [128, 128], F32)
    smk = {Cc: p.sb("smask%d" % Cc, [Cc, 5 * Cc], F32) for Cc in (64, 16, 4)}
    smask_s = p.sb("smask_s", [32, NB, NS], F32)
    cmod = p.sb("cmod", [128, 1], F32)
    lnbc = p.sb("lnbc", [128, KVR], F32)
    v128 = p.sb("v128", [128, 99], F32)
    v64 = p.sb("v64", [64, 112], F32)
    p.dma("sp", ident[:], C["ident"], writes=[ct])
    p.dma("pool", rotm[:], C["rot"], writes=[ct])
    p.dma("sp", cmask[:], C["cmask"], writes=[ct])
    for Cc in (64, 16, 4):
        p.dma("sp", smk[Cc][:], C["smask%d" % Cc], writes=[ct])
    p.dma("sp", smask_s[:], C["smask_s"], writes=[ct])
    p.dma("sp", cmod[:], C["cmod"], writes=[ct])
    p.dma("sp", lnbc[:], latnorm.partition_broadcast(128), writes=[ct])
    CP("dve", identb[:], ident[:], [ct], [ct])
    CP("dve", identr[:], ident[0:64, 0:64], [ct], [ct])
    onesb = p.sb("onesb", [128, 128], F32)
    MEMSET("pool", onesb[:], 1.0, [ct])
    CP("dve", ones_r[:], onesb[:], [ct], [ct])
    MEMSET("pool", onesf[:], 1.0, [ct])
    stg = p.sb("stg", [128, 1024], F32)
    stgt = Trk()
    p.dma("sp", stg[0:99, 0:128], vec128, writes=[stgt])
    p.dma("sp", stg[0:112, 128:192], vec64, writes=[stgt])
    b_, bt_ = bank()
    TR(b_[:, 0:99], stg[0:99, 0:128], ident[0:99, 0:99], [stgt, ct], bt_)
    TR(b_[0:64, 128:240], stg[0:112, 128:192], ident[0:112, 0:112], [stgt, ct], bt_)
    CP("dve", v128[:], b_[:, 0:99], [bt_], [ct])
    CP("dve", v64[:], b_[0:64, 128:240], [bt_], [ct])
    nw0 = p.sb("nw0", [64, 32], F32)
    TS("dve", nw0[:], v64[:, 0:32], -1.0, None, ALU.mult, None, [ct], [ct])

    wres_t = Trk()
    w1s = p.sb("w1s", [128, 8, 64], BF16)
    a1s = p.sb("a1s", [128, 8, 64], BF16)
    g1s = p.sb("g1s", [128, 8, 128], BF16)
    w2s = p.sb("w2s", [64, D], BF16)
    a2s = p.sb("a2s", [64, D], BF16)
    g2s = p.sb("g2s", [128, D], BF16)
    wukT = p.sb("wukT", [128, MH, KVR], BF16)

    def kcv(ap):
        return ap.rearrange("(kc p) m -> p kc m", p=128)
    p.dma("pool", w1s[:], kcv(W["rw_w1"]), writes=[wres_t])
    p.dma("pool", a1s[:], kcv(W["rw_a1"]), writes=[wres_t])
    p.dma("pool", g1s[:], kcv(W["rw_g1"]), writes=[wres_t])
    p.dma("pool", w2s[:], W["rw_w2"], writes=[wres_t])
    p.dma("pool", a2s[:], W["rw_a2"], writes=[wres_t])
    p.dma("pool", g2s[:], W["rw_g2"], writes=[wres_t])

    NSLAB = 4
    slabs = [p.sb("slab%d" % i, [128, 4096], BF16) for i in range(NSLAB)]
    slabt = [Trk() for _ in range(NSLAB)]
    slabi = [0]

    def load_slab(parts):
        i = slabi[0] % NSLAB
        slabi[0] += 1
        for (vf, src) in parts:
            p.dma("pool", vf(slabs[i]), src, writes=[slabt[i]])
        return slabs[i], slabt[i]

    def v_k8(s):
        return s[:].rearrange("p (a b) -> p a b", a=8)

    def v_k4(s):
        return s[:].rearrange("p (a b) -> p a b", a=4)

    def v_h16(s):
        return s[0:64, :].rearrange("p (a b) -> p a b", a=16)

    sl_, slt_ = load_slab([(lambda s: s[:, 0:2048].rearrange("p (a b) -> p a b", a=2), kcv(W["w_uk"]))])
    wuk_nat = sl_[:, 0:2048].rearrange("p (a b) -> p a b", a=2)
    for h in range(MH):
        hb_, hbt_ = bbank()
        for rc in range(2):
            TR(hb_[:, rc * 128:(rc + 1) * 128], wuk_nat[:, rc, h * 128:(h + 1) * 128], identb[:], [slt_, ct], hbt_)
        CP("dve", wukT[:, h, :], hb_[:, 0:256], [hbt_], [wres_t])

    NMAX = NT
    X = p.sb("X", [128, 8, NMAX], F32)
    Xt = [Trk() for _ in range(8)]
    XNX = p.sb("XNX", [128, 16, NMAX], BF16)
    XNb = XNX[:, 0:8, :]
    XXb = XNX[:, 8:16, :]
    XNt = Trk()
    XXt = XNt
    XM = [p.sb("XM%d" % i, [128, 8, NMAX], BF16) for i in range(4)]
    XMt = [Trk() for _ in range(4)]
    OGs = XNX[0:64, :, :]
    OGt = XNt
    carry = p.sb("carry", [128, 8], F32)
    carryt = Trk()
    rstd = p.sb("rstd", [128, NMAX], F32)
    rstdt = Trk()
    sq = [p.sb("sq%d" % i, [128, NMAX], BF16) for i in range(2)]
    ones_b = p.sb("ones_b", [128, 128], BF16)
    CP("dve", ones_b[:], onesb[:], [ct], [ct])
    sqt = [Trk() for _ in range(2)]
    sqi = [0]
    HW = p.sb("HW", [64, NMAX], BF16)
    HA = p.sb("HA", [64, NMAX], BF16)
    HG = p.sb("HG", [128, NMAX], BF16)
    Hlt = Trk()
    tmpA = p.sb("tmpA", [128, NMAX], F32)
    tmpAt = Trk()
    NG = 20
    Gp = [p.sb("G%d" % i, [128, 256], F32) for i in range(NG)]
    Gt = [Trk() for _ in range(NG)]

    def g32(i, parts=128, n=None):
        return Gp[i][0:parts, 0:(n if n is not None else 256)]

    def gbf(i, parts=128):
        return Gp[i][0:parts, :].bitcast(BF16)

    GR = [p.sb("GR%d" % i, [64, 256], F32R) for i in range(5)]
    GRt = [Trk() for _ in range(5)]

    NCH = 4
    TM = p.sb("TM", [64, NCH, 192], F32R)
    TMt = [Trk() for _ in range(NCH)]
    MMs = p.sb("MMs", [64, NCH, 320], F32R)
    MMt = [Trk() for _ in range(NCH)]
    NN = [p.sb("NN%d" % i, [64, NCH, 128], F32R) for i in range(2)]
    NNt = [Trk() for _ in range(2)]
    TT_ = [p.sb("TT%d" % i, [64, NCH, 64], F32R) for i in range(2)]
    TTt = [Trk() for _ in range(2)]
    WT = p.sb("WT", [64, 64], F32R)
    WTt = Trk()
    UT = p.sb("UT", [64, 64], F32R)
    UTt = Trk()
    SNAT = stg[0:64, :].rearrange("p (a b) -> p a b", a=16)
    SNATt = stgt
    kvcol = p.sb("kvcol", [128, 8], F32)
    kvcolt = Trk()
    ropet = p.sb("ropet", [128, 64], F32)
    ropett = Trk()
    ropef = p.sb("ropef", [64, NMAX], F32)
    rope_s2 = p.sb("rope_s2", [64, NMAX], F32)
    ropeft = Trk()
    QPr = p.sb("QPr", [64, NMAX], BF16)
    QPf = p.sb("QPf", [64, NMAX], F32)
    QPt = Trk()
    QLh = p.sb("QLh", [128, 2, NMAX], BF16)
    QPEh = p.sb("QPEh", [64, NMAX], BF16)
    QLt = Trk()
    OLT = p.sb("OLT", [128, 2, NMAX], BF16)
    OLTt = Trk()
    sm2 = p.sb("sm2", [128, 16], F32)
    sm2t = Trk()
    sm3 = p.sb("sm3", [128, 16], F32)
    sm3t = Trk()
    sm_ = p.sb("sm_", [128, 16], F32)
    smt = Trk()
    idxf = p.sb("idxf", [128, NGRP], F32)
    idxi = p.sb("idxi", [128, NGRP], I32)
    idxr = p.sb("idxr", [128, NGRP], I32)
    idxt = Trk()
    QB = p.sb("QB", [128, 2, 32], BF16)
    QPB = p.sb("QPB", [64, 32], BF16)
    QBt = Trk()
    shT = p.sb("shT", [128, 8, NB], F32)
    shTt = Trk()
    LATTs = p.sb("LATTs", [128, 2, 128], BF16)
    KRTs = p.sb("KRTs", [64, 128], BF16)
    LATTOKs = p.sb("LATTOKs", [128, 1, KVR], BF16)
    KVst = Trk()

    AW = max(NKB * 320 + 2048, 6 * 2048 + NB * 128) + 64
    arena = p.sb("arena", [128, AW], F32)
    _off = [0]

    def carve(nwords, dtype=F32, parts=128):
        a = arena[0:parts, _off[0]:_off[0] + nwords]
        _off[0] += nwords
        if dtype is not F32:
            a = a.bitcast(dtype)
        return a
    _off[0] = 0
    LATT = carve(NKB * 128, BF16).rearrange("p (a b) -> p a b", a=2)
    KRT = carve(NKB * 64, BF16, 64)
    LATTOK = carve(NKB * 128, BF16).rearrange("p (a b) -> p a b", a=NKB)
    S32 = carve(NH * 64, F32, 64).rearrange("p (a b) -> p a b", a=NH)
    SR = p.sb("SR", [64, max(NH, NB), 64], F32R)
    KVt = Trk()
    S32t = [Trk() for _ in range(NH)]
    SRt = [Trk() for _ in range(NH)]
    _off[0] = 0
    stL = [carve(8 * KVR).rearrange("p (a b) -> p a b", a=8) for _ in range(2)]
    stK = [carve(8 * ROPE).rearrange("p (a b) -> p a b", a=8) for _ in range(2)]
    stt_ = [Trk() for _ in range(2)]
    stLb = carve(8 * KVR // 2, BF16).rearrange("p (a b) -> p a b", a=8)
    stKb = carve(8 * ROPE // 2, BF16).rearrange("p (a b) -> p a b", a=8)
    stbt = Trk()
    LTs = carve(1024, BF16).rearrange("p (a b) -> p a b", a=2)
    KTs = carve(512, BF16, 64)
    LTst = Trk()
    SS32 = carve(NB * 64, F32, 64).rearrange("p (a b) -> p a b", a=NB)
    SSR = SR
    SSt = [Trk() for _ in range(NB)]
    SSRt = [Trk() for _ in range(NB)]
    QLs = carve(MH * 2 * NS // 2, BF16).rearrange("p (h r n) -> p h r n", h=MH, r=2)
    OLTs = carve(MH * 2 * NS // 2, BF16).rearrange("p (h r n) -> p h r n", h=MH, r=2)
    QPEs = carve(MH * NS // 2, BF16, 64).rearrange("p (h n) -> p h n", h=MH)
    QLst = Trk()
    OLTst = Trk()

    epsc = p.sb("epsc", [128, 4], F32)
    MEMSET("pool", epsc[:, 0:1], NORM_EPS, [ct])
    MEMSET("pool", epsc[:, 1:2], GN_EPS, [ct])

    def load_x_block(rows_src, nt, col0):
        for (src, r0, n) in rows_src:
            p.dma("sp", stg[r0:r0 + n, :], src, writes=[stgt])
        for g in range(2):
            b, bt = bank()
            for j in range(4):
                kc = g * 4 + j
                TR(b[:, j * 128:j * 128 + nt], stg[0:nt, kc * 128:(kc + 1) * 128], ident[0:nt, 0:nt], [stgt, ct], bt)
            for j in range(4):
                kc = g * 4 + j
                CP("act" if j % 2 else "dve", X[:, kc, col0:col0 + nt], b[:, j * 128:j * 128 + nt], [bt], [Xt[kc]])

    def rms_rstd(N, srcs, nfeat):
        b, bt = bank()
        n = len(srcs)
        for i_, (ap, t) in enumerate(srcs):
            i = sqi[0] % 2
            sqi[0] += 1
            ACT(sq[i][:, :N], ap, AF.Square, [t], [sqt[i]])
            MM(b[:, :N], [(ones_b[:, :], sq[i][:, :N])], [sqt[i], ct], bt, start=(i_ == 0), stop=(i_ == n - 1))
        ACT(rstd[:, :N], b[:, :N], AF.Ln, [bt, ct], [rstdt], scale=1.0 / nfeat, bias=epsc[:, 0:1])
        ACT(rstd[:, :N], rstd[:, :N], AF.Exp, [rstdt], [rstdt], scale=-0.5)

    def norm_to_bf16(N, gcol0, dst, dstt):
        rms_rstd(N, [(X[:, kc, :N], Xt[kc]) for kc in range(8)], D)
        for kc in range(8):
            STT(dst[:, kc, :N], X[:, kc, :N], v128[:, gcol0 + kc:gcol0 + kc + 1], rstd[:, :N], ALU.mult, ALU.mult,
                [Xt[kc], rstdt, ct], [dstt])

    RI = dict(r=0, k=1, v=2, e1=3, L=4, Lex=5, a=6, kkraw=7, k2=8, P=9, Pex=10, Pinv=11, kka=12, G=13, t1=14, gsb=15, xnf=16)
    RI.update(logd=RI["e1"], kk=RI["kkraw"], bonus=RI["e1"], Bh=RI["Pinv"], Kh=RI["a"], cen=RI["Lex"], y=RI["Pex"])
    RR = dict(At=0, Bt=1, Kt=2, Rt=3, kksq=4)
    RR.update(rkk=RR["kksq"], osb=RR["At"], censq=RR["Bt"])

    def rwkv_layer(N, C_, chunks, mode):
        msk = smk[C_]
        rms_rstd(N, [(X[:, kc, :N], Xt[kc]) for kc in range(8)], D)
        xnf, xnft = g32(RI["xnf"], 128, N), Gt[RI["xnf"]]
        for kc in range(8):
            STT(xnf, X[:, kc, :N], v128[:, kc:kc + 1], rstd[:, :N], ALU.mult, ALU.mult, [Xt[kc], rstdt, ct], [xnft])
            CP("act", XNb[:, kc, :N], xnf, [xnft], [XNt])
            if mode == "p":
                if N > 1:
                    TT("dve", XXb[:, kc, 1:N], xnf[:, 0:N - 1], xnf[:, 1:N], ALU.subtract, [xnft], [XXt])
                TT("dve", XXb[:, kc, 0:1], carry[:, kc:kc + 1], xnf[:, 0:1], ALU.subtract, [xnft, carryt], [XXt])
                CP("dve", carry[:, kc:kc + 1], xnf[:, N - 1:N], [xnft, carryt], [carryt])
            else:
                xn4 = xnf.rearrange("p (b t) -> p b t", t=DEC_SEQ)
                xx4 = XXb[:, kc, :N].rearrange("p (b t) -> p b t", t=DEC_SEQ)
                TT("dve", xx4[:, :, 1:DEC_SEQ], xn4[:, :, 0:DEC_SEQ - 1], xn4[:, :, 1:DEC_SEQ], ALU.subtract, [xnft], [XXt])
                TT("dve", xx4[:, :, 0:1], shT[:, kc, :].unsqueeze(2), xn4[:, :, 0:1], ALU.subtract, [xnft, shTt], [XXt])
                CP("dve", shT[:, kc, :].unsqueeze(2), xn4[:, :, DEC_SEQ - 1:DEC_SEQ], [xnft, shTt, XXt], [shTt])

        def make_xm(m, dst, dstt):
            for kc in range(8):
                STT(dst[:, kc, :N], XXb[:, kc, :N], v128[:, 48 + m * 8 + kc:48 + m * 8 + kc + 1], XNb[:, kc, :N],
                    ALU.mult, ALU.add, [XXt, XNt, ct], [dstt])
        for (m, wsb, M_) in ((1, w1s, 64), (4, a1s, 64), (5, g1s, 128)):
            make_xm(m, XM[3], XMt[3])
            b, bt = bank()
            MM(b[0:M_, :N], [(wsb[:, kc, :], XM[3][:, kc, :N]) for kc in range(8)], [XMt[3], wres_t], bt)
            if m == 1:
                ACT(tmpA[0:64, :N], b[0:64, :N], AF.Exp, [bt], [tmpAt], scale=2.0)
                TS("dve", tmpA[0:64, :N], tmpA[0:64, :N], 1.0, None, ALU.add, None, [tmpAt], [tmpAt])
                RECIP(tmpA[0:64, :N], tmpA[0:64, :N], [tmpAt], [tmpAt])
                TS("dve", HW[:, :N], tmpA[0:64, :N], -2.0, 1.0, ALU.mult, ALU.add, [tmpAt], [Hlt])
            elif m == 4:
                CP("act", HA[:, :N], b[0:64, :N], [bt], [Hlt])
            else:
                ACT(tmpA[:, :N], b[:, :N], AF.Exp, [bt], [tmpAt], scale=-1.0)
                TS("dve", tmpA[:, :N], tmpA[:, :N], 1.0, None, ALU.add, None, [tmpAt], [tmpAt])
                RECIP(HG[:, :N], tmpA[:, :N], [tmpAt], [Hlt])
        for i, m in enumerate((0, 2, 3)):
            make_xm(m, XM[i], XMt[i])

        def T_(n):
            return g32(RI[n], 64, N), Gt[RI[n]]

        def R_(n):
            return GR[RR[n]][:, 0:N], GRt[RR[n]]

        def Tc(n, c0, cs):
            return Gp[RI[n]][0:64, c0:c0 + cs]

        def Rc(n, c0, cs):
            return GR[RR[n]][:, c0:c0 + cs]
        wnames = ("rw_wr", "rw_wk", "rw_wv")
        for h in range(NH):
            if h % 8 == 0:
                g = h // 8
                sl = [load_slab([(v_k8, kcv(W[nm])[:, :, g * 512:(g + 1) * 512])]) for nm in wnames]
            if mode == "s":
                p.dma("sp", SNAT[:, 0:NB, :], swkv[:, h, :, :].rearrange("b v k -> v b k"), writes=[SNATt])
                for g0 in range(0, NB, 8):
                    b, bt = bank()
                    nb_ = min(8, NB - g0)
                    for j in range(nb_):
                        TR(b[0:64, j * 64:(j + 1) * 64], SNAT[:, g0 + j, :], ident[0:64, 0:64], [SNATt, ct], bt)
                    CP("act", SSR[:, g0:g0 + nb_, :], b[0:64, 0:nb_ * 64].rearrange("p (a b) -> p a b", a=nb_), [bt], [SSRt[g0 + j] for j in range(nb_)])
            hc = (h % 8) * 64
            r, rt = T_("r")
            k, kt = T_("k")
            v, vt = T_("v")
            for i, (dst, dt_) in enumerate(((r, rt), (k, kt), (v, vt))):
                b, bt = bank()
                sv = v_k8(sl[i][0])
                MM(b[0:64, :N], [(sv[:, kc, hc:hc + 64], XM[i][:, kc, :N]) for kc in range(8)], [XMt[i], sl[i][1]], bt)
                CP("act" if i == 1 else "dve", dst, b[0:64, :N], [bt], [dt_])
            zb, zbt = bank()
            if 3 * N <= 512:
                zps, aps, gps = zb[0:64, 0:N], zb[0:64, N:2 * N], zb[0:64, 2 * N:3 * N]
                gbt_ = zbt
            else:
                gb_, gbt_ = bank()
                zps, aps, gps = zb[0:64, 0:N], zb[0:64, N:2 * N], gb_[0:64, 0:N]
            MM(zps, [(w2s[:, h * 64:(h + 1) * 64], HW[:, :N])], [Hlt, wres_t], zbt)
            MM(aps, [(a2s[:, h * 64:(h + 1) * 64], HA[:, :N])], [Hlt, wres_t], zbt)
            MM(gps, [(g2s[:, h * 64:(h + 1) * 64], HG[:, :N])], [Hlt, wres_t], gbt_)
            e1, e1t = T_("e1")
            L, Lt = T_("L")
            Lex, Lext = T_("Lex")
            a, at = T_("a")
            kkraw, kkrawt = T_("kkraw")
            k2, k2t = T_("k2")
            P_, Pt_ = T_("P")
            Pex, Pext = T_("Pex")
            Pinv, Pinvt = T_("Pinv")
            kka, kkat = T_("kka")
            G, Gt_ = T_("G")
            t1, t1t = T_("t1")
            gsb, gsbt = T_("gsb")
            At, Att = R_("At")
            Bt, Btt = R_("Bt")
            Kt, Ktt = R_("Kt")
            Rt, Rtt = R_("Rt")
            kksq, kksqt = R_("kksq")
            logd, logdt = e1, e1t
            kk, kkt = kkraw, kkrawt
            ACT(e1, zps, AF.Exp, [zbt, ct], [e1t], scale=-1.0, bias=nw0[:, h:h + 1])
            TS("dve", e1, e1, 1.0, None, ALU.add, None, [e1t], [e1t])
            RECIP(e1, e1, [e1t], [e1t])
            ACT(logd, e1, AF.Copy, [e1t], [e1t], scale=-math.exp(-0.5))
            for (c0, cs) in chunks:
                p.op("dve", lambda e, c0=c0, cs=cs: e.tensor_tensor_scan(Tc("L", c0, cs), onesf[0:64, 0:cs], Tc("logd", c0, cs), 0.0, ALU.mult, ALU.add),
                     reads=[logdt, ct], writes=[Lt])
            TT("dve", Lex, L, logd, ALU.subtract, [Lt, logdt], [Lext])
            ACT(a, aps, AF.Exp, [zbt, ct], [at], scale=-1.0, bias=nw0[:, 16 + h:17 + h])
            TS("dve", a, a, 1.0, None, ALU.add, None, [at], [at])
            RECIP(a, a, [at], [at])
            CP("act", gsb, gps, [gbt_], [gsbt])
            ACT(kkraw, k, AF.Copy, [kt, ct], [kkrawt], scale=v64[:, 32 + h:33 + h])
            ACT(kksq, kkraw, AF.Square, [kkrawt], [kksqt])
            b, bt = bank()
            MM(b[0:64, :N], [(ones_r[0:64, 0:64], kksq)], [kksqt, ct], bt)
            TS("dve", t1, b[0:64, :N], 1e-24, None, ALU.max, None, [bt], [t1t])
            ACT(t1, t1, AF.Ln, [t1t], [t1t])
            ACT(t1, t1, AF.Exp, [t1t], [t1t], scale=-0.5)
            TT("dve", kk, kkraw, t1, ALU.mult, [kkrawt, t1t], [kkt])
            TS("dve", t1, a, -1.0, v64[:, 48 + h:49 + h], ALU.add, ALU.mult, [at, ct, t1t], [t1t])
            STT(k2, t1, 1.0, k, ALU.add, ALU.mult, [t1t, kt], [k2t])
            ACT(P_, L, AF.Exp, [Lt], [Pt_])
            ACT(Pex, Lex, AF.Exp, [Lext], [Pext])
            ACT(Pinv, L, AF.Exp, [Lt], [Pinvt], scale=-1.0)
            STT(At, kk, -1.0, Pex, ALU.mult, ALU.mult, [kkt, Pext], [Att])
            TT("dve", kka, kk, a, ALU.mult, [kkt, at], [kkat])
            TT("dve", Bt, kka, Pinv, ALU.mult, [kkat, Pinvt], [Btt])
            TT("dve", Kt, k2, Pinv, ALU.mult, [k2t, Pinvt], [Ktt])
            TT("dve", Rt, r, P_, ALU.mult, [rt, Pt_], [Rtt])
            for (c0, cs) in chunks:
                ACT(Tc("G", c0, cs), Tc("L", c0, cs), AF.Exp, [Lt], [Gt_], scale=-1.0, bias=Tc("L", c0 + cs - 1, 1))
            Bh, Bht = T_("Bh")
            Kh, Kht = T_("Kh")
            TT("dve", Bh, kka, G, ALU.mult, [kkat, Gt_, Pinvt], [Bht])
            TT("dve", Kh, k2, G, ALU.mult, [k2t, Gt_, at], [Kht])
            rkk, rkkt = R_("rkk")
            bonus, bonust = T_("bonus")
            STT(rkk, r, v64[:, 64 + h:65 + h], k2, ALU.mult, ALU.mult, [rt, k2t, ct, kksqt], [rkkt])
            b, bt = bank()
            MM(b[0:64, :N], [(ones_r[0:64, 0:64], rkk)], [rkkt, ct], bt)
            TT("dve", bonus, v, b[0:64, :N], ALU.mult, [vt, bt, e1t], [bonust])
            ob, obt = obank, obankt
            for ug in range(0, len(chunks), NCH):
                cl = chunks[ug:ug + NCH]
                nch = len(cl)
                for ci, (c0, cs) in enumerate(cl):
                    b, bt = bank()
                    TR(b[0:cs, 0:64], Tc("Bh", c0, cs), ident[0:64, 0:64], [Bht, ct], bt)
                    TR(b[0:cs, 64:128], Tc("Kh", c0, cs), ident[0:64, 0:64], [Kht, ct], bt)
                    TR(b[0:cs, 128:192], Tc("v", c0, cs), ident[0:64, 0:64], [vt, ct], bt)
                    CP("act", TM[0:cs, ci, :], b[0:cs, 0:192], [bt], [TMt[ci]])
                upb = max(1, 512 // (5 * C_))
                for u0 in range(0, nch, upb):
                    b, bt = bank()
                    us = list(range(u0, min(nch, u0 + upb)))
                    for ui, ci in enumerate(us):
                        c0, cs = cl[ci]
                        o = ui * 5 * C_
                        A_, B_, K_, R__ = Rc("At", c0, cs), Rc("Bt", c0, cs), Rc("Kt", c0, cs), Rc("Rt", c0, cs)
                        for kind, (l_, r_) in enumerate(((B_, A_), (A_, B_), (K_, A_), (B_, R__), (K_, R__))):
                            MM(b[0:cs, o + kind * C_:o + kind * C_ + cs], [(l_, r_)], [Att, Btt, Ktt, Rtt], bt)
                    for ui, ci in enumerate(us):
                        c0, cs = cl[ci]
                        o = ui * 5 * C_
                        TT("dve", MMs[0:cs, ci, 0:5 * C_], b[0:cs, o:o + 5 * C_], msk[0:cs, :], ALU.mult, [bt, ct], [MMt[ci]])
                csz = cl[0][1]
                lv = _levels(csz)
                allM = [MMt[ci] for ci in range(nch)]
                TT("dve", TT_[0][0:csz, 0:nch, 0:csz], MMs[0:csz, 0:nch, 0:csz],
                   identr[0:csz, 0:csz].unsqueeze(1).broadcast_to([csz, nch, csz]), ALU.add, allM + [ct], [TTt[0]])
                cur = 0
                for l in range(1, lv):
                    last = (l == lv - 1)
                    b, bt = bank()
                    for ci in range(nch):
                        if l == 1:
                            Np, NpT, rd = MMs[0:csz, ci, 0:csz], MMs[0:csz, ci, C_:C_ + csz], [MMt[ci]]
                        else:
                            Np, NpT, rd = NN[(l - 1) % 2][0:csz, ci, 0:csz], NN[(l - 1) % 2][0:csz, ci, 64:64 + csz], [NNt[(l - 1) % 2]]
                        if not last:
                            MM(b[0:csz, ci * 128:ci * 128 + csz], [(NpT, Np)], rd, bt)
                        MM(b[0:csz, ci * 128 + 64:ci * 128 + 64 + csz], [(Np, NpT)], rd, bt)
                    bv = b[0:csz, 0:nch * 128].rearrange("p (a b) -> p a b", a=nch)
                    if not last:
                        CP("act", NN[l % 2][0:csz, 0:nch, 0:csz], bv[:, :, 0:csz], [bt], [NNt[l % 2]])
                    CP("act", NN[l % 2][0:csz, 0:nch, 64:64 + csz], bv[:, :, 64:64 + csz], [bt], [NNt[l % 2]])
                    b, bt = bank()
                    for ci in range(nch):
                        MM(b[0:csz, ci * 64:ci * 64 + csz], [(NN[l % 2][0:csz, ci, 64:64 + csz], TT_[cur][0:csz, ci, 0:csz])],
                           [NNt[l % 2], TTt[cur]], bt)
                    bv = b[0:csz, 0:nch * 64].rearrange("p (a b) -> p a b", a=nch)
                    TT("dve", TT_[1 - cur][0:csz, 0:nch, 0:csz], bv[:, :, 0:csz], TT_[cur][0:csz, 0:nch, 0:csz], ALU.add,
                       [bt, TTt[cur]], [TTt[1 - cur]])
                    cur = 1 - cur
                Tfin, Tfint = TT_[cur], TTt[cur]
                for ci, (c0, cs) in enumerate(cl):
                    if mode == "p":
                        s32, s32t, sr, srt = S32[:, h, :], S32t[h], SR[:, h, :], SRt[h]
                    else:
                        bi = ug + ci
                        s32, s32t, sr, srt = SS32[:, bi, :], SSt[bi], SSR[:, bi, :], SSRt[bi]
                    A_, R__ = Rc("At", c0, cs), Rc("Rt", c0, cs)
                    Mak = MMs[0:cs, ci, 2 * C_:2 * C_ + cs]
                    Mbr = MMs[0:cs, ci, 3 * C_:3 * C_ + cs]
                    Mkr = MMs[0:cs, ci, 4 * C_:4 * C_ + cs]
                    BhT, KhT, VT = TM[0:cs, ci, 0:64], TM[0:cs, ci, 64:128], TM[0:cs, ci, 128:192]
                    b, bt = bank()
                    MM(b[0:cs, 0:64], [(A_, sr), (Mak, VT)], [Att, srt, MMt[ci], TMt[ci]], bt)
                    CP("act", WT[0:cs, :], b[0:cs, 0:64], [bt], [WTt])
                    MM(b[0:cs, 64:128], [(Tfin[0:cs, ci, 0:cs], WT[0:cs, :])], [Tfint, WTt], bt)
                    CP("act", UT[0:cs, :], b[0:cs, 64:128], [bt], [UTt])
                    MM(ob[0:64, c0:c0 + cs], [(sr, R__), (UT[0:cs, :], Mbr), (VT, Mkr)], [srt, Rtt, UTt, MMt[ci], TMt[ci]], obt)
                    MM(b[0:64, 128:192], [(BhT, UT[0:cs, :]), (KhT, VT)], [TMt[ci], UTt], bt)
                    STT(sr, sr.bitcast(F32), Tc("P", c0 + cs - 1, 1), b[0:64, 128:192], ALU.mult, ALU.add, [srt, Pt_, bt], [srt])
            osb, osbt = R_("osb")
            censq, censqt = R_("censq")
            cen, cent = T_("cen")
            y, yt = T_("y")
            CP("act", osb, ob[0:64, :N], [obt, Att], [osbt])
            b, bt = bank()
            MM(b[0:64, :N], [(ones_r[0:64, 0:64], osb)], [osbt, ct], bt)
            STT(cen, b[0:64, :N], -1.0 / 64, osb.bitcast(F32), ALU.mult, ALU.add, [bt, osbt, Lext], [cent])
            ACT(censq, cen, AF.Square, [cent, Btt], [censqt])
            b, bt = bank()
            MM(b[0:64, :N], [(ones_r[0:64, 0:64], censq)], [censqt, ct], bt)
            ACT(t1, b[0:64, :N], AF.Ln, [bt, ct], [t1t], scale=1.0 / 64, bias=epsc[0:64, 1:2])
            ACT(t1, t1, AF.Exp, [t1t], [t1t], scale=-0.5)
            TT("dve", y, cen, t1, ALU.mult, [cent, t1t, Pext], [yt])
            TS("dve", y, y, v64[:, 80 + h:81 + h], v64[:, 96 + h:97 + h], ALU.mult, ALU.add, [yt, ct], [yt])
            TT("dve", y, y, bonus, ALU.add, [yt, bonust], [yt])
            TT("dve", OGs[:, h, :N], y, gsb, ALU.mult, [yt, gsbt], [OGt])
            if mode == "s":
                for g0 in range(0, NB, 8):
                    b, bt = bank()
                    nb_ = min(8, NB - g0)
                    for j in range(nb_):
                        TR(b[0:64, j * 64:(j + 1) * 64], SSR[:, g0 + j, :].bitcast(F32), ident[0:64, 0:64], [SSRt[g0 + j], ct], bt)
                    CP("dve", SNAT[:, g0:g0 + nb_, :], b[0:64, 0:nb_ * 64].rearrange("p (a b) -> p a b", a=nb_), [bt, SNATt], [SNATt])
                OUT(o_wkvs[:, h, :, :].rearrange("b v k -> v b k"), SNAT[:, 0:NB, :], [SNATt])
        for g in range(4):
            sl_, slt_ = load_slab([(v_h16, W["rw_wo"].rearrange("(h p) m -> p h m", p=64)[:, :, g * 256:(g + 1) * 256])])
            sv = v_h16(sl_)
            for mm_ in range(2):
                m = g * 2 + mm_
                b, bt = bank()
                MM(b[:, :N], [(sv[:, h, mm_ * 128:(mm_ + 1) * 128], OGs[:, h, :N]) for h in range(NH)], [OGt, slt_], bt)
                TT("dve", X[:, m, :N], X[:, m, :N], b[:, :N], ALU.add, [Xt[m], bt], [Xt[m]])

    def ffn_layer(N, li):
        xb, xbt = XM[3], XMt[3]
        norm_to_bf16(N, 8 if li == 0 else 32, xb, xbt)
        up = W["ffn_up%d" % li]
        dn = W["ffn_down%d" % li]

        def loads(jb):
            su = load_slab([(v_k8, kcv(up)[:, :, jb * 512:(jb + 1) * 512])])
            sd = load_slab([(v_k4, dn[jb * 512:(jb + 1) * 512, :].rearrange("(kc p) m -> p kc m", p=128))])
            return su, sd
        nxt = loads(0)
        for jb in range(8):
            (su_, sut), (sd_, sdt) = nxt
            su, sd = v_k8(su_), v_k4(sd_)
            for oc in range(4):
                gi = (jb % 2) * 4 + oc
                hh, hht = gbf(gi)[:, :N], Gt[gi]
                hr_, hrt = gbf(8 + oc % 2)[:, :N], Gt[8 + oc % 2]
                b, bt = bank()
                MM(b[:, :N], [(su[:, kc, oc * 128:(oc + 1) * 128], xb[:, kc, :N]) for kc in range(8)], [xbt, sut], bt)
                ACT(hr_, b[:, :N], AF.Relu, [bt], [hrt])
                TT("dve", hh, hr_, hr_, ALU.mult, [hrt], [hht])
            if jb + 1 < 8:
                nxt = loads(jb + 1)
            for m in range(8):
                b, bt = bank()
                MM(b[:, :N], [(sd[:, kc, m * 128:(m + 1) * 128], gbf((jb % 2) * 4 + kc)[:, :N]) for kc in range(4)],
                   [Gt[(jb % 2) * 4 + kc] for kc in range(4)] + [sdt], bt)
                TT("dve", X[:, m, :N], X[:, m, :N], b[:, :N], ALU.add, [Xt[m], bt], [Xt[m]])

    def kv_path(N, tok0, rope_tok, lat_out, kr_out, blk0, LATT_, KRT_, LATTOK_, kvt_):
        xb, xbt = XM[3], XMt[3]
        norm_to_bf16(N, 16, xb, xbt)
        sl_, slt_ = load_slab([(lambda s: s[:, 0:8 * 320].rearrange("p (a b) -> p a b", a=8)[:, :, 0:KVR], kcv(W["w_dkv"])),
                               (lambda s: s[:, 0:8 * 320].rearrange("p (a b) -> p a b", a=8)[:, :, KVR:KVR + ROPE], kcv(W["w_kr"]))])
        wdkv = sl_[:, 0:8 * 320].rearrange("p (a b) -> p a b", a=8)
        kvA, kvAt = g32(0), Gt[0]
        kvB, kvBt = g32(1), Gt[1]
        latb, latbt = gbf(2), Gt[2]
        nblk = (N + 127) // 128
        for tb in range(nblk):
            c0 = tb * 128
            nt = min(128, N - c0)
            b, bt = bank()
            MM(b[0:nt, 0:KVR + ROPE], [(xb[:, kc, c0:c0 + nt], wdkv[:, kc, :]) for kc in range(8)], [xbt, slt_], bt)
            MEMSET("dve", kvcol[0:nt, 0:1], 0.0, [kvcolt])
            ACT(kvA[0:nt, :], b[0:nt, 0:KVR], AF.Square, [bt], [kvAt, kvcolt], accum=kvcol[0:nt, 0:1])
            ACT(kvcol[0:nt, 1:2], kvcol[0:nt, 0:1], AF.Ln, [kvcolt, ct], [kvcolt], scale=1.0 / KVR, bias=epsc[0:nt, 0:1])
            ACT(kvcol[0:nt, 1:2], kvcol[0:nt, 1:2], AF.Exp, [kvcolt], [kvcolt], scale=-0.5)
            STT(kvA[0:nt, :], b[0:nt, 0:KVR], kvcol[0:nt, 1:2], lnbc[0:nt, :], ALU.mult, ALU.mult, [bt, kvcolt, ct, kvAt], [kvAt])
            p.dma("sp", ropet[0:nt, :], rope_tok[tok0 + c0:tok0 + c0 + nt, :], writes=[ropett])
            x1 = b[0:nt, KVR:KVR + 32]
            x2 = b[0:nt, KVR + 32:KVR + 64]
            o1, o2 = kvB[0:nt, 0:32], kvB[0:nt, 32:64]
            t_a, t_b = kvB[0:nt, 64:96], kvB[0:nt, 96:128]
            TT("dve", t_a, x1, ropet[0:nt, 0:32], ALU.mult, [bt, ropett, kvBt], [kvBt])
            TT("dve", t_b, x2, ropet[0:nt, 32:64], ALU.mult, [bt, ropett, kvBt], [kvBt])
            TT("dve", o1, t_a, t_b, ALU.subtract, [kvBt], [kvBt])
            TT("dve", t_a, x1, ropet[0:nt, 32:64], ALU.mult, [bt, ropett, kvBt], [kvBt])
            TT("dve", t_b, x2, ropet[0:nt, 0:32], ALU.mult, [bt, ropett, kvBt], [kvBt])
            TT("dve", o2, t_a, t_b, ALU.add, [kvBt], [kvBt])
            OUT(lat_out[tok0 + c0:tok0 + c0 + nt, :], kvA[0:nt, :], [kvAt])
            OUT(kr_out[tok0 + c0:tok0 + c0 + nt, :], kvB[0:nt, 0:ROPE], [kvBt])
            kb = blk0 + tb
            CP("act", LATTOK_[0:nt, kb, :], kvA[0:nt, :], [kvAt], [kvt_])
            CP("dve", latb[0:nt, 0:ROPE], kvB[0:nt, 0:ROPE], [kvBt], [latbt])
            hb_, hbt_ = bbank()
            for rc in range(2):
                TR(hb_[:, rc * 128:rc * 128 + nt], LATTOK_[0:nt, kb, rc * 128:(rc + 1) * 128], identb[0:nt, 0:nt], [kvt_, ct], hbt_)
            TR(hb_[0:64, 256:256 + nt], latb[0:nt, 0:ROPE], identb[0:nt, 0:nt], [latbt, ct], hbt_)
            for rc in range(2):
                CP("dve" if rc else "act", LATT_[:, rc, kb * 128:kb * 128 + nt], hb_[:, rc * 128:rc * 128 + nt], [hbt_], [kvt_])
            CP("dve", KRT_[:, kb * 128:kb * 128 + nt], hb_[0:64, 256:256 + nt], [hbt_], [kvt_])

    GI_CQ = (0, 1, 2)
    GI_CQN = (3, 4, 5)
    GI_QN, GI_PB, GI_PT, GI_ACC, GI_OLB = 6, 7, 8, 9, 10
    GI_OH = tuple(range(11, 19))
    GI_X2 = 19

    def mla_queries(N, tok0, rope_fm, head_cb):
        xb, xbt = XM[3], XMt[3]
        norm_to_bf16(N, 24, xb, xbt)
        p.dma("sp", ropef[:, :N], rope_fm[0:64, tok0:tok0 + N], writes=[ropeft])
        p.dma("sp", rope_s2[:, :N], rope_fm[64:128, tok0:tok0 + N], writes=[ropeft])
        sl_, slt_ = load_slab([(lambda s: s[:, 0:8 * QR].rearrange("p (a b) -> p a b", a=8), kcv(W["w_dq"]))])
        wdq = sl_[:, 0:8 * QR].rearrange("p (a b) -> p a b", a=8)
        for m in range(3):
            b, bt = bank()
            MM(b[:, :N], [(wdq[:, kc, m * 128:(m + 1) * 128], xb[:, kc, :N]) for kc in range(8)], [xbt, slt_], bt)
            CP("act", g32(GI_CQ[m], 128, N), b[:, :N], [bt], [Gt[GI_CQ[m]]])
        rms_rstd(N, [(g32(GI_CQ[m], 128, N), Gt[GI_CQ[m]]) for m in range(3)], QR)
        for m in range(3):
            STT(gbf(GI_CQN[m])[:, :N], g32(GI_CQ[m], 128, N), v128[:, 96 + m:97 + m], rstd[:, :N], ALU.mult, ALU.mult,
                [Gt[GI_CQ[m]], rstdt, ct], [Gt[GI_CQN[m]]])
        cqn_t = [Gt[GI_CQN[m]] for m in range(3)]
        QN, QNt = gbf(GI_QN), Gt[GI_QN]
        for hg in range(2):
            sq_, sqt_ = load_slab([(lambda s: s[:, 0:3 * 768].rearrange("p (a b) -> p a b", a=3),
                                    kcv(W["w_uq"])[:, :, hg * 768:(hg + 1) * 768])])
            wuq = sq_[:, 0:3 * 768].rearrange("p (a b) -> p a b", a=3)
            for hh_ in range(4):
                h = hg * 4 + hh_
                b, bt = bank()
                MM(b[:, :N], [(wuq[:, kc, hh_ * 192:hh_ * 192 + 128], gbf(GI_CQN[kc])[:, :N]) for kc in range(3)], cqn_t + [sqt_], bt)
                CP("act", QN[:, :N], b[:, :N], [bt], [QNt])
                b2, bt2 = bank()
                MM(b2[0:64, :N], [(wuq[:, kc, hh_ * 192 + 128:hh_ * 192 + 192], gbf(GI_CQN[kc])[:, :N]) for kc in range(3)], cqn_t + [sqt_], bt2)
                CP("act", QPr[:, :N], b2[0:64, :N], [bt2], [QPt])
                TT("dve", QPf[:, :N], b2[0:64, :N], ropef[:, :N], ALU.mult, [bt2, ropeft], [QPt])
                b3, bt3 = bank()
                MM(b3[0:64, :N], [(rotm[:, :], QPr[:, :N])], [QPt, ct], bt3)
                TT("dve", tmpA[0:64, :N], b3[0:64, :N], rope_s2[:, :N], ALU.mult, [bt3, ropeft], [tmpAt])
                head_cb(h, QN, QNt)

    def load_wuv():
        sl_, slt_ = load_slab([(lambda s: s[:, 0:2048].rearrange("p (a b) -> p a b", a=2), kcv(W["w_uv"]))])
        return sl_[:, 0:2048].rearrange("p (a b) -> p a b", a=2), slt_

    def prompt_attention(N, tok0):
        wuv, wuvt = load_wuv()
        Pbs = [(gbf(GI_PB), Gt[GI_PB]), (gbf(GI_ACC), Gt[GI_ACC])]
        PTs = [(gbf(GI_PT).rearrange("p (a b) -> p a b", a=4), Gt[GI_PT]), (gbf(GI_X2).rearrange("p (a b) -> p a b", a=4), Gt[GI_X2])]
        olb, olbt = gbf(GI_OLB), Gt[GI_OLB]

        def per_head(h, QN, QNt):
            STT(QPEh[:, :N], QPf[:, :N], 1.0, tmpA[0:64, :N], ALU.mult, ALU.add, [QPt, tmpAt], [QLt])
            ACT(QPEh[:, :N], QPEh[:, :N], AF.Copy, [QLt], [QLt], scale=ATTN_SCALE)
            for rc in range(2):
                b, bt = bank()
                MM(b[:, :N], [(wukT[:, h, rc * 128:(rc + 1) * 128], QN[:, :N])], [QNt, wres_t], bt)
                ACT(QLh[:, rc, :N], b[:, :N], AF.Copy, [bt], [QLt], scale=ATTN_SCALE)
            nqb = (N + 127) // 128
            for qb in range(nqb):
                q0 = qb * 128
                nq = min(128, N - q0)
                kend = tok0 + q0 + nq
                nseg = (kend + 511) // 512
                MXc = sm_[0:nq, 0:1]
                NM = sm_[0:nq, 1:2]
                Lc = sm_[0:nq, 2:3]
                RL = sm_[0:nq, 3:4]

                def scores(s):
                    k0 = s * 512
                    kl = min(512, kend - k0)
                    b, bt = bank()
                    MM(b[0:nq, 0:kl], [(QLh[:, 0, q0:q0 + nq], LATT[:, 0, k0:k0 + kl]), (QLh[:, 1, q0:q0 + nq], LATT[:, 1, k0:k0 + kl]),
                                       (QPEh[:, q0:q0 + nq], KRT[:, k0:k0 + kl])], [QLt, KVt], bt)
                    dpos = tok0 + q0 - k0
                    if 0 <= dpos < 512:
                        TT("dve", b[0:nq, dpos:dpos + nq], b[0:nq, dpos:dpos + nq], cmask[0:nq, 0:nq], ALU.add, [bt, ct], [bt])
                    return b, bt, k0, kl
                for s in range(nseg):
                    b, bt, k0, kl = scores(s)
                    p.op("dve", lambda e, b=b, kl=kl, s=s, nq=nq: e.reduce_max(sm2[0:nq, s:s + 1], b[0:nq, 0:kl], AX.X), reads=[bt], writes=[sm2t])
                p.op("dve", lambda e, nq=nq, nseg=nseg, MXc=MXc: e.reduce_max(MXc, sm2[0:nq, 0:nseg], AX.X), reads=[sm2t], writes=[smt])
                TS("dve", NM, MXc, -1.0, None, ALU.mult, None, [smt], [smt])
                MEMSET("dve", sm3[0:nq, 0:nseg], 0.0, [sm3t])
                for s in range(nseg):
                    b, bt, k0, kl = scores(s)
                    Pb, Pbt = Pbs[s % 2]
                    PT, PTt = PTs[s % 2]
                    ACT(Pb[0:nq, 0:kl], b[0:nq, 0:kl], AF.Exp, [bt, smt], [Pbt, sm3t], bias=NM, accum=sm3[0:nq, s:s + 1])
                    nkb = (kl + 127) // 128
                    hb_, hbt_ = bbank()
                    for j in range(nkb):
                        kn = min(128, kl - j * 128)
                        TR(hb_[0:kn, j * 128:j * 128 + nq], Pb[0:nq, j * 128:j * 128 + kn], identb[0:nq, 0:nq], [Pbt, ct], hbt_)
                    if kl == 512:
                        CP("act" if s % 2 else "dve", PT[:, 0:4, 0:nq], hb_[:, 0:512].rearrange("p (a b) -> p a b", a=4)[:, :, 0:nq], [hbt_], [PTt])
                    else:
                        for j in range(nkb):
                            kn = min(128, kl - j * 128)
                            CP("act" if j % 2 else "dve", PT[0:kn, j, 0:nq], hb_[0:kn, j * 128:j * 128 + nq], [hbt_], [PTt])
                    prs = []
                    for j in range(nkb):
                        kn = min(128, kl - j * 128)
                        prs.append((PT[0:kn, j, 0:nq], LATTOK[0:kn, k0 // 128 + j, :]))
                    MM(obank[0:nq, 0:KVR], prs, [PTt, KVt], obankt, start=(s == 0), stop=(s == nseg - 1))
                p.op("dve", lambda e, nq=nq, nseg=nseg, Lc=Lc: e.reduce_sum(Lc, sm3[0:nq, 0:nseg], AX.X), reads=[sm3t], writes=[smt])
                RECIP(RL, Lc, [smt], [smt])
                TS("dve", olb[0:nq, 0:KVR], obank[0:nq, 0:KVR], RL, None, ALU.mult, None, [obankt, smt], [olbt])
                hb_, hbt_ = bbank()
                for rc in range(2):
                    TR(hb_[:, rc * 128:rc * 128 + nq], olb[0:nq, rc * 128:(rc + 1) * 128], identb[0:nq, 0:nq], [olbt, ct], hbt_)
                for rc in range(2):
                    CP("act" if rc else "dve", OLT[:, rc, q0:q0 + nq], hb_[:, rc * 128:rc * 128 + nq], [hbt_], [OLTt])
            b, bt = bank()
            MM(b[:, :N], [(wuv[:, rc, h * 128:(h + 1) * 128], OLT[:, rc, :N]) for rc in range(2)], [OLTt, wuvt], bt)
            CP("act", gbf(GI_OH[h])[:, :N], b[:, :N], [bt], [Gt[GI_OH[h]]])
        return per_head

    def mla_out(N):
        for g in range(2):
            sl_, slt_ = load_slab([(v_k8, kcv(W["w_o_mla"])[:, :, g * 512:(g + 1) * 512])])
            sv = v_k8(sl_)
            for mm_ in range(4):
                m = g * 4 + mm_
                b, bt = bank()
                MM(b[:, :N], [(sv[:, h, mm_ * 128:(mm_ + 1) * 128], gbf(GI_OH[h])[:, :N]) for h in range(MH)],
                   [Gt[GI_OH[h]] for h in range(MH)] + [slt_], bt)
                TT("dve", X[:, m, :N], X[:, m, :N], b[:, :N], ALU.add, [Xt[m], bt], [Xt[m]])

    def final_out(N, dst_rows):
        rms_rstd(N, [(X[:, kc, :N], Xt[kc]) for kc in range(8)], D)
        for kc in range(8):
            STT(g32(kc, 128, N), X[:, kc, :N], v128[:, 40 + kc:41 + kc], rstd[:, :N], ALU.mult, ALU.mult, [Xt[kc], rstdt, ct], [Gt[kc]])
        nblk = (N + 127) // 128
        for tb in range(nblk):
            c0 = tb * 128
            nt = min(128, N - c0)
            for g in range(2):
                b, bt = bank()
                for j in range(4):
                    kc = g * 4 + j
                    TR(b[0:nt, j * 128:(j + 1) * 128], Gp[kc][:, c0:c0 + nt], ident[:, :], [Gt[kc], ct], bt)
                CP("act" if g else "dve", stg[0:nt, g * 512:(g + 1) * 512], b[0:nt, :], [bt, stgt], [stgt])
            for (dst, r0, n) in dst_rows[tb]:
                OUT(dst, stg[r0:r0 + n, :], [stgt])

    tiles = []
    t0 = 0
    while t0 < T:
        n = min(NT, T - t0)
        tiles.append((t0, n))
        t0 += n
    MEMSET("pool", carry[:, :], 0.0, [carryt])
    zer = p.sb("zer", [64, 64], F32)
    MEMSET("pool", zer[:, :], 0.0, [ct])
    for h in range(NH):
        CP("dve", SR[:, h, :], zer[:, :], [ct], [SRt[h]])
    PH = os.environ.get("MK_PH", "rwfkmgo")
    MAXT = int(os.environ.get("MK_MAXT", "999"))
    SPH = os.environ.get("MK_SPH", "rfkmago")
    if cfg.get("DO_PROMPT", True) and "P" not in os.environ.get("MK_SKIP", ""):
        for ti, (tok0, N) in enumerate(tiles):
            if ti >= MAXT:
                break
            nblk = (N + 127) // 128
            for tb in range(nblk):
                g0 = tok0 + tb * 128
                nt = min(128, N - tb * 128)
                rows = []
                if g0 < N_META:
                    rows.append((meta[g0:N_META, :], 0, N_META - g0))
                    rows.append((xp[0:nt - (N_META - g0), :], N_META - g0, nt - (N_META - g0)))
                else:
                    rows.append((xp[g0 - N_META:g0 - N_META + nt, :], 0, nt))
                load_x_block(rows, nt, tb * 128)
            Cc = 64 if N >= 64 else N
            chunks = [(c0, Cc) for c0 in range(0, N, Cc)]
            if "r" in PH:
                rwkv_layer(N, Cc, chunks, "p")
            if (ti == len(tiles) - 1 or ti == MAXT - 1) and "w" in PH:
                b, bt = bank()
                TR(b[0:8, 0:128], carry[:, :], ident[:, :], [carryt, ct], bt)
                CP("dve", stg[0:8, 0:128], b[0:8, 0:128], [bt, stgt], [stgt])
                OUT(o_shiftp, stg[0:8, 0:128], [stgt])
                for g0 in range(0, NH, 8):
                    b, bt = bank()
                    for j in range(8):
                        TR(b[0:64, j * 64:(j + 1) * 64], SR[:, g0 + j, :].bitcast(F32), ident[0:64, 0:64], [SRt[g0 + j], ct], bt)
                    CP("dve", SNAT[:, 0:8, :], b[0:64, 0:512].rearrange("p (a b) -> p a b", a=8), [bt, SNATt], [SNATt])
                    OUT(o_wkvp[g0:g0 + 8, :, :].rearrange("h v k -> v h k"), SNAT[:, 0:8, :], [SNATt])
            if "f" in PH:
                ffn_layer(N, 0)
            if "k" in PH:
                kv_path(N, tok0, C["rope_tok_p"], o_latp, o_krp, tok0 // 128, LATT, KRT, LATTOK, KVt)
            if "m" in PH:
                mla_queries(N, tok0, C["rope_fm_p"], prompt_attention(N, tok0))
                mla_out(N)
            if "g" in PH:
                ffn_layer(N, 1)
            dst_rows = []
            for tb in range(nblk):
                g0 = tok0 + tb * 128
                nt = min(128, N - tb * 128)
                if g0 < N_META:
                    dst_rows.append([(o_yp[0:nt - (N_META - g0), :], N_META - g0, nt - (N_META - g0))])
                else:
                    dst_rows.append([(o_yp[g0 - N_META:g0 - N_META + nt, :], 0, nt)])
            if "o" in PH:
                final_out(N, dst_rows)

    p.barrier()

    def sample_q_cb(N):
        def per_head(h, QN, QNt):
            STT(QPEs[:, h, :N], QPf[:, :N], 1.0, tmpA[0:64, :N], ALU.mult, ALU.add, [QPt, tmpAt], [QLst])
            ACT(QPEs[:, h, :N], QPEs[:, h, :N], AF.Copy, [QLst], [QLst], scale=ATTN_SCALE)
            for rc in range(2):
                b, bt = bank()
                MM(b[:, :N], [(wukT[:, h, rc * 128:(rc + 1) * 128], QN[:, :N])], [QNt, wres_t], bt)
                ACT(QLs[:, h, rc, :N], b[:, :N], AF.Copy, [bt], [QLst], scale=ATTN_SCALE)
        return per_head

    def sample_attend_all(N):
        nq = 32
        M_, L_, BM, MN, NM, CR, RS, RL = [sm_[0:nq, i:i + 1] for i in range(8)]
        Pb, Pbt = gbf(GI_PB), Gt[GI_PB]
        PT = gbf(GI_PT).rearrange("p (a b) -> p a b", a=4)
        PTt = Gt[GI_PT]
        acc, acct = g32(GI_ACC), Gt[GI_ACC]
        olb, olbt = gbf(GI_OLB), Gt[GI_OLB]

        def softmax_block(b, bt, kl, vblocks):
            p.op("dve", lambda e: e.reduce_max(BM, b[0:nq, 0:kl], AX.X), reads=[bt, smt], writes=[smt])
            TT("dve", MN, M_, BM, ALU.max, [smt], [smt])
            TS("dve", NM, MN, -1.0, None, ALU.mult, None, [smt], [smt])
            MEMSET("dve", RS, 0.0, [smt])
            ACT(CR, M_, AF.Exp, [smt], [smt], bias=NM)
            ACT(Pb[0:nq, 0:kl], b[0:nq, 0:kl], AF.Exp, [bt, smt], [Pbt, smt], bias=NM, accum=RS)
            STT(L_, L_, CR, RS, ALU.mult, ALU.add, [smt], [smt])
            CP("dve", M_, MN, [smt], [smt])
            nkb = len(vblocks)
            hb_, hbt_ = bbank()
            o = 0
            for j, (kn, vap, rd) in enumerate(vblocks):
                TR(hb_[0:kn, j * 32:j * 32 + nq], Pb[0:nq, o:o + kn], identb[0:nq, 0:nq], [Pbt, ct], hbt_)
                o += kn
            kn0 = vblocks[0][0]
            CP("dve", PT[0:kn0, 0:nkb, 0:nq], hb_[0:kn0, 0:nkb * 32].rearrange("p (a b) -> p a b", a=nkb), [hbt_], [PTt])
            b2, bt2 = bank()
            rds = [PTt]
            for (_, _, rd) in vblocks:
                rds += rd
            MM(b2[0:nq, 0:KVR], [(PT[0:kn, j, 0:nq], vap) for j, (kn, vap, rd) in enumerate(vblocks)], rds, bt2)
            STT(acc[0:nq, :], acc[0:nq, :], CR, b2[0:nq, 0:KVR], ALU.mult, ALU.add, [acct, smt, bt2], [acct])

        gi = 0
        for bi in range(NB):
            p.dma("sp", idxr[:, :], ptrep[bi], writes=[idxt])
            CP("dve", idxf[:, :], idxr[:, :], [idxt], [idxt])
            TS("dve", idxf[:, :], idxf[:, :], 16.0, cmod[:, 0:1], ALU.mult, ALU.add, [idxt, ct], [idxt])
            CP("dve", idxi[:, :], idxf[:, :], [idxt], [idxt])
            for rc in range(2):
                CP("dve", QB[:, rc, :].rearrange("p (h t) -> p h t", t=DEC_SEQ), QLs[:, :, rc, bi * DEC_SEQ:(bi + 1) * DEC_SEQ], [QLst], [QBt])
            CP("dve", QPB[:, :].rearrange("p (h t) -> p h t", t=DEC_SEQ), QPEs[:, :, bi * DEC_SEQ:(bi + 1) * DEC_SEQ], [QLst], [QBt])
            MEMSET("dve", sm_[0:nq, 0:1], NEG, [smt])
            MEMSET("dve", sm_[0:nq, 1:2], 0.0, [smt])
            MEMSET("dve", acc[0:nq, :], 0.0, [acct])
            for g in range(NGRP):
                si = gi % 2
                gi += 1
                p.dma("pool", None, None, reads=[idxt], writes=[stt_[si]],
                      fn=lambda e, si=si, g=g: e.indirect_dma_start(
                          out=stL[si].rearrange("p a b -> p (a b)"), out_offset=None, in_=c_lat,
                          in_offset=bass.IndirectOffsetOnAxis(ap=idxi[:, g:g + 1], axis=0)))
                p.dma("pool", None, None, reads=[idxt], writes=[stt_[si]],
                      fn=lambda e, si=si, g=g: e.indirect_dma_start(
                          out=stK[si].rearrange("p a b -> p (a b)"), out_offset=None, in_=c_kr,
                          in_offset=bass.IndirectOffsetOnAxis(ap=idxi[:, g:g + 1], axis=0)))
                CP("dve", stLb[:, 0:4, :], stL[si][:, 0:4, :], [stt_[si]], [stbt])
                CP("act", stLb[:, 4:8, :], stL[si][:, 4:8, :], [stt_[si]], [stbt])
                CP("act", stKb[:, :, :], stK[si][:, :, :], [stt_[si]], [stbt])
                for rc in range(2):
                    hb_, hbt_ = bbank()
                    for j in range(8):
                        TR(hb_[:, j * 128:(j + 1) * 128], stLb[:, j, rc * 128:(rc + 1) * 128], identb[:, :], [stbt, ct], hbt_)
                    CP("act" if rc else "dve", LTs[:, rc, :], hb_[:, :], [hbt_], [LTst])
                hb_, hbt_ = bbank()
                for j in range(8):
                    TR(hb_[0:64, j * 128:(j + 1) * 128], stKb[:, j, :], identb[:, :], [stbt, ct], hbt_)
                CP("dve", KTs[:, :], hb_[0:64, :], [hbt_], [LTst])
                for s in range(2):
                    b, bt = bank()
                    k0 = s * 512
                    MM(b[0:nq, 0:512], [(QB[:, 0, :], LTs[:, 0, k0:k0 + 512]), (QB[:, 1, :], LTs[:, 1, k0:k0 + 512]),
                                        (QPB[:, :], KTs[:, k0:k0 + 512])], [QBt, LTst], bt)
                    softmax_block(b, bt, 512, [(128, stLb[:, s * 4 + j, :], [stbt]) for j in range(4)])
            b, bt = bank()
            MM(b[0:nq, 0:NS], [(QB[:, 0, :], LATTs[:, 0, 0:NS]), (QB[:, 1, :], LATTs[:, 1, 0:NS]), (QPB[:, :], KRTs[:, 0:NS])],
               [QBt, KVst], bt)
            TT("dve", b[0:nq, 0:NS], b[0:nq, 0:NS], smask_s[:, bi, :], ALU.add, [bt, ct], [bt])
            softmax_block(b, bt, NS, [(NS, LATTOKs[0:NS, 0, :], [KVst])])
            RECIP(RL, L_, [smt], [smt])
            TS("dve", olb[0:nq, 0:KVR], acc[0:nq, :], RL, None, ALU.mult, None, [acct, smt], [olbt])
            hb_, hbt_ = bbank()
            for rc in range(2):
                TR(hb_[:, rc * 32:rc * 32 + nq], olb[0:nq, rc * 128:(rc + 1) * 128], identb[0:nq, 0:nq], [olbt, ct], hbt_)
            for rc in range(2):
                CP("dve", OLTs[:, :, rc, bi * DEC_SEQ:(bi + 1) * DEC_SEQ], hb_[:, rc * 32:rc * 32 + nq].rearrange("p (h t) -> p h t", t=DEC_SEQ),
                   [hbt_], [OLTst])
        wuv, wuvt = load_wuv()
        for h in range(MH):
            b, bt = bank()
            MM(b[:, :N], [(wuv[:, rc, h * 128:(h + 1) * 128], OLTs[:, h, rc, :N]) for rc in range(2)], [OLTst, wuvt], bt)
            CP("act", gbf(GI_OH[h])[:, :N], b[:, :N], [bt], [Gt[GI_OH[h]]])

    if cfg.get("DO_SAMPLE", True) and "S" not in os.environ.get("MK_SKIP", ""):
        N = NS
        load_x_block([(xs[:, :], 0, NS)], NS, 0)
        p.dma("sp", stg[0:NB, :], sshift, writes=[stgt])
        b, bt = bank()
        for kc in range(8):
            TR(b[:, kc * NB:(kc + 1) * NB], stg[0:NB, kc * 128:(kc + 1) * 128], ident[0:NB, 0:NB], [stgt, ct], bt)
        CP("dve", shT[:, :, :], b[:, 0:8 * NB].rearrange("p (a b) -> p a b", a=8), [bt], [shTt])
        chunks = [(bi * DEC_SEQ, DEC_SEQ) for bi in range(NB)]
        if "r" in SPH:
            rwkv_layer(N, DEC_SEQ, chunks, "s")
        for g in range(2):
            b, bt = bank()
            for j in range(4):
                kc = g * 4 + j
                TR(b[0:NB, j * 128:(j + 1) * 128], shT[:, kc, :], ident[:, :], [shTt, ct], bt)
            CP("dve", stg[0:NB, g * 512:(g + 1) * 512], b[0:NB, :], [bt, stgt], [stgt])
        OUT(o_shifts, stg[0:NB, :], [stgt])
        if "f" in SPH:
            ffn_layer(N, 0)
        if "k" in SPH:
            kv_path(N, 0, C["rope_tok_s"], o_lats, o_krs, 0, LATTs, KRTs, LATTOKs, KVst)
        if "m" in SPH:
            mla_queries(N, 0, C["rope_fm_s"], sample_q_cb(N))
        if "a" in SPH:
            sample_attend_all(N)
            mla_out(N)
        if "g" in SPH:
            ffn_layer(N, 1)
        final_out(N, [[(o_ys[:, :], 0, NS)]])

    p.finish([out_trk])
    return nc


def _run(inputs, cfg, n_cores=8):
    f = lambda a: np.ascontiguousarray(np.asarray(a))
    NB = cfg["NB"]
    NPG = cfg["NPG"]
    nseq = inputs["x_prompt"].shape[0]
    consts = make_consts(cfg)
    shared = {}
    for nm in ("rw_wr", "rw_wk", "rw_wv", "rw_wo", "rw_w1", "rw_w2", "rw_a1", "rw_a2", "rw_g1", "rw_g2"):
        shared[nm] = f(inputs[nm][0])
    for li in range(2):
        shared["ffn_up%d" % li] = f(inputs["ffn_up"][li])
        shared["ffn_down%d" % li] = f(inputs["ffn_down"][li])
    shared["w_dkv"] = f(inputs["w_dkv"])
    shared["w_kr"] = f(inputs["w_kr"])
    shared["w_uk"] = f(np.asarray(inputs["w_uk"]).reshape(KVR, MH * 128))
    shared["w_uv"] = f(np.asarray(inputs["w_uv"]).reshape(KVR, MH * 128))
    shared["w_dq"] = f(inputs["w_dq"][0])
    shared["w_uq"] = f(np.asarray(inputs["w_uq"][0]).reshape(QR, MH * 192))
    shared["w_o_mla"] = f(inputs["w_o_mla"][0])
    v128 = np.concatenate([
        np.asarray(inputs["norm_mix"][0]).reshape(8, 128), np.asarray(inputs["norm_ffn"][0]).reshape(8, 128),
        np.asarray(inputs["kv_norm"]).reshape(8, 128), np.asarray(inputs["norm_mix"][1]).reshape(8, 128),
        np.asarray(inputs["norm_ffn"][1]).reshape(8, 128), np.asarray(inputs["norm_final"]).reshape(8, 128),
        np.asarray(inputs["rw_mu"][0]).reshape(48, 128), np.asarray(inputs["q_norm"][0]).reshape(3, 128)], axis=0)
    shared["vec128"] = f(v128.astype(np.float32))
    v64 = np.concatenate([np.asarray(inputs[k][0]).reshape(16, 64) for k in
                          ("rw_w0", "rw_a0", "rw_kk", "rw_ka", "rw_rk", "rw_lnx_g", "rw_lnx_b")], axis=0)
    shared["vec64"] = f(v64.astype(np.float32))
    shared["latnorm"] = f(np.asarray(inputs["lat_norm"]).reshape(1, KVR))
    shared["meta"] = f(inputs["meta_tokens"])
    nphys = inputs["cache_latent"].shape[0]
    shared["c_lat"] = f(np.asarray(inputs["cache_latent"]).reshape(nphys * 16, 8 * KVR))
    shared["c_kr"] = f(np.asarray(inputs["cache_krope"]).reshape(nphys * 16, 8 * ROPE))
    for k, v in consts.items():
        shared["c_" + k] = f(v)
    pt = np.asarray(inputs["page_table"]).astype(np.int32)
    ngrp = NPG // 8
    in_maps = []
    for c in range(n_cores):
        m = dict(shared)
        m["xp"] = f(inputs["x_prompt"][c % nseq])
        bs = slice(c * NB, (c + 1) * NB)
        m["xs"] = f(np.asarray(inputs["x_sample"][bs]).reshape(NB * DEC_SEQ, D))
        m["swkv"] = f(inputs["state_wkv"][0][bs])
        m["sshift"] = f(inputs["state_shift"][0][bs])
        ptc = pt[bs].reshape(NB, ngrp, 8)
        rep = np.repeat(ptc.transpose(0, 2, 1)[:, :, None, :], 16, axis=2)
        m["ptrep"] = f(rep.reshape(NB, 128, ngrp).astype(np.int32))
        in_maps.append(m)
    nc = build(cfg)
    res = run_bass_kernel_spmd(nc, in_maps, core_ids=list(range(n_cores)))
    return res.results


def _assemble(results, cfg, nseq, n_cores=8):
    NB = cfg["NB"]
    r = results
    y_prompt = np.stack([r[b]["o_yp"] for b in range(nseq)], axis=0)
    y_sample = np.concatenate([r[c]["o_ys"].reshape(NB, DEC_SEQ, D) for c in range(n_cores)], axis=0)
    wkv_p = np.stack([r[b]["o_wkvp"] for b in range(nseq)], axis=0)[None]
    shift_p = np.stack([r[b]["o_shiftp"].reshape(D) for b in range(nseq)], axis=0)[None]
    lat_p = np.stack([r[b]["o_latp"] for b in range(nseq)], axis=0)
    kr_p = np.stack([r[b]["o_krp"] for b in range(nseq)], axis=0)
    wkv_s = np.concatenate([r[c]["o_wkvs"] for c in range(n_cores)], axis=0)[None]
    shift_s = np.concatenate([r[c]["o_shifts"] for c in range(n_cores)], axis=0)[None]
    lat_s = np.concatenate([r[c]["o_lats"].reshape(NB, DEC_SEQ, KVR) for c in range(n_cores)], axis=0)
    kr_s = np.concatenate([r[c]["o_krs"].reshape(NB, DEC_SEQ, ROPE) for c in range(n_cores)], axis=0)
    outs = (y_prompt, y_sample, wkv_p, shift_p, lat_p, kr_p, wkv_s, shift_s, lat_s, kr_s)
    return tuple(np.ascontiguousarray(o.astype(np.float32)) for o in outs)


def kernel(**inputs):
    seq = inputs["x_prompt"].shape[1]
    nseq = inputs["x_prompt"].shape[0]
    db = inputs["x_sample"].shape[0]
    npg = inputs["page_table"].shape[1]
    cfg = dict(SEQ=seq, T=seq + N_META, NB=db // 8, NPG=npg, NPHYS=inputs["cache_latent"].shape[0], PAST=npg * 128)
    results = _run(inputs, cfg)
    return _assemble(results, cfg, nseq)
```
